# Optimizing a Trainium2 kernel written in Bass

```python
import jax, jax.numpy as jnp
from jax import lax
import numpy as np

D_MODEL = 1024
BATCH = 4
SEQ = 8192
DEPTH = 4

GRID_W = 64
NA_HEADS = 8
NA_HEAD_DIM = 64
NA_WIDTH = NA_HEADS * NA_HEAD_DIM
NA_ROWS = 8
NA_COLS = 16
RET_HEADS = 4
RET_QK_DIM = 64
RET_V_DIM = 2 * RET_QK_DIM
RET_QK_WIDTH = RET_HEADS * RET_QK_DIM
RET_V_WIDTH = RET_HEADS * RET_V_DIM
RET_CHUNK = 128
ROPE_BASE = 10000.0
MIX_WIDTH = NA_WIDTH + RET_V_WIDTH
IN_WIDTH = 3 * NA_WIDTH + 2 * RET_QK_WIDTH + 2 * RET_V_WIDTH
D_FF = 4 * D_MODEL
EPS = 1e-6
NEG = -1e30

kernel_name = 'hybrid_natten_retention_encoder'

F32 = jnp.float32


def rms_norm(x, gain):
    x32 = x.astype(F32)
    y = x32 * lax.rsqrt(jnp.mean(jnp.square(x32), axis=-1, keepdims=True) + EPS)
    return (y * gain.astype(F32)).astype(x.dtype)


def head_group_norm(y, gain):
    bsz, seq, heads, dv = y.shape
    mu = jnp.mean(y, axis=-1, keepdims=True)
    var = jnp.mean(jnp.square(y - mu), axis=-1, keepdims=True)
    y = (y - mu) * lax.rsqrt(var + EPS)
    return y.reshape(bsz, seq, heads * dv) * gain.astype(F32)


def rotary(x, pos):
    d = x.shape[-1]
    inv = 1.0 / (ROPE_BASE ** (jnp.arange(0, d, 2, dtype=F32) / d))
    ang = pos.astype(F32)[:, None] * inv[None, :]
    cos = jnp.cos(ang)[None, :, None, :]
    sin = jnp.sin(ang)[None, :, None, :]
    x1, x2 = x[..., : d // 2], x[..., d // 2:]
    return jnp.concatenate([x1 * cos - x2 * sin, x1 * sin + x2 * cos], axis=-1)


def neighbourhood_attention(q, k, v, rpb):
    bsz, seq, heads, dh = q.shape
    rows = seq // GRID_W
    kh = min(NA_ROWS, rows)
    n_cb = GRID_W // NA_COLS
    band = 2 * NA_COLS
    col = np.arange(GRID_W)
    col_start = np.clip(col - NA_COLS // 2, 0, GRID_W - NA_COLS)
    band_start = np.clip(np.arange(n_cb) * NA_COLS - NA_COLS // 2, 0, GRID_W - band)
    band_idx = band_start[:, None] + np.arange(band)
    q_col = col.reshape(n_cb, NA_COLS)
    lo = col_start[q_col][..., None]
    kcol = band_idx[:, None, :]
    valid = jnp.asarray((kcol >= lo) & (kcol < lo + NA_COLS))
    dc = np.clip(kcol - q_col[..., None], -(NA_COLS - 1), NA_COLS - 1) + NA_COLS - 1
    bias_c = rpb.astype(F32)[:, :, dc]

    kb = k.reshape(bsz, rows, GRID_W, heads, dh)[:, :, band_idx]
    vb = v.reshape(bsz, rows, GRID_W, heads, dh)[:, :, band_idx]
    q_rows = jnp.moveaxis(q.reshape(bsz, rows, n_cb, NA_COLS, heads, dh), 1, 0)
    scale = dh ** -0.5

    def one_row(args):
        q_row, r = args
        rs = jnp.clip(r - kh // 2, 0, rows - kh)
        k_win = lax.dynamic_slice_in_dim(kb, rs, kh, axis=1)
        v_win = lax.dynamic_slice_in_dim(vb, rs, kh, axis=1)
        s = jnp.einsum('bcqhd,bicjhd->bhcqij', q_row, k_win).astype(F32) * scale
        dr = rs - r + jnp.arange(kh) + NA_ROWS - 1
        bias = jnp.take(bias_c, dr, axis=1)
        s = s + jnp.transpose(bias, (0, 2, 3, 1, 4))[None]
        s = jnp.where(valid[None, None, :, :, None, :], s, NEG)
        p = jax.nn.softmax(s.reshape(s.shape[:4] + (kh * band,)), axis=-1)
        p = p.reshape(s.shape).astype(v.dtype)
        return jnp.einsum('bhcqij,bicjhd->bcqhd', p, v_win)

    out = lax.map(one_row, (q_rows, jnp.arange(rows)))
    return jnp.moveaxis(out, 0, 1).reshape(bsz, seq, heads * dh)


def retention_scan(q, k, v, log_gamma, include_diag):
    bsz, seq, heads, dk = q.shape
    dv = v.shape[-1]
    c = RET_CHUNK
    n = seq // c
    qc = q.reshape(bsz, n, c, heads, dk)
    kc = k.reshape(bsz, n, c, heads, dk)
    vc = v.reshape(bsz, n, c, heads, dv)
    idx = jnp.arange(c, dtype=F32)
    diff = idx[:, None] - idx[None, :]
    keep = (diff >= 0) if include_diag else (diff > 0)
    dmat = jnp.where(keep[None], jnp.exp(log_gamma[:, None, None] * jnp.maximum(diff, 0.0)[None]), 0.0)
    s = jnp.einsum('bnihd,bnjhd->bnhij', qc, kc) * dmat
    intra = jnp.einsum('bnhij,bnjhe->bnihe', s, vc)
    zeta = jnp.exp(log_gamma[:, None] * (c - 1 - idx)[None, :])
    chunk_kv = jnp.einsum('bnjhd,hj,bnjhe->nbhde', kc, zeta, vc)
    chunk_decay = jnp.exp(log_gamma * c)[None, :, None, None]

    def step(state, kv):
        return chunk_decay * state + kv, state

    _, prev = lax.scan(step, jnp.zeros((bsz, heads, dk, dv), F32), chunk_kv)
    xi = jnp.exp(log_gamma[:, None] * (idx + 1.0)[None, :])
    cross = jnp.einsum('bnihd,nbhde,hi->bnihe', qc, prev, xi)
    return (intra + cross).reshape(bsz, seq, heads, dv)


def bidirectional_retention(q, k, v, lg_fwd, lg_bwd):
    flip = lambda t: jnp.flip(t, axis=1)
    fwd = retention_scan(q, k, v, lg_fwd, True)
    bwd = flip(retention_scan(flip(q), flip(k), flip(v), lg_bwd, False))
    return fwd + bwd


def log_decay(z):
    return jnp.log1p(-jnp.exp(z.astype(F32)))


def setup_inputs(seed: int = 0) -> dict:
    key = jax.random.key(seed)
    ks = jax.random.split(key, 12)
    base_decay = -(5.0 + jnp.arange(RET_HEADS, dtype=F32)) * float(np.log(2.0))
    x = jax.random.normal(ks[0], (BATCH, SEQ, D_MODEL), F32)
    w_in = jax.random.normal(ks[1], (DEPTH, D_MODEL, IN_WIDTH), F32) * D_MODEL ** -0.5
    w_out = jax.random.normal(ks[2], (DEPTH, MIX_WIDTH, D_MODEL), F32) * MIX_WIDTH ** -0.5
    na_rpb = jax.random.normal(ks[3], (DEPTH, NA_HEADS, 2 * NA_ROWS - 1, 2 * NA_COLS - 1), F32) * 0.02
    ret_decay_fwd = base_decay[None] + 0.05 * jax.random.normal(ks[4], (DEPTH, RET_HEADS), F32)
    ret_decay_bwd = base_decay[None] + 0.05 * jax.random.normal(ks[5], (DEPTH, RET_HEADS), F32)
    ret_norm_gain = 1.0 + 0.02 * jax.random.normal(ks[6], (DEPTH, RET_V_WIDTH), F32)
    norm_mix = 1.0 + 0.02 * jax.random.normal(ks[7], (DEPTH, D_MODEL), F32)
    norm_mlp = 1.0 + 0.02 * jax.random.normal(ks[8], (DEPTH, D_MODEL), F32)
    w_up = jax.random.normal(ks[9], (DEPTH, D_MODEL, D_FF), F32) * D_MODEL ** -0.5
    w_down = jax.random.normal(ks[10], (DEPTH, D_FF, D_MODEL), F32) * D_FF ** -0.5
    norm_final = 1.0 + 0.02 * jax.random.normal(ks[11], (D_MODEL,), F32)
    return {'x': x, 'w_in': w_in, 'w_out': w_out, 'na_rpb': na_rpb,
            'ret_decay_fwd': ret_decay_fwd, 'ret_decay_bwd': ret_decay_bwd,
            'ret_norm_gain': ret_norm_gain, 'norm_mix': norm_mix, 'norm_mlp': norm_mlp,
            'w_up': w_up, 'w_down': w_down, 'norm_final': norm_final}


def reference(x, w_in, w_out, na_rpb, ret_decay_fwd, ret_decay_bwd, ret_norm_gain,
              norm_mix, norm_mlp, w_up, w_down, norm_final):
    bsz, seq, _ = x.shape
    pos = jnp.arange(seq)
    splits = np.cumsum([NA_WIDTH, NA_WIDTH, NA_WIDTH, RET_QK_WIDTH, RET_QK_WIDTH, RET_V_WIDTH])
    for l in range(DEPTH):
        h = rms_norm(x, norm_mix[l])
        proj = h @ w_in[l]
        na_q, na_k, na_v, r_q, r_k, r_v, r_g = jnp.split(proj, splits, axis=-1)
        na_out = neighbourhood_attention(na_q.reshape(bsz, seq, NA_HEADS, NA_HEAD_DIM),
                                         na_k.reshape(bsz, seq, NA_HEADS, NA_HEAD_DIM),
                                         na_v.reshape(bsz, seq, NA_HEADS, NA_HEAD_DIM),
                                         na_rpb[l])
        rq = rotary(r_q.reshape(bsz, seq, RET_HEADS, RET_QK_DIM).astype(F32), pos)
        rk = rotary(r_k.reshape(bsz, seq, RET_HEADS, RET_QK_DIM).astype(F32), pos) * RET_QK_DIM ** -0.5
        rv = r_v.reshape(bsz, seq, RET_HEADS, RET_V_DIM).astype(F32)
        ret = bidirectional_retention(rq, rk, rv, log_decay(ret_decay_fwd[l]), log_decay(ret_decay_bwd[l]))
        ret = head_group_norm(ret, ret_norm_gain[l])
        ret_out = (jax.nn.silu(r_g.astype(F32)) * ret).astype(x.dtype)
        mix = jnp.concatenate([na_out.astype(x.dtype), ret_out], axis=-1)
        x = x + mix @ w_out[l]
        h = rms_norm(x, norm_mlp[l])
        x = x + jnp.square(jax.nn.relu(h @ w_up[l])) @ w_down[l]
    return rms_norm(x, norm_final)
```

```python
import contextlib
import numpy as np
import ml_dtypes
import concourse.bass as bass
import concourse.mybir as mybir
from concourse.bass_utils import run_bass_kernel_spmd

F32 = mybir.dt.float32
BF16 = mybir.dt.bfloat16
AF = mybir.ActivationFunctionType
ALU = mybir.AluOpType
AX = mybir.AxisListType

D = 1024
TOK = 4096
NT = 32
DEPTH = 4
EPS = 1e-6
VAR_DR0 = [3, 2, 1, 0, -1, -2, -3, -4, -5, 3, 4, 5, 6, -5, -6, -7]
VAR_MODE = [1, 0, 0, 0, 0, 0, 0, 0, 2, 0, 0, 0, 0, 0, 0, 0]
VIDX_BOTH = {2: 1, 1: 2, 0: 3, -1: 4, -2: 5, -3: 6, -4: 7, 3: 9, 4: 10, 5: 11, 6: 12, -5: 13, -6: 14, -7: 15}
DBG_OUT = ('xs', 'qna', 'kna', 'vna', 'qtx', 'intra', 'sgd', 'nat')
TOP_KR = [-4, -2, 0, 2, 4, 6]
BOT_KR = [56, 58, 60, 62, 64, 66]


class _Op:
    __slots__ = ("eng", "fn", "deps", "dma", "semkey", "signal", "val", "inc", "barrier")

    def __init__(self, eng, fn, dma, semkey, inc):
        self.eng = eng
        self.fn = fn
        self.deps = ()
        self.dma = dma
        self.semkey = semkey
        self.signal = dma
        self.val = 0
        self.inc = inc
        self.barrier = False


class Prog:
    ENGS = ("pe", "act", "dve", "pool", "sp")
    SAME_ENG_SYNC = ("act", "dve", "pool")

    def __init__(self):
        self.ops = []
        self.res = {}

    def op(self, eng, fn, reads=(), writes=(), dma=False, semkey=None):
        import os
        if len(self.ops) >= int(os.environ.get('KN_MAXOPS', '100000000')):
            return -1
        deps = set()
        for r in reads:
            st = self.res.get(r)
            if st is not None and st[0] is not None:
                deps.add(st[0])
        for w in writes:
            st = self.res.get(w)
            if st is not None:
                if st[0] is not None:
                    deps.add(st[0])
                deps.update(st[1].values())
                deps.update(st[2])
        idx = len(self.ops)
        o = _Op(eng, fn, dma, semkey, 16 if dma else 1)
        pr = set()
        for j in deps:
            oj = self.ops[j]
            if (not oj.dma) and (not dma) and oj.eng == eng and eng not in self.SAME_ENG_SYNC:
                continue
            pr.add(j)
        o.deps = pr
        self.ops.append(o)
        for r in reads:
            st = self.res.setdefault(r, [None, {}, []])
            if dma:
                st[2].append(idx)
            else:
                st[1][eng] = idx
        for w in writes:
            self.res[w] = [idx, {}, []]
        return idx

    def barrier(self):
        o = _Op("sp", None, False, None, 0)
        o.barrier = True
        self.ops.append(o)
        self.res = {}

    def emit(self, nc):
        ops = self.ops
        last = {}
        for o in ops:
            if o.barrier:
                for e, lo in last.items():
                    lo.signal = True
                continue
            for j in o.deps:
                ops[j].signal = True
            if not o.dma:
                last[o.eng] = o
        engcnt = {e: 0 for e in self.ENGS}
        dmacnt = {}
        for o in ops:
            if o.barrier:
                continue
            if o.dma:
                dmacnt[o.semkey] = dmacnt.get(o.semkey, 0) + o.inc
                o.val = dmacnt[o.semkey]
            elif o.signal:
                engcnt[o.eng] += 1
                o.val = engcnt[o.eng]
        with contextlib.ExitStack() as es:
            engsem = {e: es.enter_context(nc.semaphore("s_" + e)) for e in self.ENGS}
            dmasem = {}
            for k in dmacnt:
                dmasem[k] = es.enter_context(nc.semaphore("d_%d" % len(dmasem)))
            streams = {e: [] for e in self.ENGS}
            waited = {e: {} for e in self.ENGS}
            cur_eng = {e: 0 for e in self.ENGS}
            cur_dma = {}
            for o in ops:
                if o.barrier:
                    for e in self.ENGS:
                        wl = []
                        for e2 in self.ENGS:
                            if e2 != e and cur_eng[e2] > waited[e].get(("e", e2), 0):
                                waited[e][("e", e2)] = cur_eng[e2]
                                wl.append((engsem[e2], cur_eng[e2]))
                        for k, v in cur_dma.items():
                            if v > waited[e].get(("d", k), 0):
                                waited[e][("d", k)] = v
                                wl.append((dmasem[k], v))
                        if wl:
                            streams[e].append((wl, None, None, 0))
                    continue
                need = {}
                for j in o.deps:
                    oj = ops[j]
                    if oj.dma:
                        s = ("d", oj.semkey)
                        sem = dmasem[oj.semkey]
                    else:
                        s = ("e", oj.eng)
                        sem = engsem[oj.eng]
                    if oj.val > need.get(s, (None, 0))[1]:
                        need[s] = (sem, oj.val)
                wl = []
                for s, (sem, v) in need.items():
                    if waited[o.eng].get(s, 0) < v:
                        waited[o.eng][s] = v
                        wl.append((sem, v))
                if o.dma:
                    mysem, inc = dmasem[o.semkey], o.inc
                    cur_dma[o.semkey] = o.val
                elif o.signal:
                    mysem, inc = engsem[o.eng], 1
                    cur_eng[o.eng] = o.val
                else:
                    mysem, inc = None, 0
                streams[o.eng].append((wl, o.fn, mysem, inc))
            final = [(dmasem[k], v) for k, v in dmacnt.items()]
            final += [(engsem[e], engcnt[e]) for e in self.ENGS if engcnt[e] > 0]

            def run(stream, lastw=None):
                def f(eng):
                    for wl, fn, mysem, inc in stream:
                        for sem, v in wl:
                            eng.wait_ge(sem, v)
                        if fn is None:
                            continue
                        ins = fn(eng)
                        if mysem is not None:
                            ins.then_inc(mysem, inc)
                    if lastw:
                        for sem, v in lastw:
                            eng.wait_ge(sem, v)
                return f

            with nc.Block() as block:
                block.tensor(run(streams["pe"]))
                block.scalar(run(streams["act"]))
                block.vector(run(streams["dve"]))
                block.gpsimd(run(streams["pool"]))
                block.sync(run(streams["sp"], final))
        return {e: len(streams[e]) for e in self.ENGS}


def ap_of(t, offset, dims):
    return bass.AP(t, offset, [list(d) for d in dims])


def na_units(g):
    R0 = 8 * g
    ilo, ihi = R0, R0 + 7
    bnd = None
    if g == 0:
        ilo = 4
        bnd = (0, 3, TOP_KR)
    if g == 7:
        ihi = 60
        bnd = (61, 63, BOT_KR)
    units = []
    klo = ilo - 5
    klo += klo % 2
    khi = ihi + 3
    khi -= khi % 2
    for kr in range(klo, khi + 1, 2):
        r0 = max(ilo, kr - 3)
        r1 = min(ihi, kr + 5)
        if r0 <= r1:
            units.append((kr, r0, r1, "int"))
    if bnd is not None:
        for kr in bnd[2]:
            units.append((kr, bnd[0], bnd[1], "bnd"))
    return units


def build(depth=DEPTH, stop=9, debug=False, lite=False):
    nc = bass.Bass("TRN2", target_bir_lowering=False)
    dt_in = lambda name, shape, dt=F32: nc.dram_tensor(name, list(shape), dt, kind="ExternalInput").ap()
    x_in = dt_in("x", [TOK, D])
    LD = 1 if lite else DEPTH
    w_in = dt_in("w_in", [LD, D, 3072])
    w_out = dt_in("w_out", [LD, D, D])
    w_up = dt_in("w_up", [LD, D, 4096])
    w_dn = dt_in("w_down", [LD, 4096, D])
    nmix = dt_in("norm_mix", [DEPTH, D])
    nmlp = dt_in("norm_mlp", [DEPTH, D])
    nfin = dt_in("norm_final", [1, D])
    gret_d = dt_in("ret_norm_gain", [DEPTH, 512])
    dec_d = dt_in("decays", [DEPTH, 8])
    gtab_d = dt_in("gtab", [DEPTH, 128, 8 * 16 * 64])
    mask_d = dt_in("maskt", [128, 16 * 64], BF16)
    rm_d = dt_in("rmt", [128, 42])
    cs_d = dt_in("cs", [TOK, 64])
    ab_d = dt_in("abt", [128, 256])
    c4_d = dt_in("c4t", [128, 4])
    cm_d = dt_in("cmask", [128, 4])
    id_d = dt_in("ident", [128, 128], BF16)
    out_d = nc.dram_tensor("out", [TOK, D], F32, kind="ExternalOutput").ap()

    import os as _os0
    _sw0 = _os0.environ.get("KN_SW", "")
    def dscr(name, shape, dt):
        if "e" in _sw0:
            shape = [8, 8]
        return (nc.dram_tensor(name, list(shape), dt, kind="ExternalOutput").ap() if debug and name in DBG_OUT else nc.dram_tensor(name, list(shape), dt).ap())
    xs = dscr("xs", [TOK, D], F32)
    qna = dscr("qna", [8, 64, TOK], BF16)
    kna = dscr("kna", [8, 64, TOK], BF16)
    vna = dscr("vna", [NT, 128, 640], BF16)
    qtx = dscr("qtx", [NT, 128, 512], BF16)
    intra_d = dscr("intra", [NT, 128, 512], F32)
    sg_d = dscr("sgd", [NT, 128, 512], F32)
    nat = dscr("nat", [8, 64, TOK], BF16)
    cst_in = dscr("cst_in", [128, 512], F32)
    cst_out = dscr("cst_out", [256, 512], F32)
    HW = 2048 + 2560
    chal_in = dscr("chal_in", [128, HW], BF16)
    chal_out = dscr("chal_out", [256, HW], BF16)
    import os as _os3
    RG = [[2 * i, 2 * i + 1] for i in range(int(_os3.environ.get('KN_NCORES', '8')) // 2)]

    P = Prog()
    cc_count = [0]
    with contextlib.ExitStack() as es:
        sbt = lambda name, shape, dt: es.enter_context(nc.sbuf_tensor(name, list(shape), dt))
        import os as _os2
        BIG = sbt("BIG", [128, 65536 if not _os2.environ.get("KN_SMALL") else 32768], BF16)
        WF = sbt("WF", [128, 7168], F32)
        WB = sbt("WB", [128, 12288], BF16)
        ident = sbt("ident_s", [128, 128], BF16)
        onesf = sbt("onesf", [128, 64], F32)
        zerosb = sbt("zerosb", [128, 512], BF16)
        ABt = sbt("ABt", [128, 256], F32)
        C4 = sbt("C4", [128, 4], F32)
        cmask = sbt("cmask_s", [128, 4], F32)
        lg = sbt("lg", [128, 8], F32)
        lgp = sbt("lgp", [128, 4], F32)
        TS = sbt("TS", [128, 16], F32)
        Mpp = sbt("Mpp", [128, 512], F32)
        Dfull = sbt("Dfull", [128, 512], F32)
        rmt = sbt("rmt_s", [128, 42], F32)
        gain = sbt("gain", [128, 1024], F32)
        gret = sbt("gret", [128, 512], F32)
        khalo = sbt("khalo", [64, 4096], BF16)
        vhalo = sbt("vhalo", [128, 2560], BF16)
        small = sbt("small", [128, 64], F32)
        ccdummy = sbt("ccdummy", [128, 8], F32)
        ccsem = es.enter_context(nc.semaphore("ccsem"))
        psb = [es.enter_context(nc.psum_tensor("psb%d" % i, [128, 512], F32)) for i in range(1 if "f" in _sw0 else 8)]

        def PSF(i, rows=128, c0=0, c1=512):
            return psb[i][0:rows, c0:c1]

        def PSB16(i, rows=128):
            return psb[i][0:rows, :].bitcast(BF16)

        Win = BIG[:, 0:24576].rearrange("p (k n) -> p k n", k=8)
        EB = BIG[:, 0:8192].rearrange("p (h v q) -> p h v q", h=8, v=16)
        WoutNA = BIG[0:64, 8192:16384].rearrange("p (h n) -> p h n", h=8)
        WoutR = BIG[:, 16384:20480].rearrange("p (k n) -> p k n", k=4)
        gstage = BIG[:, 20480:24576].bitcast(F32)
        KV = BIG[:, 24576:40960].rearrange("p (s n) -> p s n", s=32)
        NAq = BIG[0:64, 24576:28672].rearrange("p (h n) -> p h n", h=8)
        NAk = BIG[0:64, 28672:36864].rearrange("p (h n) -> p h n", h=8)
        NAv = WB[:, 7168:12288].rearrange("p (t n) -> p t n", t=8)
        State = BIG[:, 40960:57344].rearrange("p (s n) -> p s n", s=32)
        klo = BIG[0:64, 57344:61440]
        khi = BIG[0:64, 61440:65536]
        Wup = BIG[:, 0:32768].rearrange("p (k n) -> p k n", k=8)
        Wdn = BIG[:, 32768:65536].rearrange("p (f n) -> p f n", f=32)

        dma_ct = [0]

        def DMA(q, out, in_, reads, writes, semkey):
            P.op(q, lambda e, o=out, i=in_: e.dma_start(out=o, in_=i), reads=reads, writes=writes, dma=True, semkey=semkey)

        import os as _os
        _sw = _os.environ.get("KN_SW", "")
        if "a" not in _sw:
            DMA("sp", ident[:], id_d, [], ["ident"], "c0")
            DMA("sp", ABt[:], ab_d, [], ["ABt"], "c1")
        if "b" not in _sw:
            DMA("sp", C4[:], c4_d, [], ["C4"], "c2")
            DMA("sp", cmask[:], cm_d, [], ["cmask"], "c3")
            DMA("sp", rmt[:], rm_d, [], ["rmt"], "c4")
        if "c" not in _sw:
            P.op("pool", lambda e: e.memset(onesf[:], 1.0), writes=["onesf"])
            P.op("pool", lambda e: e.memset(zerosb[:], 0.0), writes=["zerosb"])
        if "d" not in _sw:
            P.barrier()

        for l in range(depth):
            if stop <= 0:
                break
            xsrc = x_in if l == 0 else xs
            for k in range(8):
                DMA("pool", Win[:, k, :], w_in[l, k * 128:(k + 1) * 128, :], [], [("win", k)], ("win", k))
            DMA("sp", gain[:], nmix[l, :].partition_broadcast(128), [], ["gain"], "gain")
            DMA("sp", lg[:], dec_d[l, :].partition_broadcast(128), [], ["lg"], "lg")
            DMA("sp", lgp[0:64, :], dec_d[l, 0:4].partition_broadcast(64), [], ["lgp0"], "lgp0")
            DMA("sp", lgp[64:128, :], dec_d[l, 4:8].partition_broadcast(64), [], ["lgp1"], "lgp1")
            P.op("act", lambda e: e.activation(out=lg[:], in_=lg[:], func=AF.Exp), reads=["lg"], writes=["lg"])
            P.op("act", lambda e: e.activation(out=lg[:], in_=lg[:], func=AF.Ln, scale=-1.0, bias=1.0), reads=["lg"], writes=["lg"])
            P.op("act", lambda e: e.activation(out=lgp[:], in_=lgp[:], func=AF.Exp), reads=["lgp0", "lgp1"], writes=["lgp"])
            P.op("act", lambda e: e.activation(out=lgp[:], in_=lgp[:], func=AF.Ln, scale=-1.0, bias=1.0), reads=["lgp"], writes=["lgp"])
            P.op("act", lambda e: e.activation(out=small[:, 0:4], in_=lgp[:], func=AF.Exp, scale=128.0), reads=["lgp"], writes=["small"])
            P.op("dve", lambda e: e.tensor_copy(out=Dfull[:].rearrange("p (h n) -> p h n", h=4),
                                                in_=ap_of(small, 0, [small[:, 0:1].ap[0], [1, 4], [0, 128]])),
                 reads=["small"], writes=["Dfull"])
            for kind, (cc, d0) in enumerate([(0, 0), (1, 4), (2, 0), (3, 4)]):
                P.op("dve", lambda e, kind=kind, cc=cc, d0=d0: e.tensor_scalar(
                    out=TS[:, kind * 4:(kind + 1) * 4], in0=lg[:, d0:d0 + 4], scalar1=C4[:, cc:cc + 1], scalar2=None, op0=ALU.mult),
                    reads=["lg", "C4"], writes=[("TSr", kind)])
            P.op("act", lambda e: e.activation(out=TS[:], in_=TS[:], func=AF.Exp), reads=[("TSr", i) for i in range(4)], writes=["TS"])
            P.op("dve", lambda e: e.tensor_scalar(out=TS[:, 8:16], in0=TS[:, 8:16], scalar1=0.125, scalar2=None, op0=ALU.mult), reads=["TS"], writes=["TS"])
            for h in range(4):
                P.op("dve", lambda e, h=h: e.tensor_scalar(out=Mpp[:, h * 128:(h + 1) * 128], in0=ABt[:, 0:128], scalar1=lg[:, h:h + 1], scalar2=None, op0=ALU.mult),
                     reads=["lg", "ABt"], writes=[("Mpp", h)])
                P.op("dve", lambda e, h=h: e.scalar_tensor_tensor(out=Mpp[:, h * 128:(h + 1) * 128], in0=ABt[:, 128:256], scalar=lg[:, 4 + h:5 + h],
                                                                   in1=Mpp[:, h * 128:(h + 1) * 128], op0=ALU.mult, op1=ALU.add),
                     reads=["lg", "ABt", ("Mpp", h)], writes=[("Mpp", h)])
            P.op("act", lambda e: e.activation(out=Mpp[:], in_=Mpp[:], func=AF.Exp), reads=[("Mpp", h) for h in range(4)], writes=["MppF"])

            f_xin = [WF[:, 0:1024], WF[:, 1024:2048]]
            f_rqk = WF[:, 2048:2560]
            f_rot = WF[:, 2560:3072]
            f_t = [WF[:, 3072 + i * 256:3072 + (i + 1) * 256] for i in range(4)]
            f_sg = [WF[:, 4096:4608], WF[:, 4608:5120]]
            f_intra = [WF[:, 5120:5632], WF[:, 5632:6144]]
            f_cs = [WF[:, 6144:6208], WF[:, 6208:6272]]
            f_st = WF[:, 6272:6336]
            b_junk = WB[:, 0:1024]
            b_h = WB[:, 1024:2048]
            b_hT = WB[:, 2048:3072].rearrange("p (k n) -> p k n", k=8)
            b_qtok = WB[:, 3072:3584]
            b_ktok = WB[:, 3584:4096]
            b_qT = WB[0:64, 4096:5120].rearrange("p (h n) -> p h n", h=8)
            b_kT = WB[0:64, 5120:6144].rearrange("p (h n) -> p h n", h=8)
            b_vaug = [WB[:, 6144:6784], WB[:, 6784:7424]]
            b_rv = WB[:, 7424:7936]
            b_Qx = WB[:, 7936:8448]
            b_Kx = WB[:, 8448:8960]
            b_QTx = [WB[:, 8960:9472], WB[:, 9472:9984]]
            b_KT = WB[0:64, 9984:10496].rearrange("p (h n) -> p h n", h=4)
            b_SM = WB[:, 10496:11008]
            for s in range(2):
                P.op("pool", lambda e, s=s: e.memset(b_vaug[s], 1.0), writes=[("vaug", s)])
            for t in range(NT):
                s = t % 2
                tok = slice(t * 128, (t + 1) * 128)
                DMA("sp", f_xin[s], xsrc[tok, :], [], [("xin", s)], ("xin", s))
                DMA("sp", f_cs[s], cs_d[tok, :], [], [("cs", s)], ("cs", s))
                P.op("act", lambda e, s=s: e.activation(out=b_junk, in_=f_xin[s], func=AF.Square, accum_out=f_st[:, 0:1]),
                     reads=[("xin", s)], writes=["junk", "ssq"])
                P.op("dve", lambda e: e.tensor_scalar(out=f_st[:, 1:2], in0=f_st[:, 0:1], scalar1=1.0 / D, scalar2=EPS, op0=ALU.mult, op1=ALU.add),
                     reads=["ssq"], writes=["ms"])
                P.op("act", lambda e: e.activation(out=f_st[:, 3:4], in_=f_st[:, 1:2], func=AF.Sqrt), reads=["ms"], writes=["sqv"])
                P.op("dve", lambda e: e.reciprocal(out=f_st[:, 2:3], in_=f_st[:, 3:4]), reads=["sqv"], writes=["rstd"])
                P.op("dve", lambda e, s=s: e.scalar_tensor_tensor(out=b_h, in0=f_xin[s], scalar=f_st[:, 2:3], in1=gain[:], op0=ALU.mult, op1=ALU.mult),
                     reads=[("xin", s), "rstd", "gain"], writes=["h"])
                def tr_h(e):
                    r = None
                    for k in range(8):
                        r = e.transpose(PSB16(0)[:, k * 128:(k + 1) * 128], b_h[:, k * 128:(k + 1) * 128], ident[:])
                    return r
                P.op("pe", tr_h, reads=["h", "ident"], writes=["ps0"])
                P.op("act", lambda e: e.activation(out=b_hT.rearrange("p k n -> p (k n)"), in_=PSB16(0), func=AF.Copy), reads=["ps0"], writes=["hT"])
                def proj(c, bank):
                    def f(e):
                        r = None
                        for k in range(8):
                            r = e.matmul(PSF(bank), lhsT=b_hT[:, k, :], rhs=Win[:, k, c * 512:(c + 1) * 512], start=(k == 0), stop=(k == 7))
                        return r
                    return f
                winr = [("win", k) for k in range(8)]
                P.op("pe", proj(0, 1), reads=["hT"] + winr, writes=["ps1"])
                P.op("act", lambda e: e.activation(out=b_qtok, in_=PSF(1), func=AF.Copy), reads=["ps1"], writes=["qtok"])
                P.op("pe", proj(1, 2), reads=["hT"] + winr, writes=["ps2"])
                P.op("dve", lambda e: e.tensor_copy(out=b_ktok, in_=PSF(2)), reads=["ps2"], writes=["ktok"])
                P.op("pe", proj(2, 1), reads=["hT"] + winr, writes=["ps1"])
                P.op("act", lambda e, s=s: e.activation(out=b_vaug[s].rearrange("p (h n) -> p h n", h=8)[:, :, 0:64],
                                                        in_=PSF(1).rearrange("p (h n) -> p h n", h=8), func=AF.Copy),
                     reads=["ps1"], writes=[("vaug", s)])
                DMA("sp", vna[t], b_vaug[s], [("vaug", s)], [("vna", t)], ("vaug", s))
                def tr_na(src):
                    def f(e):
                        r = None
                        for h in range(8):
                            r = e.transpose(PSB16(3, 64)[:, h * 128:(h + 1) * 128], src[:, h * 64:(h + 1) * 64], ident[:])
                        return r
                    return f
                P.op("pe", tr_na(b_qtok), reads=["qtok", "ident"], writes=["ps3"])
                P.op("dve", lambda e: e.tensor_copy(out=b_qT.rearrange("p h n -> p (h n)"), in_=PSB16(3, 64)), reads=["ps3"], writes=["qT"])
                DMA("sp", qna[:, :, tok].rearrange("h d k -> d h k"), b_qT, ["qT"], [("qna", t)], "qT")
                P.op("pe", tr_na(b_ktok), reads=["ktok", "ident"], writes=["ps3"])
                P.op("act", lambda e: e.activation(out=b_kT.rearrange("p h n -> p (h n)"), in_=PSB16(3, 64), func=AF.Copy), reads=["ps3"], writes=["kT"])
                DMA("sp", kna[:, :, tok].rearrange("h d k -> d h k"), b_kT, ["kT"], [("kna", t)], "kT")
                P.op("pe", proj(3, 2), reads=["hT"] + winr, writes=["ps2"])
                P.op("act", lambda e: e.activation(out=f_rqk, in_=PSF(2), func=AF.Copy), reads=["ps2"], writes=["rqk"])
                P.op("pe", proj(4, 1), reads=["hT"] + winr, writes=["ps1"])
                P.op("dve", lambda e: e.tensor_copy(out=b_rv, in_=PSF(1)), reads=["ps1"], writes=["rv"])
                P.op("pe", proj(5, 2), reads=["hT"] + winr, writes=["ps2"])
                P.op("act", lambda e, s=s: e.activation(out=f_sg[s], in_=PSF(2), func=AF.Silu), reads=["ps2"], writes=[("sg", s)])
                DMA("sp", sg_d[t], f_sg[s], [("sg", s)], [("sgd", t)], ("sg", s))
                x4 = f_rqk.rearrange("p (g a c) -> p g a c", g=8, a=2)
                r4 = f_rot.rearrange("p (g a c) -> p g a c", g=8, a=2)
                x1, x2 = x4[:, :, 0, :], x4[:, :, 1, :]
                csap = f_cs[s]
                cosb = ap_of(csap.tensor, csap.offset, [csap.ap[0], [0, 8], [1, 32]])
                sinb = ap_of(csap.tensor, csap.offset + 32, [csap.ap[0], [0, 8], [1, 32]])
                tv = [f_t[i].rearrange("p (g c) -> p g c", g=8) for i in range(4)]
                rd = ["rqk", ("cs", s)]
                P.op("dve", lambda e, x1=x1, cosb=cosb: e.tensor_tensor(out=tv[0], in0=x1, in1=cosb, op=ALU.mult), reads=rd, writes=["t0"])
                P.op("dve", lambda e, x2=x2, sinb=sinb: e.tensor_tensor(out=tv[1], in0=x2, in1=sinb, op=ALU.mult), reads=rd, writes=["t1"])
                P.op("dve", lambda e, r4=r4: e.tensor_tensor(out=r4[:, :, 0, :], in0=tv[0], in1=tv[1], op=ALU.subtract), reads=["t0", "t1"], writes=["rot0"])
                P.op("pool", lambda e, x1=x1, sinb=sinb: e.tensor_tensor(out=tv[2], in0=x1, in1=sinb, op=ALU.mult), reads=rd, writes=["t2"])
                P.op("pool", lambda e, x2=x2, cosb=cosb: e.tensor_tensor(out=tv[3], in0=x2, in1=cosb, op=ALU.mult), reads=rd, writes=["t3"])
                P.op("pool", lambda e, r4=r4: e.tensor_tensor(out=r4[:, :, 1, :], in0=tv[2], in1=tv[3], op=ALU.add), reads=["t2", "t3"], writes=["rot1"])
                ro = f_rot
                qin = ap_of(ro.tensor, ro.offset, [ro.ap[0], [64, 4], [0, 2], [1, 64]])
                kin = ap_of(ro.tensor, ro.offset + 256, [ro.ap[0], [64, 4], [0, 2], [1, 64]])
                tsq = ap_of(TS, 0, [TS[:, 0:1].ap[0], [1, 4], [4, 2], [0, 64]])
                tsk = ap_of(TS, 8, [TS[:, 0:1].ap[0], [1, 4], [4, 2], [0, 64]])
                P.op("dve", lambda e, qin=qin: e.tensor_tensor(out=b_Qx.rearrange("p (h a c) -> p h a c", h=4, a=2), in0=qin, in1=tsq, op=ALU.mult),
                     reads=["rot0", "rot1", "TS"], writes=["Qx"])
                P.op("pool", lambda e, kin=kin: e.tensor_tensor(out=b_Kx.rearrange("p (h a c) -> p h a c", h=4, a=2), in0=kin, in1=tsk, op=ALU.mult),
                     reads=["rot0", "rot1", "TS"], writes=["Kx"])
                def tr_ret(e):
                    r = None
                    for h in range(4):
                        r = e.transpose(PSB16(4)[:, h * 128:(h + 1) * 128], b_Qx[:, h * 128:(h + 1) * 128], ident[:])
                    for h in range(4):
                        r = e.transpose(PSB16(3, 64)[:, h * 128:(h + 1) * 128], b_Kx[:, h * 128:h * 128 + 64], ident[:])
                    return r
                P.op("pe", tr_ret, reads=["Qx", "Kx", "ident"], writes=["ps4", "ps3"])
                P.op("act", lambda e, s=s: e.activation(out=b_QTx[s], in_=PSB16(4)[:, 0:512], func=AF.Copy), reads=["ps4"], writes=[("QTx", s)])
                P.op("dve", lambda e: e.tensor_copy(out=b_KT.rearrange("p h n -> p (h n)"), in_=PSB16(3, 64)[:, 0:512]), reads=["ps3"], writes=["KT"])
                DMA("sp", qtx[t], b_QTx[s], [("QTx", s)], [("qtx", t)], ("QTx", s))
                def st_mm(e, s=s):
                    r = None
                    for h in range(4):
                        r = e.matmul(PSF(5, 128, h * 128, (h + 1) * 128), lhsT=b_KT[:, h, :], rhs=b_QTx[s][0:64, h * 128:(h + 1) * 128], start=True, stop=True)
                    return r
                P.op("pe", st_mm, reads=["KT", ("QTx", s)], writes=["ps5"])
                P.op("dve", lambda e: e.tensor_tensor(out=b_SM, in0=PSF(5), in1=Mpp[:], op=ALU.mult), reads=["ps5", "MppF"], writes=["SM"])
                def in_mm(e):
                    r = None
                    for h in range(4):
                        r = e.matmul(PSF(6, 128, h * 128, (h + 1) * 128), lhsT=b_SM[:, h * 128:(h + 1) * 128], rhs=b_rv[:, h * 128:(h + 1) * 128], start=True, stop=True)
                    return r
                P.op("pe", in_mm, reads=["SM", "rv"], writes=["ps6"])
                P.op("act", lambda e, s=s: e.activation(out=f_intra[s], in_=PSF(6), func=AF.Copy), reads=["ps6"], writes=[("intra", s)])
                DMA("sp", intra_d[t], f_intra[s], [("intra", s)], [("intrad", t)], ("intra", s))
                def kv_mm(e):
                    r = None
                    for h in range(4):
                        r = e.matmul(PSF(7, 128, h * 128, (h + 1) * 128), lhsT=b_Kx[:, h * 128:(h + 1) * 128], rhs=b_rv[:, h * 128:(h + 1) * 128], start=True, stop=True)
                    return r
                P.op("pe", kv_mm, reads=["Kx", "rv"], writes=["ps7"])
                P.op("dve", lambda e, t=t: e.tensor_copy(out=KV[0:64, t, :], in_=PSF(7, 64)), reads=["ps7"], writes=[("KVf", t)])
                P.op("dve", lambda e, t=t: e.tensor_copy(out=KV[64:128, 31 - t, :], in_=psb[7][64:128, :]), reads=["ps7"], writes=[("KVb", 31 - t)])
            P.barrier()

            if stop <= 1:
                break
            ACC = WF[:, 0:512]
            G2 = WF[:, 512:1536]
            vlo = WB[:, 0:2560]
            vhi = WB[:, 2560:5120]
            P.op("pool", lambda e: e.memset(ACC, 0.0), writes=["ACC"])
            for s_ in range(32):
                P.op("dve", lambda e: e.tensor_tensor(out=ACC, in0=ACC, in1=Dfull[:], op=ALU.mult), reads=["ACC", "Dfull"], writes=["ACC"])
                P.op("dve", lambda e, s_=s_: e.tensor_tensor(out=ACC, in0=ACC, in1=KV[:, s_, :], op=ALU.add), reads=["ACC"], writes=["ACC"])
            DMA("sp", cst_in, ACC, ["ACC"], ["cst_in"], "cst")
            for i, kt in enumerate([0, 1, 30, 31]):
                DMA("sp", chal_in[(i // 2) * 64:(i // 2 + 1) * 64, (i % 2) * 1024:(i % 2 + 1) * 1024].rearrange("d (h k) -> d h k", h=8),
                    kna[:, :, kt * 128:(kt + 1) * 128].rearrange("h d k -> d h k"), [], [("chk", i)], ("chk", i))
                DMA("sp", chal_in[:, 2048 + i * 640:2048 + (i + 1) * 640], vna[kt], [], [("chv", i)], ("chv", i))

            def cc1(e):
                cc_count[0] += 1
                i = e.collective_compute("AllGather", ALU.bypass, replica_groups=RG, ins=[cst_in], outs=[cst_out])
                i.then_inc(ccsem)
                e.wait_ge(ccsem, cc_count[0])
                return e.memset(ccdummy[:], 0.0)

            def cc2(e):
                cc_count[0] += 1
                i = e.collective_compute("AllGather", ALU.bypass, replica_groups=RG, ins=[chal_in], outs=[chal_out])
                i.then_inc(ccsem)
                e.wait_ge(ccsem, cc_count[0])
                return e.memset(ccdummy[:], 0.0)
            P.op("pool", cc1, reads=["cst_in"], writes=["cst_out"])
            P.op("pool", cc2, reads=[("chk", i) for i in range(4)] + [("chv", i) for i in range(4)], writes=["chal_out"])
            DMA("sp", G2.rearrange("p (r n) -> p r n", r=2), cst_out.rearrange("(r p) n -> p r n", p=128), ["cst_out"], ["G2"], "G2")
            P.op("dve", lambda e: e.tensor_scalar(out=ACC, in0=G2[:, 0:512], scalar1=cmask[:, 0:1], scalar2=None, op0=ALU.mult), reads=["G2", "cmask"], writes=["ACC"])
            P.op("dve", lambda e: e.scalar_tensor_tensor(out=ACC, in0=G2[:, 512:1024], scalar=cmask[:, 1:2], in1=ACC, op0=ALU.mult, op1=ALU.add),
                 reads=["G2", "cmask", "ACC"], writes=["ACC"])
            for s_ in range(32):
                P.op("act", lambda e, s_=s_: e.activation(out=State[0:64, s_, :], in_=ACC[0:64, :], func=AF.Copy), reads=["ACC"], writes=[("StateF", s_)])
                P.op("act", lambda e, s_=s_: e.activation(out=State[64:128, 31 - s_, :], in_=ACC[64:128, :], func=AF.Copy), reads=["ACC"], writes=[("StateB", s_)])
                P.op("dve", lambda e: e.tensor_tensor(out=ACC, in0=ACC, in1=Dfull[:], op=ALU.mult), reads=["ACC", "Dfull"], writes=["ACC"])
                P.op("dve", lambda e, s_=s_: e.tensor_tensor(out=ACC, in0=ACC, in1=KV[:, s_, :], op=ALU.add), reads=["ACC"], writes=["ACC"])
            DMA("sp", klo[:, 0:2048], chal_out[0:64, 0:2048], ["chal_out"], ["klo0"], "klo0")
            DMA("sp", klo[:, 2048:4096], chal_out[64:128, 0:2048], ["chal_out"], ["klo1"], "klo1")
            DMA("sp", khi[:, 0:2048], chal_out[128:192, 0:2048], ["chal_out"], ["khi0"], "khi0")
            DMA("sp", khi[:, 2048:4096], chal_out[192:256, 0:2048], ["chal_out"], ["khi1"], "khi1")
            DMA("sp", vlo, chal_out[0:128, 2048:HW], ["chal_out"], ["vlo"], "vlo")
            DMA("sp", vhi, chal_out[128:256, 2048:HW], ["chal_out"], ["vhi"], "vhi")
            P.op("pool", lambda e: e.tensor_scalar(out=khalo[:], in0=klo, scalar1=cmask[0:64, 2:3], scalar2=None, op0=ALU.mult), reads=["klo0", "klo1", "cmask"], writes=["khalo"])
            P.op("dve", lambda e: e.scalar_tensor_tensor(out=khalo[:], in0=khi, scalar=cmask[0:64, 3:4], in1=khalo[:], op0=ALU.mult, op1=ALU.add),
                 reads=["khi0", "khi1", "cmask", "khalo"], writes=["khalo"])
            P.op("pool", lambda e: e.tensor_scalar(out=vhalo[:], in0=vlo, scalar1=cmask[:, 2:3], scalar2=None, op0=ALU.mult), reads=["vlo", "cmask"], writes=["vhalo"])
            P.op("dve", lambda e: e.scalar_tensor_tensor(out=vhalo[:], in0=vhi, scalar=cmask[:, 3:4], in1=vhalo[:], op0=ALU.mult, op1=ALU.add),
                 reads=["vhi", "cmask", "vhalo"], writes=["vhalo"])
            P.barrier()

            if stop <= 2:
                break
            maskT = WB[:, 0:1024]
            DMA("sp", maskT, mask_d, [], ["maskT"], "maskT")
            for h in range(8):
                DMA("sp", gstage[:, (h % 2) * 1024:(h % 2 + 1) * 1024], gtab_d[l, :, h * 1024:(h + 1) * 1024], [], [("gst", h % 2)], ("gst", h % 2))
                P.op("act", lambda e, h=h: e.activation(out=gstage[:, (h % 2) * 1024:(h % 2 + 1) * 1024], in_=gstage[:, (h % 2) * 1024:(h % 2 + 1) * 1024], func=AF.Exp),
                     reads=[("gst", h % 2)], writes=[("gst", h % 2)])
                P.op("dve", lambda e, h=h: e.tensor_tensor(out=EB[:, h, :, :].rearrange("p v q -> p (v q)"), in0=gstage[:, (h % 2) * 1024:(h % 2 + 1) * 1024], in1=maskT, op=ALU.mult),
                     reads=[("gst", h % 2), "maskT"], writes=[("EB", h)])
            b_e = [WB[:, 1024:1536], WB[:, 1536:2048]]
            b_p = [WB[:, 2048:2560], WB[:, 2560:3072]]
            b_nat = WB[0:64, 3072:7168].rearrange("p (h n) -> p h n", h=8)
            f_rc = WF[:, 0:512]
            f_bc = WF[0:64, 512:1024]
            kh4 = khalo[:].rearrange("p (i h k) -> p i h k", i=4, h=8)
            vh4 = vhalo[:].rearrange("p (i n) -> p i n", i=4)
            ucount = 0
            for g in range(8):
                w0, w1 = max(0, 4 * g - 2), min(31, 4 * g + 5)
                nw = w1 - w0 + 1
                gt = slice(g * 512, (g + 1) * 512)
                DMA("sp", NAq, qna[:, :, gt].rearrange("h d k -> d h k"), [], ["NAq"], "NAq")
                DMA("sp", NAk[:, :, 0:nw * 128], kna[:, :, w0 * 128:(w1 + 1) * 128].rearrange("h d k -> d h k"), [], ["NAk"], "NAk")
                DMA("sp", NAv[:, 0:nw, :], vna[w0:w1 + 1].rearrange("t p n -> p t n"), [], ["NAv"], "NAv")
                units = na_units(g)
                for h in range(8):
                    ob = 2 + (h % 2)
                    P.op("pe", lambda e, ob=ob: e.matmul(PSF(ob, 65), lhsT=zerosb[0:1, 0:65], rhs=zerosb[0:1, 0:512], start=True, stop=False),
                         reads=["zerosb"], writes=[("ps", ob)])
                    for ui, (kr, r0, r1, kind) in enumerate(units):
                        sbk = ucount % 2
                        ucount += 1
                        nq = r1 - r0 + 1
                        c0 = (r0 - 8 * g) * 64
                        c1 = c0 + nq * 64
                        if kr < 0:
                            kap = kh4[:, (kr + 4) // 2 + 2, h, :]
                            vap = vh4[:, (kr + 4) // 2 + 2, h * 80:h * 80 + 65]
                            kres, vres = "khalo", "vhalo"
                        elif kr >= 64:
                            kap = kh4[:, (kr - 64) // 2, h, :]
                            vap = vh4[:, (kr - 64) // 2, h * 80:h * 80 + 65]
                            kres, vres = "khalo", "vhalo"
                        else:
                            wi = kr // 2 - w0
                            kap = NAk[:, h, wi * 128:(wi + 1) * 128]
                            vap = NAv[:, wi, h * 80:h * 80 + 65]
                            kres, vres = "NAk", "NAv"
                        P.op("pe", lambda e, sbk=sbk, kap=kap, c0=c0, c1=c1, nq=nq, h=h: e.matmul(PSF(sbk, 128, 0, nq * 64), lhsT=kap, rhs=NAq[:, h, c0:c1], start=True, stop=True),
                             reads=[kres, "NAq"], writes=[("ps", sbk)])
                        P.op("act", lambda e, sbk=sbk, nq=nq: e.activation(out=b_e[sbk][:, 0:nq * 64], in_=PSF(sbk, 128, 0, nq * 64), func=AF.Exp, scale=0.125),
                             reads=[("ps", sbk)], writes=[("e", sbk)])
                        meng = "dve" if (ucount % 2 == 0) else "pool"
                        if kind == "int":
                            v0 = 3 - (kr - r0)
                            P.op(meng, lambda e, sbk=sbk, nq=nq, v0=v0, h=h: e.tensor_tensor(
                                out=b_p[sbk][:, 0:nq * 64], in0=b_e[sbk][:, 0:nq * 64],
                                in1=EB[:, h, v0:v0 + nq, :].rearrange("p v q -> p (v q)"), op=ALU.mult),
                                reads=[("e", sbk), ("EB", h)], writes=[("p", sbk)])
                        else:
                            for r in range(r0, r1 + 1):
                                vi = VIDX_BOTH[kr - r]
                                if r0 == 0:
                                    u = r * 6 + TOP_KR.index(kr)
                                else:
                                    u = 24 + (r - 61) * 6 + BOT_KR.index(kr)
                                j = r - r0
                                P.op("dve", lambda e, sbk=sbk, j=j, vi=vi, u=u, h=h: e.scalar_tensor_tensor(
                                    out=b_p[sbk][:, j * 64:(j + 1) * 64], in0=b_e[sbk][:, j * 64:(j + 1) * 64], scalar=rmt[:, u:u + 1],
                                    in1=EB[:, h, vi, :], op0=ALU.mult, op1=ALU.mult),
                                    reads=[("e", sbk), ("EB", h), "rmt"], writes=[("p", sbk)])
                        last = ui == len(units) - 1
                        P.op("pe", lambda e, ob=ob, vap=vap, sbk=sbk, c0=c0, c1=c1, nq=nq, last=last: e.matmul(
                            PSF(ob, 65, c0, c1), lhsT=vap, rhs=b_p[sbk][:, 0:nq * 64], start=False, stop=last),
                            reads=[vres, ("p", sbk)], writes=[("ps", ob)])
                    P.op("dve", lambda e, ob=ob: e.reciprocal(out=f_rc[64:65, :], in_=psb[ob][64:65, :]), reads=[("ps", ob)], writes=["rc"])
                    P.op("pe", lambda e: e.matmul(PSF(4, 64), lhsT=onesf[64:65, 0:64], rhs=f_rc[64:65, :], start=True, stop=True), reads=["rc", "onesf"], writes=[("ps", 4)])
                    P.op("act", lambda e: e.activation(out=f_bc, in_=PSF(4, 64), func=AF.Copy), reads=[("ps", 4)], writes=["bc"])
                    P.op("dve", lambda e, ob=ob, h=h: e.tensor_tensor(out=b_nat[:, h, :], in0=PSF(ob, 64), in1=f_bc, op=ALU.mult), reads=[("ps", ob), "bc"], writes=["natsb"])
                DMA("sp", nat[:, :, gt].rearrange("h d k -> d h k"), b_nat, ["natsb"], [("nat", g)], "natsb")
            P.barrier()

            if stop <= 3:
                break
            DMA("pool", WoutNA, w_out[l, 0:512, :].rearrange("(h d) n -> d h n", d=64), [], ["WoutNA"], "WoutNA")
            DMA("pool", WoutR, w_out[l, 512:1024, :].rearrange("(k p) n -> p k n", p=128), [], ["WoutR"], "WoutR")
            DMA("sp", gret[:], gret_d[l, :].partition_broadcast(128), [], ["gret"], "gret")
            f_x = [WF[:, 0:1024], WF[:, 1024:2048]]
            f_in = [WF[:, 2048:2560], WF[:, 2560:3072]]
            f_sgl = [WF[:, 3072:3584], WF[:, 3584:4096]]
            f_y = WF[:, 4096:4608]
            f_ysq = WF[:, 4608:5120]
            f_st2 = WF[:, 5120:5184]
            f_xo = [WB[:, 0:2048].bitcast(F32), WB[:, 2048:4096].bitcast(F32)]
            b_qx = [WB[:, 4096:4608], WB[:, 4608:5120]]
            b_na = [WB[0:64, 5120:6144].rearrange("p (h n) -> p h n", h=8), WB[0:64, 6144:7168].rearrange("p (h n) -> p h n", h=8)]
            b_mix = WB[:, 7168:7680]
            b_mixT = WB[:, 7680:8192].rearrange("p (k n) -> p k n", k=4)
            for t in range(NT):
                s = t % 2
                tok = slice(t * 128, (t + 1) * 128)
                DMA("sp", f_x[s], xsrc[tok, :], [], [("x2", s)], ("x2", s))
                DMA("sp", b_qx[s], qtx[t], [], [("qx2", s)], ("qx2", s))
                DMA("sp", f_in[s], intra_d[t], [], [("in2", s)], ("in2", s))
                DMA("sp", f_sgl[s], sg_d[t], [], [("sg2", s)], ("sg2", s))
                DMA("sp", b_na[s], nat[:, :, tok].rearrange("h d k -> d h k"), [], [("na2", s)], ("na2", s))

                def cross(e, s=s, t=t):
                    r = None
                    for h in range(4):
                        r = e.matmul(PSF(0, 128, h * 128, (h + 1) * 128), lhsT=b_qx[s][:, h * 128:(h + 1) * 128], rhs=State[:, t, h * 128:(h + 1) * 128], start=True, stop=True)
                    return r
                P.op("pe", cross, reads=[("qx2", s)], writes=[("ps", 0)])
                P.op("dve", lambda e, s=s: e.tensor_tensor(out=f_y, in0=PSF(0), in1=f_in[s], op=ALU.add), reads=[("ps", 0), ("in2", s)], writes=["y"])
                y3 = f_y.rearrange("p (h n) -> p h n", h=4)
                for h in range(4):
                    P.op("act", lambda e, h=h: e.activation(out=f_ysq[:, h * 128:(h + 1) * 128], in_=f_y[:, h * 128:(h + 1) * 128], func=AF.Copy, accum_out=f_st2[:, h:h + 1]),
                         reads=["y"], writes=[("ysqa", h), ("s1", h)])
                for h in range(4):
                    P.op("act", lambda e, h=h: e.activation(out=f_ysq[:, h * 128:(h + 1) * 128], in_=f_y[:, h * 128:(h + 1) * 128], func=AF.Square, accum_out=f_st2[:, 4 + h:5 + h]),
                         reads=["y", ("ysqa", h)], writes=[("ysqa", h), ("s2", h)])
                P.op("dve", lambda e: e.tensor_scalar(out=f_st2[:, 8:12], in0=f_st2[:, 0:4], scalar1=1.0 / 128, scalar2=None, op0=ALU.mult), reads=[("s1", h) for h in range(4)], writes=["mean"])
                P.op("dve", lambda e: e.tensor_tensor(out=f_st2[:, 12:16], in0=f_st2[:, 8:12], in1=f_st2[:, 8:12], op=ALU.mult), reads=["mean"], writes=["msq"])
                P.op("dve", lambda e: e.scalar_tensor_tensor(out=f_st2[:, 16:20], in0=f_st2[:, 4:8], scalar=1.0 / 128, in1=f_st2[:, 12:16], op0=ALU.mult, op1=ALU.subtract),
                     reads=[("s2", h) for h in range(4)] + ["msq"], writes=["var"])
                P.op("dve", lambda e: e.tensor_scalar(out=f_st2[:, 24:28], in0=f_st2[:, 16:20], scalar1=EPS, scalar2=None, op0=ALU.add), reads=["var"], writes=["vare"])
                P.op("act", lambda e: e.activation(out=f_st2[:, 28:32], in_=f_st2[:, 24:28], func=AF.Sqrt), reads=["vare"], writes=["sqv2"])
                P.op("dve", lambda e: e.reciprocal(out=f_st2[:, 20:24], in_=f_st2[:, 28:32]), reads=["sqv2"], writes=["rstd2"])
                st = f_st2
                meanb = ap_of(st.tensor, st.offset + 8, [st.ap[0], [1, 4], [0, 128]])
                rstdb = ap_of(st.tensor, st.offset + 20, [st.ap[0], [1, 4], [0, 128]])
                P.op("dve", lambda e, y3=y3, meanb=meanb: e.tensor_tensor(out=y3, in0=y3, in1=meanb, op=ALU.subtract), reads=["y", "mean"], writes=["y"])
                P.op("pool", lambda e, y3=y3, rstdb=rstdb: e.tensor_tensor(out=y3, in0=y3, in1=rstdb, op=ALU.mult), reads=["y", "rstd2"], writes=["y"])
                P.op("dve", lambda e: e.tensor_tensor(out=f_y, in0=f_y, in1=gret[:], op=ALU.mult), reads=["y", "gret"], writes=["y"])
                P.op("pool", lambda e, s=s: e.tensor_tensor(out=b_mix, in0=f_y, in1=f_sgl[s], op=ALU.mult), reads=["y", ("sg2", s)], writes=["mix"])

                def tr_mix(e):
                    r = None
                    for k in range(4):
                        r = e.transpose(PSB16(1)[:, k * 128:(k + 1) * 128], b_mix[:, k * 128:(k + 1) * 128], ident[:])
                    return r
                P.op("pe", tr_mix, reads=["mix", "ident"], writes=[("ps", 1)])
                P.op("act", lambda e: e.activation(out=b_mixT.rearrange("p k n -> p (k n)"), in_=PSB16(1)[:, 0:512], func=AF.Copy), reads=[("ps", 1)], writes=["mixT"])
                for c in range(2):
                    def oproj(e, c=c, s=s):
                        r = None
                        for h in range(8):
                            e.matmul(PSF(2 + c), lhsT=b_na[s][:, h, :], rhs=WoutNA[:, h, c * 512:(c + 1) * 512], start=(h == 0), stop=False)
                        for k in range(4):
                            r = e.matmul(PSF(2 + c), lhsT=b_mixT[:, k, :], rhs=WoutR[:, k, c * 512:(c + 1) * 512], start=False, stop=(k == 3))
                        return r
                    P.op("pe", oproj, reads=[("na2", s), "mixT", "WoutNA", "WoutR"], writes=[("ps", 2 + c)])
                    P.op("dve", lambda e, c=c, s=s: e.tensor_tensor(out=f_xo[s][:, c * 512:(c + 1) * 512], in0=PSF(2 + c), in1=f_x[s][:, c * 512:(c + 1) * 512], op=ALU.add),
                         reads=[("ps", 2 + c), ("x2", s)], writes=[("xo", s, c)])
                DMA("sp", xs[tok, :], f_xo[s], [("xo", s, 0), ("xo", s, 1)], [("xs", t)], ("xo", s))
            P.barrier()

            if stop <= 4:
                break
            for k in range(8):
                DMA("pool", Wup[:, k, :], w_up[l, k * 128:(k + 1) * 128, :], [], [("wup", k)], ("wup", k))
            for f8 in range(8):
                DMA("pool", Wdn[:, f8 * 4:(f8 + 1) * 4, :], w_dn[l, f8 * 512:(f8 + 1) * 512, :].rearrange("(f p) n -> p f n", p=128), [], [("wdn", f8)], ("wdn", f8))
            DMA("sp", gain[:], nmlp[l, :].partition_broadcast(128), [], ["gain"], "gain")
            lastl = l == depth - 1
            if lastl:
                DMA("sp", gret[:], nfin[0, 0:512].partition_broadcast(128), [], ["gf0"], "gf0")
                gfin2 = WF[:, 6144:6656]
                DMA("sp", gfin2, nfin[0, 512:1024].partition_broadcast(128), [], ["gf1"], "gf1")
            m_x = [WF[:, 0:2048], WF[:, 2048:4096]]
            m_r = [WF[:, 4096:4352], WF[:, 4352:4608]]
            m_st = WF[:, 4608:4672]
            m_h = WB[:, 0:2048]
            m_hT = WB[:, 2048:4096].rearrange("p (k n) -> p k n", k=8)
            m_hid = [WB[:, 4096 + i * 256:4096 + (i + 1) * 256] for i in range(4)]
            m_junk = WB[:, 5120:6144]
            wupr = [("wup", k) for k in range(8)]
            for gi in range(NT // 2):
                s = gi % 2
                tok2 = slice(gi * 256, (gi + 1) * 256)
                DMA("sp", m_x[s].rearrange("p (j n) -> p j n", j=2), xs[tok2, :].rearrange("(j p) n -> p j n", p=128), [], [("mx", s)], ("mx", s))
                for j in range(2):
                    P.op("act", lambda e, s=s, j=j: e.activation(out=m_junk, in_=m_x[s][:, j * 1024:(j + 1) * 1024], func=AF.Square, accum_out=m_st[:, j:j + 1]),
                         reads=[("mx", s)], writes=["mjunk", ("mssq", j)])
                    P.op("dve", lambda e, j=j: e.tensor_scalar(out=m_st[:, 2 + j:3 + j], in0=m_st[:, j:j + 1], scalar1=1.0 / D, scalar2=EPS, op0=ALU.mult, op1=ALU.add),
                         reads=[("mssq", j)], writes=[("mms", j)])
                    P.op("act", lambda e, j=j: e.activation(out=m_st[:, 6 + j:7 + j], in_=m_st[:, 2 + j:3 + j], func=AF.Sqrt), reads=[("mms", j)], writes=[("msq", j)])
                    P.op("dve", lambda e, j=j: e.reciprocal(out=m_st[:, 4 + j:5 + j], in_=m_st[:, 6 + j:7 + j]), reads=[("msq", j)], writes=[("mrstd", j)])
                    P.op("dve", lambda e, s=s, j=j: e.scalar_tensor_tensor(out=m_h[:, j * 1024:(j + 1) * 1024], in0=m_x[s][:, j * 1024:(j + 1) * 1024],
                                                                                              scalar=m_st[:, 4 + j:5 + j], in1=gain[:], op0=ALU.mult, op1=ALU.mult),
                         reads=[("mx", s), ("mrstd", j), "gain"], writes=[("mh", j)])

                    def tr_m(e, j=j):
                        r = None
                        for k in range(8):
                            r = e.transpose(PSB16(6 + j)[:, k * 128:(k + 1) * 128], m_h[:, j * 1024 + k * 128:j * 1024 + (k + 1) * 128], ident[:])
                        return r
                    P.op("pe", tr_m, reads=[("mh", j), "ident"], writes=[("ps", 6 + j)])
                    P.op("act", lambda e, j=j: e.activation(out=m_hT[:, :, j * 128:(j + 1) * 128], in_=PSB16(6 + j).rearrange("p (k n) -> p k n", k=8), func=AF.Copy),
                         reads=[("ps", 6 + j)], writes=[("mhT", j)])
                for f in range(32):
                    ub = 4 + (f % 2)
                    hb = f % 4

                    def up(e, f=f, ub=ub):
                        r = None
                        for k in range(8):
                            r = e.matmul(PSF(ub, 128, 0, 256), lhsT=Wup[:, k, f * 128:(f + 1) * 128], rhs=m_hT[:, k, :], start=(k == 0), stop=(k == 7))
                        return r
                    P.op("pe", up, reads=[("mhT", 0), ("mhT", 1)] + wupr, writes=[("ps", ub)])
                    P.op("act", lambda e, ub=ub, f=f: e.activation(out=m_r[f % 2], in_=PSF(ub, 128, 0, 256), func=AF.Relu), reads=[("ps", ub)], writes=[("mr", f % 2)])
                    P.op("dve" if f % 2 == 0 else "pool", lambda e, f=f, hb=hb: e.tensor_tensor(out=m_hid[hb], in0=m_r[f % 2], in1=m_r[f % 2], op=ALU.mult),
                         reads=[("mr", f % 2)], writes=[("mhid", hb)])

                    def down(e, f=f, hb=hb):
                        r = None
                        for j in range(2):
                            for c in range(2):
                                r = e.matmul(PSF(j * 2 + c), lhsT=m_hid[hb][:, j * 128:(j + 1) * 128], rhs=Wdn[:, f, c * 512:(c + 1) * 512], start=(f == 0), stop=(f == 31))
                        return r
                    P.op("pe", down, reads=[("mhid", hb), ("wdn", f // 4)], writes=[("ps", 0), ("ps", 1), ("ps", 2), ("ps", 3)])
                for j in range(2):
                    for c in range(2):
                        P.op("dve", lambda e, s=s, j=j, c=c: e.tensor_tensor(out=m_x[s][:, j * 1024 + c * 512:j * 1024 + (c + 1) * 512], in0=PSF(j * 2 + c),
                                                                              in1=m_x[s][:, j * 1024 + c * 512:j * 1024 + (c + 1) * 512], op=ALU.add),
                             reads=[("ps", j * 2 + c), ("mx", s)], writes=[("mx", s)])
                if not lastl:
                    DMA("sp", xs[tok2, :].rearrange("(j p) n -> p j n", p=128), m_x[s].rearrange("p (j n) -> p j n", j=2), [("mx", s)], [("xsm", gi)], ("mx", s))
                else:
                    for j in range(2):
                        P.op("act", lambda e, s=s, j=j: e.activation(out=m_junk, in_=m_x[s][:, j * 1024:(j + 1) * 1024], func=AF.Square, accum_out=m_st[:, 8 + j:9 + j]),
                             reads=[("mx", s)], writes=["mjunk", ("fssq", j)])
                        P.op("dve", lambda e, j=j: e.tensor_scalar(out=m_st[:, 10 + j:11 + j], in0=m_st[:, 8 + j:9 + j], scalar1=1.0 / D, scalar2=EPS, op0=ALU.mult, op1=ALU.add),
                             reads=[("fssq", j)], writes=[("fms", j)])
                        P.op("act", lambda e, j=j: e.activation(out=m_st[:, 14 + j:15 + j], in_=m_st[:, 10 + j:11 + j], func=AF.Sqrt), reads=[("fms", j)], writes=[("fsq", j)])
                        P.op("dve", lambda e, j=j: e.reciprocal(out=m_st[:, 12 + j:13 + j], in_=m_st[:, 14 + j:15 + j]), reads=[("fsq", j)], writes=[("frstd", j)])
                        P.op("dve", lambda e, s=s, j=j: e.scalar_tensor_tensor(out=m_x[s][:, j * 1024:j * 1024 + 512], in0=m_x[s][:, j * 1024:j * 1024 + 512],
                                                                                scalar=m_st[:, 12 + j:13 + j], in1=gret[:], op0=ALU.mult, op1=ALU.mult),
                             reads=[("mx", s), ("frstd", j), "gf0"], writes=[("mx", s)])
                        P.op("dve", lambda e, s=s, j=j: e.scalar_tensor_tensor(out=m_x[s][:, j * 1024 + 512:(j + 1) * 1024], in0=m_x[s][:, j * 1024 + 512:(j + 1) * 1024],
                                                                                 scalar=m_st[:, 12 + j:13 + j], in1=gfin2, op0=ALU.mult, op1=ALU.mult),
                             reads=[("mx", s), ("frstd", j), "gf1"], writes=[("mx", s)])
                    DMA("sp", out_d[tok2, :].rearrange("(j p) n -> p j n", p=128), m_x[s].rearrange("p (j n) -> p j n", j=2), [("mx", s)], [("outd", gi)], ("mx", s))
            P.barrier()
        counts = P.emit(nc)
    return nc, counts


def _const_tables():
    qc = np.arange(64)
    kc = np.arange(64)
    cs_ = np.clip(qc - 8, 0, 48)
    colvalid = (kc[:, None] >= cs_[None, :]) & (kc[:, None] < cs_[None, :] + 16)
    dcidx = np.clip(kc[:, None] - qc[None, :], -15, 15) + 15
    mask = np.zeros((128, 16, 64), np.float32)
    dridx = np.zeros((128, 16), np.int64)
    for v in range(16):
        for kin in range(2):
            ok = (VAR_MODE[v] == 0) or (VAR_MODE[v] == 1 and kin == 0) or (VAR_MODE[v] == 2 and kin == 1)
            mask[kin * 64:(kin + 1) * 64, v, :] = colvalid * (1.0 if ok else 0.0)
            dridx[kin * 64:(kin + 1) * 64, v] = np.clip(VAR_DR0[v] + kin, -7, 7) + 7
    i = np.arange(128)
    jj, ii = np.meshgrid(i, i, indexing="ij")
    A = np.where(jj <= ii, -128.0, (jj - ii - 128.0)).astype(np.float32)
    B = np.maximum(jj - ii, 0).astype(np.float32)
    ab = np.concatenate([A, B], axis=1).astype(np.float32)
    c4 = np.stack([i + 1.0, 128.0 - i, 127.0 - i, i * 1.0], axis=1).astype(np.float32)
    return mask, dridx, dcidx, ab, c4


def _rm_table(hf):
    rm = np.zeros((128, 42), np.float32)
    def valid(r_loc, krow_loc):
        gr = r_loc + 64 * hf
        gk = krow_loc + 64 * hf
        rs = min(max(gr - 4, 0), 120)
        return 1.0 if (0 <= gk <= 127 and rs <= gk <= rs + 7) else 0.0
    for r in range(4):
        for ki, kr in enumerate(TOP_KR):
            for kin in range(2):
                rm[kin * 64:(kin + 1) * 64, r * 6 + ki] = valid(r, kr + kin)
    for r in range(61, 64):
        for ki, kr in enumerate(BOT_KR):
            for kin in range(2):
                rm[kin * 64:(kin + 1) * 64, 24 + (r - 61) * 6 + ki] = valid(r, kr + kin)
    return rm


_CACHE = {}


def kernel(x, w_in, w_out, na_rpb, ret_decay_fwd, ret_decay_bwd, ret_norm_gain,
           norm_mix, norm_mlp, w_up, w_down, norm_final, _depth=DEPTH, _stop=9, _debug=False, _lite=False):
    f32 = lambda a: np.ascontiguousarray(np.asarray(a, dtype=np.float32))
    x = f32(x)
    key = (_depth, _stop, _debug, _lite)
    if key not in _CACHE:
        _CACHE[key] = build(_depth, _stop, _debug, _lite)
    nc, _ = _CACHE[key]
    mask, dridx, dcidx, ab, c4 = _const_tables()
    rpb = f32(na_rpb)
    pk = np.arange(128) % 64
    gt = rpb[:, :, dridx[:, :, None], dcidx[pk][:, None, :]]
    gtab = np.ascontiguousarray(np.transpose(gt, (0, 2, 1, 3, 4))).reshape(DEPTH, 128, 8 * 16 * 64)
    maskt = mask.reshape(128, 1024).astype(ml_dtypes.bfloat16)
    decays = np.ascontiguousarray(np.concatenate([f32(ret_decay_fwd), f32(ret_decay_bwd)], axis=1))
    ident = np.eye(128, dtype=np.float32).astype(ml_dtypes.bfloat16)
    inv = (1.0 / (np.float32(10000.0) ** (np.arange(0, 64, 2, dtype=np.float32) / np.float32(64)))).astype(np.float32)
    shared = {
        "w_in": f32(w_in)[:1 if _lite else DEPTH], "w_out": f32(w_out)[:1 if _lite else DEPTH], "w_up": f32(w_up)[:1 if _lite else DEPTH], "w_down": f32(w_down)[:1 if _lite else DEPTH],
        "norm_mix": f32(norm_mix), "norm_mlp": f32(norm_mlp), "norm_final": f32(norm_final).reshape(1, D),
        "ret_norm_gain": f32(ret_norm_gain), "decays": decays, "gtab": gtab, "maskt": maskt,
        "abt": ab, "c4t": c4, "ident": ident,
    }
    in_maps = []
    for c in range(8):
        b, hf = c // 2, c % 2
        pos = (np.arange(TOK) + hf * TOK).astype(np.float32)
        ang = (pos[:, None] * inv[None, :]).astype(np.float32)
        cs = np.concatenate([np.cos(ang), np.sin(ang)], axis=1).astype(np.float32)
        cm = np.zeros((128, 4), np.float32)
        if hf == 0:
            cm[64:, 1] = 1.0
            cm[:, 3] = 1.0
        else:
            cm[:64, 0] = 1.0
            cm[:, 2] = 1.0
        m = dict(shared)
        m["x"] = np.ascontiguousarray(x[b, hf * TOK:(hf + 1) * TOK, :])
        m["cs"] = cs
        m["cmask"] = cm
        m["rmt"] = _rm_table(hf)
        in_maps.append(m)
    res = run_bass_kernel_spmd(nc, in_maps, core_ids=list(range(8)))
    if _debug:
        return res.results
    out = np.empty((4, 2 * TOK, D), np.float32)
    for c in range(8):
        out[c // 2, (c % 2) * TOK:(c % 2 + 1) * TOK, :] = np.asarray(res.results[c]["out"], dtype=np.float32)
    return out
```

```python
import contextlib
import numpy as np
import ml_dtypes
import concourse.bass as bass
import concourse.mybir as mybir
from concourse.bass_utils import run_bass_kernel_spmd

F32 = mybir.dt.float32
BF16 = mybir.dt.bfloat16
AF = mybir.ActivationFunctionType
ALU = mybir.AluOpType
AX = mybir.AxisListType

D = 1024
TOK = 4096
NT = 32
DEPTH = 4
EPS = 1e-6
VAR_DR0 = [3, 2, 1, 0, -1, -2, -3, -4, -5, 3, 4, 5, 6, -5, -6, -7]
VAR_MODE = [1, 0, 0, 0, 0, 0, 0, 0, 2, 0, 0, 0, 0, 0, 0, 0]
VIDX_BOTH = {2: 1, 1: 2, 0: 3, -1: 4, -2: 5, -3: 6, -4: 7, 3: 9, 4: 10, 5: 11, 6: 12, -5: 13, -6: 14, -7: 15}
DBG_OUT = ('xs', 'qna', 'kna', 'vna', 'qtx', 'intra', 'sgd', 'nat')
TOP_KR = [-4, -2, 0, 2, 4, 6]
BOT_KR = [56, 58, 60, 62, 64, 66]


class _Op:
    __slots__ = ("eng", "fn", "deps", "dma", "semkey", "signal", "val", "inc", "barrier")

    def __init__(self, eng, fn, dma, semkey, inc):
        self.eng = eng
        self.fn = fn
        self.deps = ()
        self.dma = dma
        self.semkey = semkey
        self.signal = dma
        self.val = 0
        self.inc = inc
        self.barrier = False


class Prog:
    ENGS = ("pe", "act", "dve", "pool", "sp")
    SAME_ENG_SYNC = ("act", "dve", "pool")

    def __init__(self):
        self.ops = []
        self.res = {}

    def op(self, eng, fn, reads=(), writes=(), dma=False, semkey=None):
        import os
        if len(self.ops) >= int(os.environ.get('KN_MAXOPS', '100000000')):
            return -1
        deps = set()
        for r in reads:
            st = self.res.get(r)
            if st is not None and st[0] is not None:
                deps.add(st[0])
        for w in writes:
            st = self.res.get(w)
            if st is not None:
                if st[0] is not None:
                    deps.add(st[0])
                deps.update(st[1].values())
                deps.update(st[2])
        idx = len(self.ops)
        o = _Op(eng, fn, dma, semkey, 16 if dma else 1)
        pr = set()
        for j in deps:
            oj = self.ops[j]
            if (not oj.dma) and (not dma) and oj.eng == eng and eng not in self.SAME_ENG_SYNC:
                continue
            pr.add(j)
        o.deps = pr
        self.ops.append(o)
        for r in reads:
            st = self.res.setdefault(r, [None, {}, []])
            if dma:
                st[2].append(idx)
            else:
                st[1][eng] = idx
        for w in writes:
            self.res[w] = [idx, {}, []]
        return idx

    def barrier(self):
        o = _Op("sp", None, False, None, 0)
        o.barrier = True
        self.ops.append(o)
        self.res = {}

    def emit(self, nc):
        ops = self.ops
        last = {}
        for o in ops:
            if o.barrier:
                for e, lo in last.items():
                    lo.signal = True
                continue
            for j in o.deps:
                ops[j].signal = True
            if not o.dma:
                last[o.eng] = o
        engcnt = {e: 0 for e in self.ENGS}
        dmacnt = {}
        for o in ops:
            if o.barrier:
                continue
            if o.dma:
                dmacnt[o.semkey] = dmacnt.get(o.semkey, 0) + o.inc
                o.val = dmacnt[o.semkey]
            elif o.signal:
                engcnt[o.eng] += 1
                o.val = engcnt[o.eng]
        with contextlib.ExitStack() as es:
            engsem = {e: es.enter_context(nc.semaphore("s_" + e)) for e in self.ENGS}
            dmasem = {}
            for k in dmacnt:
                dmasem[k] = es.enter_context(nc.semaphore("d_%d" % len(dmasem)))
            streams = {e: [] for e in self.ENGS}
            waited = {e: {} for e in self.ENGS}
            cur_eng = {e: 0 for e in self.ENGS}
            cur_dma = {}
            for o in ops:
                if o.barrier:
                    for e in self.ENGS:
                        wl = []
                        for e2 in self.ENGS:
                            if e2 != e and cur_eng[e2] > waited[e].get(("e", e2), 0):
                                waited[e][("e", e2)] = cur_eng[e2]
                                wl.append((engsem[e2], cur_eng[e2]))
                        for k, v in cur_dma.items():
                            if v > waited[e].get(("d", k), 0):
                                waited[e][("d", k)] = v
                                wl.append((dmasem[k], v))
                        if wl:
                            streams[e].append((wl, None, None, 0))
                    continue
                need = {}
                for j in o.deps:
                    oj = ops[j]
                    if oj.dma:
                        s = ("d", oj.semkey)
                        sem = dmasem[oj.semkey]
                    else:
                        s = ("e", oj.eng)
                        sem = engsem[oj.eng]
                    if oj.val > need.get(s, (None, 0))[1]:
                        need[s] = (sem, oj.val)
                wl = []
                for s, (sem, v) in need.items():
                    if waited[o.eng].get(s, 0) < v:
                        waited[o.eng][s] = v
                        wl.append((sem, v))
                if o.dma:
                    mysem, inc = dmasem[o.semkey], o.inc
                    cur_dma[o.semkey] = o.val
                elif o.signal:
                    mysem, inc = engsem[o.eng], 1
                    cur_eng[o.eng] = o.val
                else:
                    mysem, inc = None, 0
                streams[o.eng].append((wl, o.fn, mysem, inc))
            final = [(dmasem[k], v) for k, v in dmacnt.items()]
            final += [(engsem[e], engcnt[e]) for e in self.ENGS if engcnt[e] > 0]

            def run(stream, lastw=None):
                def f(eng):
                    for wl, fn, mysem, inc in stream:
                        for sem, v in wl:
                            eng.wait_ge(sem, v)
                        if fn is None:
                            continue
                        ins = fn(eng)
                        if mysem is not None:
                            ins.then_inc(mysem, inc)
                    if lastw:
                        for sem, v in lastw:
                            eng.wait_ge(sem, v)
                return f

            with nc.Block() as block:
                block.tensor(run(streams["pe"]))
                block.scalar(run(streams["act"]))
                block.vector(run(streams["dve"]))
                block.gpsimd(run(streams["pool"]))
                block.sync(run(streams["sp"], final))
        return {e: len(streams[e]) for e in self.ENGS}


def ap_of(t, offset, dims):
    return bass.AP(t, offset, [list(d) for d in dims])


def na_units(g):
    R0 = 8 * g
    ilo, ihi = R0, R0 + 7
    bnd = None
    if g == 0:
        ilo = 4
        bnd = (0, 3, TOP_KR)
    if g == 7:
        ihi = 60
        bnd = (61, 63, BOT_KR)
    units = []
    klo = ilo - 5
    klo += klo % 2
    khi = ihi + 3
    khi -= khi % 2
    for kr in range(klo, khi + 1, 2):
        r0 = max(ilo, kr - 3)
        r1 = min(ihi, kr + 5)
        if r0 <= r1:
            units.append((kr, r0, r1, "int"))
    if bnd is not None:
        for kr in bnd[2]:
            units.append((kr, bnd[0], bnd[1], "bnd"))
    return units


def build(depth=DEPTH, stop=9, debug=False, lite=False):
    nc = bass.Bass("TRN2", target_bir_lowering=False)
    dt_in = lambda name, shape, dt=F32: nc.dram_tensor(name, list(shape), dt, kind="ExternalInput").ap()
    x_in = dt_in("x", [TOK, D])
    LD = 1 if lite else DEPTH
    w_in = dt_in("w_in", [LD, D, 3072])
    w_out = dt_in("w_out", [LD, D, D])
    w_up = dt_in("w_up", [LD, D, 4096])
    w_dn = dt_in("w_down", [LD, 4096, D])
    nmix = dt_in("norm_mix", [DEPTH, D])
    nmlp = dt_in("norm_mlp", [DEPTH, D])
    nfin = dt_in("norm_final", [1, D])
    gret_d = dt_in("ret_norm_gain", [DEPTH, 512])
    dec_d = dt_in("decays", [DEPTH, 8])
    gtab_d = dt_in("gtab", [DEPTH, 128, 8 * 16 * 64])
    mask_d = dt_in("maskt", [128, 16 * 64], BF16)
    rm_d = dt_in("rmt", [128, 42])
    cs_d = dt_in("cs", [TOK, 64])
    ab_d = dt_in("abt", [128, 256])
    c4_d = dt_in("c4t", [128, 4])
    cm_d = dt_in("cmask", [128, 4])
    id_d = dt_in("ident", [128, 128], BF16)
    out_d = nc.dram_tensor("out", [TOK, D], F32, kind="ExternalOutput").ap()

    import os as _os0
    _sw0 = _os0.environ.get("KN_SW", "")
    def dscr(name, shape, dt):
        if "e" in _sw0:
            shape = [8, 8]
        return (nc.dram_tensor(name, list(shape), dt, kind="ExternalOutput").ap() if debug and name in DBG_OUT else nc.dram_tensor(name, list(shape), dt).ap())
    xs = dscr("xs", [TOK, D], F32)
    qna = dscr("qna", [8, 64, TOK], BF16)
    kna = dscr("kna", [8, 64, TOK], BF16)
    vna = dscr("vna", [NT, 128, 640], BF16)
    qtx = dscr("qtx", [NT, 128, 512], BF16)
    intra_d = dscr("intra", [NT, 128, 512], F32)
    sg_d = dscr("sgd", [NT, 128, 512], F32)
    nat = dscr("nat", [8, 64, TOK], BF16)
    cst_in = dscr("cst_in", [128, 512], F32)
    cst_out = dscr("cst_out", [256, 512], F32)
    HW = 2048 + 2560
    chal_in = dscr("chal_in", [128, HW], BF16)
    chal_out = dscr("chal_out", [256, HW], BF16)
    import os as _os3
    RG = [[2 * i, 2 * i + 1] for i in range(int(_os3.environ.get('KN_NCORES', '8')) // 2)]

    P = Prog()
    cc_count = [0]
    with contextlib.ExitStack() as es:
        sbt = lambda name, shape, dt: es.enter_context(nc.sbuf_tensor(name, list(shape), dt))
        import os as _os2
        BIG = sbt("BIG", [128, 65536 if not _os2.environ.get("KN_SMALL") else 32768], BF16)
        WF = sbt("WF", [128, 7168], F32)
        WB = sbt("WB", [128, 12288], BF16)
        ident = sbt("ident_s", [128, 128], BF16)
        onesf = sbt("onesf", [128, 64], F32)
        zerosb = sbt("zerosb", [128, 512], BF16)
        ABt = sbt("ABt", [128, 256], F32)
        C4 = sbt("C4", [128, 4], F32)
        cmask = sbt("cmask_s", [128, 4], F32)
        lg = sbt("lg", [128, 8], F32)
        lgp = sbt("lgp", [128, 4], F32)
        TS = sbt("TS", [128, 16], F32)
        Mpp = sbt("Mpp", [128, 512], F32)
        Dfull = sbt("Dfull", [128, 512], F32)
        rmt = sbt("rmt_s", [128, 42], F32)
        gain = sbt("gain", [128, 1024], F32)
        gret = sbt("gret", [128, 512], F32)
        khalo = sbt("khalo", [64, 4096], BF16)
        vhalo = sbt("vhalo", [128, 2560], BF16)
        small = sbt("small", [128, 64], F32)
        ccdummy = sbt("ccdummy", [128, 8], F32)
        ccsem = es.enter_context(nc.semaphore("ccsem"))
        psb = [es.enter_context(nc.psum_tensor("psb%d" % i, [128, 512], F32)) for i in range(1 if "f" in _sw0 else 8)]

        def PSF(i, rows=128, c0=0, c1=512):
            return psb[i][0:rows, c0:c1]

        def PSB16(i, rows=128):
            return psb[i][0:rows, :].bitcast(BF16)

        Win = BIG[:, 0:24576].rearrange("p (k n) -> p k n", k=8)
        EB = BIG[:, 0:8192].rearrange("p (h v q) -> p h v q", h=8, v=16)
        WoutNA = BIG[0:64, 8192:16384].rearrange("p (h n) -> p h n", h=8)
        WoutR = BIG[:, 16384:20480].rearrange("p (k n) -> p k n", k=4)
        gstage = BIG[:, 20480:24576].bitcast(F32)
        KV = BIG[:, 24576:40960].rearrange("p (s n) -> p s n", s=32)
        NAq = BIG[0:64, 24576:28672].rearrange("p (h n) -> p h n", h=8)
        NAk = BIG[0:64, 28672:36864].rearrange("p (h n) -> p h n", h=8)
        NAv = WB[:, 7168:12288].rearrange("p (t n) -> p t n", t=8)
        State = BIG[:, 40960:57344].rearrange("p (s n) -> p s n", s=32)
        klo = BIG[0:64, 57344:61440]
        khi = BIG[0:64, 61440:65536]
        Wup = BIG[:, 0:32768].rearrange("p (k n) -> p k n", k=8)
        Wdn = BIG[:, 32768:65536].rearrange("p (f n) -> p f n", f=32)

        dma_ct = [0]

        def DMA(q, out, in_, reads, writes, semkey):
            P.op(q, lambda e, o=out, i=in_: e.dma_start(out=o, in_=i), reads=reads, writes=writes, dma=True, semkey=semkey)

        import os as _os
        _sw = _os.environ.get("KN_SW", "")
        if "a" not in _sw:
            DMA("sp", ident[:], id_d, [], ["ident"], "c0")
            DMA("sp", ABt[:], ab_d, [], ["ABt"], "c1")
        if "b" not in _sw:
            DMA("sp", C4[:], c4_d, [], ["C4"], "c2")
            DMA("sp", cmask[:], cm_d, [], ["cmask"], "c3")
            DMA("sp", rmt[:], rm_d, [], ["rmt"], "c4")
        if "c" not in _sw:
            P.op("pool", lambda e: e.memset(onesf[:], 1.0), writes=["onesf"])
            P.op("pool", lambda e: e.memset(zerosb[:], 0.0), writes=["zerosb"])
        if "d" not in _sw:
            P.barrier()

        for l in range(depth):
            if stop <= 0:
                break
            xsrc = x_in if l == 0 else xs
            for k in range(8):
                DMA("pool", Win[:, k, :], w_in[l, k * 128:(k + 1) * 128, :], [], [("win", k)], ("win", k))
            DMA("sp", gain[:], nmix[l, :].partition_broadcast(128), [], ["gain"], "gain")
            DMA("sp", lg[:], dec_d[l, :].partition_broadcast(128), [], ["lg"], "lg")
            DMA("sp", lgp[0:64, :], dec_d[l, 0:4].partition_broadcast(64), [], ["lgp0"], "lgp0")
            DMA("sp", lgp[64:128, :], dec_d[l, 4:8].partition_broadcast(64), [], ["lgp1"], "lgp1")
            P.op("act", lambda e: e.activation(out=lg[:], in_=lg[:], func=AF.Exp), reads=["lg"], writes=["lg"])
            P.op("act", lambda e: e.activation(out=lg[:], in_=lg[:], func=AF.Ln, scale=-1.0, bias=1.0), reads=["lg"], writes=["lg"])
            P.op("act", lambda e: e.activation(out=lgp[:], in_=lgp[:], func=AF.Exp), reads=["lgp0", "lgp1"], writes=["lgp"])
            P.op("act", lambda e: e.activation(out=lgp[:], in_=lgp[:], func=AF.Ln, scale=-1.0, bias=1.0), reads=["lgp"], writes=["lgp"])
            P.op("act", lambda e: e.activation(out=small[:, 0:4], in_=lgp[:], func=AF.Exp, scale=128.0), reads=["lgp"], writes=["small"])
            P.op("dve", lambda e: e.tensor_copy(out=Dfull[:].rearrange("p (h n) -> p h n", h=4),
                                                in_=ap_of(small, 0, [small[:, 0:1].ap[0], [1, 4], [0, 128]])),
                 reads=["small"], writes=["Dfull"])
            for kind, (cc, d0) in enumerate([(0, 0), (1, 4), (2, 0), (3, 4)]):
                P.op("dve", lambda e, kind=kind, cc=cc, d0=d0: e.tensor_scalar(
                    out=TS[:, kind * 4:(kind + 1) * 4], in0=lg[:, d0:d0 + 4], scalar1=C4[:, cc:cc + 1], scalar2=None, op0=ALU.mult),
                    reads=["lg", "C4"], writes=[("TSr", kind)])
            P.op("act", lambda e: e.activation(out=TS[:], in_=TS[:], func=AF.Exp), reads=[("TSr", i) for i in range(4)], writes=["TS"])
            P.op("dve", lambda e: e.tensor_scalar(out=TS[:, 8:16], in0=TS[:, 8:16], scalar1=0.125, scalar2=None, op0=ALU.mult), reads=["TS"], writes=["TS"])
            for h in range(4):
                P.op("dve", lambda e, h=h: e.tensor_scalar(out=Mpp[:, h * 128:(h + 1) * 128], in0=ABt[:, 0:128], scalar1=lg[:, h:h + 1], scalar2=None, op0=ALU.mult),
                     reads=["lg", "ABt"], writes=[("Mpp", h)])
                P.op("dve", lambda e, h=h: e.scalar_tensor_tensor(out=Mpp[:, h * 128:(h + 1) * 128], in0=ABt[:, 128:256], scalar=lg[:, 4 + h:5 + h],
                                                                   in1=Mpp[:, h * 128:(h + 1) * 128], op0=ALU.mult, op1=ALU.add),
                     reads=["lg", "ABt", ("Mpp", h)], writes=[("Mpp", h)])
            P.op("act", lambda e: e.activation(out=Mpp[:], in_=Mpp[:], func=AF.Exp), reads=[("Mpp", h) for h in range(4)], writes=["MppF"])

            f_xin = [WF[:, 0:1024], WF[:, 1024:2048]]
            f_rqk = WF[:, 2048:2560]
            f_rot = WF[:, 2560:3072]
            f_t = [WF[:, 3072 + i * 256:3072 + (i + 1) * 256] for i in range(4)]
            f_sg = [WF[:, 4096:4608], WF[:, 4608:5120]]
            f_intra = [WF[:, 5120:5632], WF[:, 5632:6144]]
            f_cs = [WF[:, 6144:6208], WF[:, 6208:6272]]
            f_st = WF[:, 6272:6336]
            b_junk = WB[:, 0:1024]
            b_h = WB[:, 1024:2048]
            b_hT = WB[:, 2048:3072].rearrange("p (k n) -> p k n", k=8)
            b_qtok = WB[:, 3072:3584]
            b_ktok = WB[:, 3584:4096]
            b_qT = WB[0:64, 4096:5120].rearrange("p (h n) -> p h n", h=8)
            b_kT = WB[0:64, 5120:6144].rearrange("p (h n) -> p h n", h=8)
            b_vaug = [WB[:, 6144:6784], WB[:, 6784:7424]]
            b_rv = WB[:, 7424:7936]
            b_Qx = WB[:, 7936:8448]
            b_Kx = WB[:, 8448:8960]
            b_QTx = [WB[:, 8960:9472], WB[:, 9472:9984]]
            b_KT = WB[0:64, 9984:10496].rearrange("p (h n) -> p h n", h=4)
            b_SM = WB[:, 10496:11008]
            for s in range(2):
                P.op("pool", lambda e, s=s: e.memset(b_vaug[s], 1.0), writes=[("vaug", s)])
            for t in range(NT):
                s = t % 2
                tok = slice(t * 128, (t + 1) * 128)
                DMA("sp", f_xin[s], xsrc[tok, :], [], [("xin", s)], ("xin", s))
                DMA("sp", f_cs[s], cs_d[tok, :], [], [("cs", s)], ("cs", s))
                P.op("act", lambda e, s=s: e.activation(out=b_junk, in_=f_xin[s], func=AF.Square, accum_out=f_st[:, 0:1]),
                     reads=[("xin", s)], writes=["junk", "ssq"])
                P.op("dve", lambda e: e.tensor_scalar(out=f_st[:, 1:2], in0=f_st[:, 0:1], scalar1=1.0 / D, scalar2=EPS, op0=ALU.mult, op1=ALU.add),
                     reads=["ssq"], writes=["ms"])
                P.op("act", lambda e: e.activation(out=f_st[:, 3:4], in_=f_st[:, 1:2], func=AF.Sqrt), reads=["ms"], writes=["sqv"])
                P.op("dve", lambda e: e.reciprocal(out=f_st[:, 2:3], in_=f_st[:, 3:4]), reads=["sqv"], writes=["rstd"])
                P.op("dve", lambda e, s=s: e.scalar_tensor_tensor(out=b_h, in0=f_xin[s], scalar=f_st[:, 2:3], in1=gain[:], op0=ALU.mult, op1=ALU.mult),
                     reads=[("xin", s), "rstd", "gain"], writes=["h"])
                def tr_h(e):
                    r = None
                    for k in range(8):
                        r = e.transpose(PSB16(0)[:, k * 128:(k + 1) * 128], b_h[:, k * 128:(k + 1) * 128], ident[:])
                    return r
                P.op("pe", tr_h, reads=["h", "ident"], writes=["ps0"])
                P.op("act", lambda e: e.activation(out=b_hT.rearrange("p k n -> p (k n)"), in_=PSB16(0), func=AF.Copy), reads=["ps0"], writes=["hT"])
                def proj(c, bank):
                    def f(e):
                        r = None
                        for k in range(8):
                            r = e.matmul(PSF(bank), lhsT=b_hT[:, k, :], rhs=Win[:, k, c * 512:(c + 1) * 512], start=(k == 0), stop=(k == 7))
                        return r
                    return f
                winr = [("win", k) for k in range(8)]
                P.op("pe", proj(0, 1), reads=["hT"] + winr, writes=["ps1"])
                P.op("act", lambda e: e.activation(out=b_qtok, in_=PSF(1), func=AF.Copy), reads=["ps1"], writes=["qtok"])
                P.op("pe", proj(1, 2), reads=["hT"] + winr, writes=["ps2"])
                P.op("dve", lambda e: e.tensor_copy(out=b_ktok, in_=PSF(2)), reads=["ps2"], writes=["ktok"])
                P.op("pe", proj(2, 1), reads=["hT"] + winr, writes=["ps1"])
                P.op("act", lambda e, s=s: e.activation(out=b_vaug[s].rearrange("p (h n) -> p h n", h=8)[:, :, 0:64],
                                                        in_=PSF(1).rearrange("p (h n) -> p h n", h=8), func=AF.Copy),
                     reads=["ps1"], writes=[("vaug", s)])
                DMA("sp", vna[t], b_vaug[s], [("vaug", s)], [("vna", t)], ("vaug", s))
                def tr_na(src):
                    def f(e):
                        r = None
                        for h in range(8):
                            r = e.transpose(PSB16(3, 64)[:, h * 128:(h + 1) * 128], src[:, h * 64:(h + 1) * 64], ident[:])
                        return r
                    return f
                P.op("pe", tr_na(b_qtok), reads=["qtok", "ident"], writes=["ps3"])
                P.op("dve", lambda e: e.tensor_copy(out=b_qT.rearrange("p h n -> p (h n)"), in_=PSB16(3, 64)), reads=["ps3"], writes=["qT"])
                DMA("sp", qna[:, :, tok].rearrange("h d k -> d h k"), b_qT, ["qT"], [("qna", t)], "qT")
                P.op("pe", tr_na(b_ktok), reads=["ktok", "ident"], writes=["ps3"])
                P.op("act", lambda e: e.activation(out=b_kT.rearrange("p h n -> p (h n)"), in_=PSB16(3, 64), func=AF.Copy), reads=["ps3"], writes=["kT"])
                DMA("sp", kna[:, :, tok].rearrange("h d k -> d h k"), b_kT, ["kT"], [("kna", t)], "kT")
                P.op("pe", proj(3, 2), reads=["hT"] + winr, writes=["ps2"])
                P.op("act", lambda e: e.activation(out=f_rqk, in_=PSF(2), func=AF.Copy), reads=["ps2"], writes=["rqk"])
                P.op("pe", proj(4, 1), reads=["hT"] + winr, writes=["ps1"])
                P.op("dve", lambda e: e.tensor_copy(out=b_rv, in_=PSF(1)), reads=["ps1"], writes=["rv"])
                P.op("pe", proj(5, 2), reads=["hT"] + winr, writes=["ps2"])
                P.op("act", lambda e, s=s: e.activation(out=f_sg[s], in_=PSF(2), func=AF.Silu), reads=["ps2"], writes=[("sg", s)])
                DMA("sp", sg_d[t], f_sg[s], [("sg", s)], [("sgd", t)], ("sg", s))
                x4 = f_rqk.rearrange("p (g a c) -> p g a c", g=8, a=2)
                r4 = f_rot.rearrange("p (g a c) -> p g a c", g=8, a=2)
                x1, x2 = x4[:, :, 0, :], x4[:, :, 1, :]
                csap = f_cs[s]
                cosb = ap_of(csap.tensor, csap.offset, [csap.ap[0], [0, 8], [1, 32]])
                sinb = ap_of(csap.tensor, csap.offset + 32, [csap.ap[0], [0, 8], [1, 32]])
                tv = [f_t[i].rearrange("p (g c) -> p g c", g=8) for i in range(4)]
                rd = ["rqk", ("cs", s)]
                P.op("dve", lambda e, x1=x1, cosb=cosb: e.tensor_tensor(out=tv[0], in0=x1, in1=cosb, op=ALU.mult), reads=rd, writes=["t0"])
                P.op("dve", lambda e, x2=x2, sinb=sinb: e.tensor_tensor(out=tv[1], in0=x2, in1=sinb, op=ALU.mult), reads=rd, writes=["t1"])
                P.op("dve", lambda e, r4=r4: e.tensor_tensor(out=r4[:, :, 0, :], in0=tv[0], in1=tv[1], op=ALU.subtract), reads=["t0", "t1"], writes=["rot0"])
                P.op("pool", lambda e, x1=x1, sinb=sinb: e.tensor_tensor(out=tv[2], in0=x1, in1=sinb, op=ALU.mult), reads=rd, writes=["t2"])
                P.op("pool", lambda e, x2=x2, cosb=cosb: e.tensor_tensor(out=tv[3], in0=x2, in1=cosb, op=ALU.mult), reads=rd, writes=["t3"])
                P.op("pool", lambda e, r4=r4: e.tensor_tensor(out=r4[:, :, 1, :], in0=tv[2], in1=tv[3], op=ALU.add), reads=["t2", "t3"], writes=["rot1"])
                ro = f_rot
                qin = ap_of(ro.tensor, ro.offset, [ro.ap[0], [64, 4], [0, 2], [1, 64]])
                kin = ap_of(ro.tensor, ro.offset + 256, [ro.ap[0], [64, 4], [0, 2], [1, 64]])
                tsq = ap_of(TS, 0, [TS[:, 0:1].ap[0], [1, 4], [4, 2], [0, 64]])
                tsk = ap_of(TS, 8, [TS[:, 0:1].ap[0], [1, 4], [4, 2], [0, 64]])
                P.op("dve", lambda e, qin=qin: e.tensor_tensor(out=b_Qx.rearrange("p (h a c) -> p h a c", h=4, a=2), in0=qin, in1=tsq, op=ALU.mult),
                     reads=["rot0", "rot1", "TS"], writes=["Qx"])
                P.op("pool", lambda e, kin=kin: e.tensor_tensor(out=b_Kx.rearrange("p (h a c) -> p h a c", h=4, a=2), in0=kin, in1=tsk, op=ALU.mult),
                     reads=["rot0", "rot1", "TS"], writes=["Kx"])
                def tr_ret(e):
                    r = None
                    for h in range(4):
                        r = e.transpose(PSB16(4)[:, h * 128:(h + 1) * 128], b_Qx[:, h * 128:(h + 1) * 128], ident[:])
                    for h in range(4):
                        r = e.transpose(PSB16(3, 64)[:, h * 128:(h + 1) * 128], b_Kx[:, h * 128:h * 128 + 64], ident[:])
                    return r
                P.op("pe", tr_ret, reads=["Qx", "Kx", "ident"], writes=["ps4", "ps3"])
                P.op("act", lambda e, s=s: e.activation(out=b_QTx[s], in_=PSB16(4)[:, 0:512], func=AF.Copy), reads=["ps4"], writes=[("QTx", s)])
                P.op("dve", lambda e: e.tensor_copy(out=b_KT.rearrange("p h n -> p (h n)"), in_=PSB16(3, 64)[:, 0:512]), reads=["ps3"], writes=["KT"])
                DMA("sp", qtx[t], b_QTx[s], [("QTx", s)], [("qtx", t)], ("QTx", s))
                def st_mm(e, s=s):
                    r = None
                    for h in range(4):
                        r = e.matmul(PSF(5, 128, h * 128, (h + 1) * 128), lhsT=b_KT[:, h, :], rhs=b_QTx[s][0:64, h * 128:(h + 1) * 128], start=True, stop=True)
                    return r
                P.op("pe", st_mm, reads=["KT", ("QTx", s)], writes=["ps5"])
                P.op("dve", lambda e: e.tensor_tensor(out=b_SM, in0=PSF(5), in1=Mpp[:], op=ALU.mult), reads=["ps5", "MppF"], writes=["SM"])
                def in_mm(e):
                    r = None
                    for h in range(4):
                        r = e.matmul(PSF(6, 128, h * 128, (h + 1) * 128), lhsT=b_SM[:, h * 128:(h + 1) * 128], rhs=b_rv[:, h * 128:(h + 1) * 128], start=True, stop=True)
                    return r
                P.op("pe", in_mm, reads=["SM", "rv"], writes=["ps6"])
                P.op("act", lambda e, s=s: e.activation(out=f_intra[s], in_=PSF(6), func=AF.Copy), reads=["ps6"], writes=[("intra", s)])
                DMA("sp", intra_d[t], f_intra[s], [("intra", s)], [("intrad", t)], ("intra", s))
                def kv_mm(e):
                    r = None
                    for h in range(4):
                        r = e.matmul(PSF(7, 128, h * 128, (h + 1) * 128), lhsT=b_Kx[:, h * 128:(h + 1) * 128], rhs=b_rv[:, h * 128:(h + 1) * 128], start=True, stop=True)
                    return r
                P.op("pe", kv_mm, reads=["Kx", "rv"], writes=["ps7"])
                P.op("dve", lambda e, t=t: e.tensor_copy(out=KV[0:64, t, :], in_=PSF(7, 64)), reads=["ps7"], writes=[("KVf", t)])
                P.op("dve", lambda e, t=t: e.tensor_copy(out=KV[64:128, 31 - t, :], in_=psb[7][64:128, :]), reads=["ps7"], writes=[("KVb", 31 - t)])
            P.barrier()

            if stop <= 1:
                break
            ACC = WF[:, 0:512]
            G2 = WF[:, 512:1536]
            vlo = WB[:, 0:2560]
            vhi = WB[:, 2560:5120]
            P.op("pool", lambda e: e.memset(ACC, 0.0), writes=["ACC"])
            for s_ in range(32):
                P.op("dve", lambda e: e.tensor_tensor(out=ACC, in0=ACC, in1=Dfull[:], op=ALU.mult), reads=["ACC", "Dfull"], writes=["ACC"])
                P.op("dve", lambda e, s_=s_: e.tensor_tensor(out=ACC, in0=ACC, in1=KV[:, s_, :], op=ALU.add), reads=["ACC"], writes=["ACC"])
            DMA("sp", cst_in, ACC, ["ACC"], ["cst_in"], "cst")
            for i, kt in enumerate([0, 1, 30, 31]):
                DMA("sp", chal_in[(i // 2) * 64:(i // 2 + 1) * 64, (i % 2) * 1024:(i % 2 + 1) * 1024].rearrange("d (h k) -> d h k", h=8),
                    kna[:, :, kt * 128:(kt + 1) * 128].rearrange("h d k -> d h k"), [], [("chk", i)], ("chk", i))
                DMA("sp", chal_in[:, 2048 + i * 640:2048 + (i + 1) * 640], vna[kt], [], [("chv", i)], ("chv", i))

            def cc1(e):
                cc_count[0] += 1
                i = e.collective_compute("AllGather", ALU.bypass, replica_groups=RG, ins=[cst_in], outs=[cst_out])
                i.then_inc(ccsem)
                e.wait_ge(ccsem, cc_count[0])
                return e.memset(ccdummy[:], 0.0)

            def cc2(e):
                cc_count[0] += 1
                i = e.collective_compute("AllGather", ALU.bypass, replica_groups=RG, ins=[chal_in], outs=[chal_out])
                i.then_inc(ccsem)
                e.wait_ge(ccsem, cc_count[0])
                return e.memset(ccdummy[:], 0.0)
            P.op("pool", cc1, reads=["cst_in"], writes=["cst_out"])
            P.op("pool", cc2, reads=[("chk", i) for i in range(4)] + [("chv", i) for i in range(4)], writes=["chal_out"])
            DMA("sp", G2.rearrange("p (r n) -> p r n", r=2), cst_out.rearrange("(r p) n -> p r n", p=128), ["cst_out"], ["G2"], "G2")
            P.op("dve", lambda e: e.tensor_scalar(out=ACC, in0=G2[:, 0:512], scalar1=cmask[:, 0:1], scalar2=None, op0=ALU.mult), reads=["G2", "cmask"], writes=["ACC"])
            P.op("dve", lambda e: e.scalar_tensor_tensor(out=ACC, in0=G2[:, 512:1024], scalar=cmask[:, 1:2], in1=ACC, op0=ALU.mult, op1=ALU.add),
                 reads=["G2", "cmask", "ACC"], writes=["ACC"])
            for s_ in range(32):
                P.op("act", lambda e, s_=s_: e.activation(out=State[0:64, s_, :], in_=ACC[0:64, :], func=AF.Copy), reads=["ACC"], writes=[("StateF", s_)])
                P.op("act", lambda e, s_=s_: e.activation(out=State[64:128, 31 - s_, :], in_=ACC[64:128, :], func=AF.Copy), reads=["ACC"], writes=[("StateB", s_)])
                P.op("dve", lambda e: e.tensor_tensor(out=ACC, in0=ACC, in1=Dfull[:], op=ALU.mult), reads=["ACC", "Dfull"], writes=["ACC"])
                P.op("dve", lambda e, s_=s_: e.tensor_tensor(out=ACC, in0=ACC, in1=KV[:, s_, :], op=ALU.add), reads=["ACC"], writes=["ACC"])
            DMA("sp", klo[:, 0:2048], chal_out[0:64, 0:2048], ["chal_out"], ["klo0"], "klo0")
            DMA("sp", klo[:, 2048:4096], chal_out[64:128, 0:2048], ["chal_out"], ["klo1"], "klo1")
            DMA("sp", khi[:, 0:2048], chal_out[128:192, 0:2048], ["chal_out"], ["khi0"], "khi0")
            DMA("sp", khi[:, 2048:4096], chal_out[192:256, 0:2048], ["chal_out"], ["khi1"], "khi1")
            DMA("sp", vlo, chal_out[0:128, 2048:HW], ["chal_out"], ["vlo"], "vlo")
            DMA("sp", vhi, chal_out[128:256, 2048:HW], ["chal_out"], ["vhi"], "vhi")
            P.op("pool", lambda e: e.tensor_scalar(out=khalo[:], in0=klo, scalar1=cmask[0:64, 2:3], scalar2=None, op0=ALU.mult), reads=["klo0", "klo1", "cmask"], writes=["khalo"])
            P.op("dve", lambda e: e.scalar_tensor_tensor(out=khalo[:], in0=khi, scalar=cmask[0:64, 3:4], in1=khalo[:], op0=ALU.mult, op1=ALU.add),
                 reads=["khi0", "khi1", "cmask", "khalo"], writes=["khalo"])
            P.op("pool", lambda e: e.tensor_scalar(out=vhalo[:], in0=vlo, scalar1=cmask[:, 2:3], scalar2=None, op0=ALU.mult), reads=["vlo", "cmask"], writes=["vhalo"])
            P.op("dve", lambda e: e.scalar_tensor_tensor(out=vhalo[:], in0=vhi, scalar=cmask[:, 3:4], in1=vhalo[:], op0=ALU.mult, op1=ALU.add),
                 reads=["vhi", "cmask", "vhalo"], writes=["vhalo"])
            P.barrier()

            if stop <= 2:
                break
            maskT = WB[:, 0:1024]
            DMA("sp", maskT, mask_d, [], ["maskT"], "maskT")
            for h in range(8):
                DMA("sp", gstage[:, (h % 2) * 1024:(h % 2 + 1) * 1024], gtab_d[l, :, h * 1024:(h + 1) * 1024], [], [("gst", h % 2)], ("gst", h % 2))
                P.op("act", lambda e, h=h: e.activation(out=gstage[:, (h % 2) * 1024:(h % 2 + 1) * 1024], in_=gstage[:, (h % 2) * 1024:(h % 2 + 1) * 1024], func=AF.Exp),
                     reads=[("gst", h % 2)], writes=[("gst", h % 2)])
                P.op("dve", lambda e, h=h: e.tensor_tensor(out=EB[:, h, :, :].rearrange("p v q -> p (v q)"), in0=gstage[:, (h % 2) * 1024:(h % 2 + 1) * 1024], in1=maskT, op=ALU.mult),
                     reads=[("gst", h % 2), "maskT"], writes=[("EB", h)])
            b_e = [WB[:, 1024:1536], WB[:, 1536:2048]]
            b_p = [WB[:, 2048:2560], WB[:, 2560:3072]]
            b_nat = WB[0:64, 3072:7168].rearrange("p (h n) -> p h n", h=8)
            f_rc = WF[:, 0:512]
            f_bc = WF[0:64, 512:1024]
            kh4 = khalo[:].rearrange("p (i h k) -> p i h k", i=4, h=8)
            vh4 = vhalo[:].rearrange("p (i n) -> p i n", i=4)
            ucount = 0
            for g in range(8):
                w0, w1 = max(0, 4 * g - 2), min(31, 4 * g + 5)
                nw = w1 - w0 + 1
                gt = slice(g * 512, (g + 1) * 512)
                DMA("sp", NAq, qna[:, :, gt].rearrange("h d k -> d h k"), [], ["NAq"], "NAq")
                DMA("sp", NAk[:, :, 0:nw * 128], kna[:, :, w0 * 128:(w1 + 1) * 128].rearrange("h d k -> d h k"), [], ["NAk"], "NAk")
                DMA("sp", NAv[:, 0:nw, :], vna[w0:w1 + 1].rearrange("t p n -> p t n"), [], ["NAv"], "NAv")
                units = na_units(g)
                for h in range(8):
                    ob = 2 + (h % 2)
                    P.op("pe", lambda e, ob=ob: e.matmul(PSF(ob, 65), lhsT=zerosb[0:1, 0:65], rhs=zerosb[0:1, 0:512], start=True, stop=False),
                         reads=["zerosb"], writes=[("ps", ob)])
                    for ui, (kr, r0, r1, kind) in enumerate(units):
                        sbk = ucount % 2
                        ucount += 1
                        nq = r1 - r0 + 1
                        c0 = (r0 - 8 * g) * 64
                        c1 = c0 + nq * 64
                        if kr < 0:
                            kap = kh4[:, (kr + 4) // 2 + 2, h, :]
                            vap = vh4[:, (kr + 4) // 2 + 2, h * 80:h * 80 + 65]
                            kres, vres = "khalo", "vhalo"
                        elif kr >= 64:
                            kap = kh4[:, (kr - 64) // 2, h, :]
                            vap = vh4[:, (kr - 64) // 2, h * 80:h * 80 + 65]
                            kres, vres = "khalo", "vhalo"
                        else:
                            wi = kr // 2 - w0
                            kap = NAk[:, h, wi * 128:(wi + 1) * 128]
                            vap = NAv[:, wi, h * 80:h * 80 + 65]
                            kres, vres = "NAk", "NAv"
                        P.op("pe", lambda e, sbk=sbk, kap=kap, c0=c0, c1=c1, nq=nq, h=h: e.matmul(PSF(sbk, 128, 0, nq * 64), lhsT=kap, rhs=NAq[:, h, c0:c1], start=True, stop=True),
                             reads=[kres, "NAq"], writes=[("ps", sbk)])
                        P.op("act", lambda e, sbk=sbk, nq=nq: e.activation(out=b_e[sbk][:, 0:nq * 64], in_=PSF(sbk, 128, 0, nq * 64), func=AF.Exp, scale=0.125),
                             reads=[("ps", sbk)], writes=[("e", sbk)])
                        meng = "dve" if (ucount % 2 == 0) else "pool"
                        if kind == "int":
                            v0 = 3 - (kr - r0)
                            P.op(meng, lambda e, sbk=sbk, nq=nq, v0=v0, h=h: e.tensor_tensor(
                                out=b_p[sbk][:, 0:nq * 64], in0=b_e[sbk][:, 0:nq * 64],
                                in1=EB[:, h, v0:v0 + nq, :].rearrange("p v q -> p (v q)"), op=ALU.mult),
                                reads=[("e", sbk), ("EB", h)], writes=[("p", sbk)])
                        else:
                            for r in range(r0, r1 + 1):
                                vi = VIDX_BOTH[kr - r]
                                if r0 == 0:
                                    u = r * 6 + TOP_KR.index(kr)
                                else:
                                    u = 24 + (r - 61) * 6 + BOT_KR.index(kr)
                                j = r - r0
                                P.op("dve", lambda e, sbk=sbk, j=j, vi=vi, u=u, h=h: e.scalar_tensor_tensor(
                                    out=b_p[sbk][:, j * 64:(j + 1) * 64], in0=b_e[sbk][:, j * 64:(j + 1) * 64], scalar=rmt[:, u:u + 1],
                                    in1=EB[:, h, vi, :], op0=ALU.mult, op1=ALU.mult),
                                    reads=[("e", sbk), ("EB", h), "rmt"], writes=[("p", sbk)])
                        last = ui == len(units) - 1
                        P.op("pe", lambda e, ob=ob, vap=vap, sbk=sbk, c0=c0, c1=c1, nq=nq, last=last: e.matmul(
                            PSF(ob, 65, c0, c1), lhsT=vap, rhs=b_p[sbk][:, 0:nq * 64], start=False, stop=last),
                            reads=[vres, ("p", sbk)], writes=[("ps", ob)])
                    P.op("dve", lambda e, ob=ob: e.reciprocal(out=f_rc[64:65, :], in_=psb[ob][64:65, :]), reads=[("ps", ob)], writes=["rc"])
                    P.op("pe", lambda e: e.matmul(PSF(4, 64), lhsT=onesf[64:65, 0:64], rhs=f_rc[64:65, :], start=True, stop=True), reads=["rc", "onesf"], writes=[("ps", 4)])
                    P.op("act", lambda e: e.activation(out=f_bc, in_=PSF(4, 64), func=AF.Copy), reads=[("ps", 4)], writes=["bc"])
                    P.op("dve", lambda e, ob=ob, h=h: e.tensor_tensor(out=b_nat[:, h, :], in0=PSF(ob, 64), in1=f_bc, op=ALU.mult), reads=[("ps", ob), "bc"], writes=["natsb"])
                DMA("sp", nat[:, :, gt].rearrange("h d k -> d h k"), b_nat, ["natsb"], [("nat", g)], "natsb")
            P.barrier()

            if stop <= 3:
                break
            DMA("pool", WoutNA, w_out[l, 0:512, :].rearrange("(h d) n -> d h n", d=64), [], ["WoutNA"], "WoutNA")
            DMA("pool", WoutR, w_out[l, 512:1024, :].rearrange("(k p) n -> p k n", p=128), [], ["WoutR"], "WoutR")
            DMA("sp", gret[:], gret_d[l, :].partition_broadcast(128), [], ["gret"], "gret")
            f_x = [WF[:, 0:1024], WF[:, 1024:2048]]
            f_in = [WF[:, 2048:2560], WF[:, 2560:3072]]
            f_sgl = [WF[:, 3072:3584], WF[:, 3584:4096]]
            f_y = WF[:, 4096:4608]
            f_ysq = WF[:, 4608:5120]
            f_st2 = WF[:, 5120:5184]
            f_xo = [WB[:, 0:2048].bitcast(F32), WB[:, 2048:4096].bitcast(F32)]
            b_qx = [WB[:, 4096:4608], WB[:, 4608:5120]]
            b_na = [WB[0:64, 5120:6144].rearrange("p (h n) -> p h n", h=8), WB[0:64, 6144:7168].rearrange("p (h n) -> p h n", h=8)]
            b_mix = WB[:, 7168:7680]
            b_mixT = WB[:, 7680:8192].rearrange("p (k n) -> p k n", k=4)
            for t in range(NT):
                s = t % 2
                tok = slice(t * 128, (t + 1) * 128)
                DMA("sp", f_x[s], xsrc[tok, :], [], [("x2", s)], ("x2", s))
                DMA("sp", b_qx[s], qtx[t], [], [("qx2", s)], ("qx2", s))
                DMA("sp", f_in[s], intra_d[t], [], [("in2", s)], ("in2", s))
                DMA("sp", f_sgl[s], sg_d[t], [], [("sg2", s)], ("sg2", s))
                DMA("sp", b_na[s], nat[:, :, tok].rearrange("h d k -> d h k"), [], [("na2", s)], ("na2", s))

                def cross(e, s=s, t=t):
                    r = None
                    for h in range(4):
                        r = e.matmul(PSF(0, 128, h * 128, (h + 1) * 128), lhsT=b_qx[s][:, h * 128:(h + 1) * 128], rhs=State[:, t, h * 128:(h + 1) * 128], start=True, stop=True)
                    return r
                P.op("pe", cross, reads=[("qx2", s)], writes=[("ps", 0)])
                P.op("dve", lambda e, s=s: e.tensor_tensor(out=f_y, in0=PSF(0), in1=f_in[s], op=ALU.add), reads=[("ps", 0), ("in2", s)], writes=["y"])
                y3 = f_y.rearrange("p (h n) -> p h n", h=4)
                for h in range(4):
                    P.op("act", lambda e, h=h: e.activation(out=f_ysq[:, h * 128:(h + 1) * 128], in_=f_y[:, h * 128:(h + 1) * 128], func=AF.Copy, accum_out=f_st2[:, h:h + 1]),
                         reads=["y"], writes=[("ysqa", h), ("s1", h)])
                for h in range(4):
                    P.op("act", lambda e, h=h: e.activation(out=f_ysq[:, h * 128:(h + 1) * 128], in_=f_y[:, h * 128:(h + 1) * 128], func=AF.Square, accum_out=f_st2[:, 4 + h:5 + h]),
                         reads=["y", ("ysqa", h)], writes=[("ysqa", h), ("s2", h)])
                P.op("dve", lambda e: e.tensor_scalar(out=f_st2[:, 8:12], in0=f_st2[:, 0:4], scalar1=1.0 / 128, scalar2=None, op0=ALU.mult), reads=[("s1", h) for h in range(4)], writes=["mean"])
                P.op("dve", lambda e: e.tensor_tensor(out=f_st2[:, 12:16], in0=f_st2[:, 8:12], in1=f_st2[:, 8:12], op=ALU.mult), reads=["mean"], writes=["msq"])
                P.op("dve", lambda e: e.scalar_tensor_tensor(out=f_st2[:, 16:20], in0=f_st2[:, 4:8], scalar=1.0 / 128, in1=f_st2[:, 12:16], op0=ALU.mult, op1=ALU.subtract),
                     reads=[("s2", h) for h in range(4)] + ["msq"], writes=["var"])
                P.op("dve", lambda e: e.tensor_scalar(out=f_st2[:, 24:28], in0=f_st2[:, 16:20], scalar1=EPS, scalar2=None, op0=ALU.add), reads=["var"], writes=["vare"])
                P.op("act", lambda e: e.activation(out=f_st2[:, 28:32], in_=f_st2[:, 24:28], func=AF.Sqrt), reads=["vare"], writes=["sqv2"])
                P.op("dve", lambda e: e.reciprocal(out=f_st2[:, 20:24], in_=f_st2[:, 28:32]), reads=["sqv2"], writes=["rstd2"])
                st = f_st2
                meanb = ap_of(st.tensor, st.offset + 8, [st.ap[0], [1, 4], [0, 128]])
                rstdb = ap_of(st.tensor, st.offset + 20, [st.ap[0], [1, 4], [0, 128]])
                P.op("dve", lambda e, y3=y3, meanb=meanb: e.tensor_tensor(out=y3, in0=y3, in1=meanb, op=ALU.subtract), reads=["y", "mean"], writes=["y"])
                P.op("pool", lambda e, y3=y3, rstdb=rstdb: e.tensor_tensor(out=y3, in0=y3, in1=rstdb, op=ALU.mult), reads=["y", "rstd2"], writes=["y"])
                P.op("dve", lambda e: e.tensor_tensor(out=f_y, in0=f_y, in1=gret[:], op=ALU.mult), reads=["y", "gret"], writes=["y"])
                P.op("pool", lambda e, s=s: e.tensor_tensor(out=b_mix, in0=f_y, in1=f_sgl[s], op=ALU.mult), reads=["y", ("sg2", s)], writes=["mix"])

                def tr_mix(e):
                    r = None
                    for k in range(4):
                        r = e.transpose(PSB16(1)[:, k * 128:(k + 1) * 128], b_mix[:, k * 128:(k + 1) * 128], ident[:])
                    return r
                P.op("pe", tr_mix, reads=["mix", "ident"], writes=[("ps", 1)])
                P.op("act", lambda e: e.activation(out=b_mixT.rearrange("p k n -> p (k n)"), in_=PSB16(1)[:, 0:512], func=AF.Copy), reads=[("ps", 1)], writes=["mixT"])
                for c in range(2):
                    def oproj(e, c=c, s=s):
                        r = None
                        for h in range(8):
                            e.matmul(PSF(2 + c), lhsT=b_na[s][:, h, :], rhs=WoutNA[:, h, c * 512:(c + 1) * 512], start=(h == 0), stop=False)
                        for k in range(4):
                            r = e.matmul(PSF(2 + c), lhsT=b_mixT[:, k, :], rhs=WoutR[:, k, c * 512:(c + 1) * 512], start=False, stop=(k == 3))
                        return r
                    P.op("pe", oproj, reads=[("na2", s), "mixT", "WoutNA", "WoutR"], writes=[("ps", 2 + c)])
                    P.op("dve", lambda e, c=c, s=s: e.tensor_tensor(out=f_xo[s][:, c * 512:(c + 1) * 512], in0=PSF(2 + c), in1=f_x[s][:, c * 512:(c + 1) * 512], op=ALU.add),
                         reads=[("ps", 2 + c), ("x2", s)], writes=[("xo", s, c)])
                DMA("sp", xs[tok, :], f_xo[s], [("xo", s, 0), ("xo", s, 1)], [("xs", t)], ("xo", s))
            P.barrier()

            if stop <= 4:
                break
            for k in range(8):
                DMA("pool", Wup[:, k, :], w_up[l, k * 128:(k + 1) * 128, :], [], [("wup", k)], ("wup", k))
            for f8 in range(8):
                DMA("pool", Wdn[:, f8 * 4:(f8 + 1) * 4, :], w_dn[l, f8 * 512:(f8 + 1) * 512, :].rearrange("(f p) n -> p f n", p=128), [], [("wdn", f8)], ("wdn", f8))
            DMA("sp", gain[:], nmlp[l, :].partition_broadcast(128), [], ["gain"], "gain")
            lastl = l == depth - 1
            if lastl:
                DMA("sp", gret[:], nfin[0, 0:512].partition_broadcast(128), [], ["gf0"], "gf0")
                gfin2 = WF[:, 6144:6656]
                DMA("sp", gfin2, nfin[0, 512:1024].partition_broadcast(128), [], ["gf1"], "gf1")
            m_x = [WF[:, 0:2048], WF[:, 2048:4096]]
            m_r = [WF[:, 4096:4352], WF[:, 4352:4608]]
            m_sts = [WF[:, 4608:4672], WF[:, 4672:4736]]
            m_hs = [WB[:, 0:2048], WB[:, 6144:8192]]
            m_hTs = [WB[:, 2048:4096].rearrange("p (k n) -> p k n", k=8), WB[:, 8192:10240].rearrange("p (k n) -> p k n", k=8)]
            m_hid = [WB[:, 4096 + i * 256:4096 + (i + 1) * 256] for i in range(4)]
            m_junk = WB[:, 5120:6144]
            wupr = [("wup", k) for k in range(8)]

            def mlp_prologue(gi):
                s = gi % 2
                m_st, m_h, m_hT = m_sts[s], m_hs[s], m_hTs[s]
                tok2 = slice(gi * 256, (gi + 1) * 256)
                DMA("sp", m_x[s].rearrange("p (j n) -> p j n", j=2), xs[tok2, :].rearrange("(j p) n -> p j n", p=128), [], [("mx", s)], ("mx", s))
                for j in range(2):
                    P.op("act", lambda e, s=s, j=j, m_st=m_st: e.activation(out=m_junk, in_=m_x[s][:, j * 1024:(j + 1) * 1024], func=AF.Square, accum_out=m_st[:, j:j + 1]),
                         reads=[("mx", s)], writes=[("mssq", s, j)])
                    P.op("dve", lambda e, j=j, m_st=m_st: e.tensor_scalar(out=m_st[:, 2 + j:3 + j], in0=m_st[:, j:j + 1], scalar1=1.0 / D, scalar2=EPS, op0=ALU.mult, op1=ALU.add),
                         reads=[("mssq", s, j)], writes=[("mms", s, j)])
                    P.op("act", lambda e, j=j, m_st=m_st: e.activation(out=m_st[:, 6 + j:7 + j], in_=m_st[:, 2 + j:3 + j], func=AF.Sqrt), reads=[("mms", s, j)], writes=[("msq", s, j)])
                    P.op("dve", lambda e, j=j, m_st=m_st: e.reciprocal(out=m_st[:, 4 + j:5 + j], in_=m_st[:, 6 + j:7 + j]), reads=[("msq", s, j)], writes=[("mrstd", s, j)])
                    P.op("dve", lambda e, s=s, j=j, m_st=m_st, m_h=m_h: e.scalar_tensor_tensor(out=m_h[:, j * 1024:(j + 1) * 1024], in0=m_x[s][:, j * 1024:(j + 1) * 1024],
                                                                                   scalar=m_st[:, 4 + j:5 + j], in1=gain[:], op0=ALU.mult, op1=ALU.mult),
                         reads=[("mx", s), ("mrstd", s, j), "gain"], writes=[("mh", s, j)])

                    def tr_m(e, j=j, m_h=m_h):
                        r = None
                        for k in range(8):
                            r = e.transpose(PSB16(6 + j)[:, k * 128:(k + 1) * 128], m_h[:, j * 1024 + k * 128:j * 1024 + (k + 1) * 128], ident[:])
                        return r
                    P.op("pe", tr_m, reads=[("mh", s, j), "ident"], writes=[("ps", 6 + j)])
                    P.op("act", lambda e, j=j, m_hT=m_hT: e.activation(out=m_hT[:, :, j * 128:(j + 1) * 128], in_=PSB16(6 + j).rearrange("p (k n) -> p k n", k=8), func=AF.Copy),
                         reads=[("ps", 6 + j)], writes=[("mhT", s, j)])

            mlp_prologue(0)
            for gi in range(NT // 2):
                s = gi % 2
                m_st, m_hT = m_sts[s], m_hTs[s]
                tok2 = slice(gi * 256, (gi + 1) * 256)

                def emit_up(f, s=s, m_hT=m_hT):
                    ub = 4 + (f % 2)

                    def up(e, f=f, ub=ub):
                        r = None
                        for k in range(8):
                            r = e.matmul(PSF(ub, 128, 0, 256), lhsT=Wup[:, k, f * 128:(f + 1) * 128], rhs=m_hT[:, k, :], start=(k == 0), stop=(k == 7))
                        return r
                    P.op("pe", up, reads=[("mhT", s, 0), ("mhT", s, 1)] + wupr, writes=[("ps", ub)])
                emit_up(0)
                for f in range(32):
                    ub = 4 + (f % 2)
                    hb = f % 4
                    if f + 1 < 32:
                        emit_up(f + 1)
                    P.op("act", lambda e, ub=ub, f=f: e.activation(out=m_r[f % 2], in_=PSF(ub, 128, 0, 256), func=AF.Relu), reads=[("ps", ub)], writes=[("mr", f % 2)])
                    P.op("dve" if f % 2 == 0 else "pool", lambda e, f=f, hb=hb: e.tensor_tensor(out=m_hid[hb], in0=m_r[f % 2], in1=m_r[f % 2], op=ALU.mult),
                         reads=[("mr", f % 2)], writes=[("mhid", hb)])

                    def down(e, f=f, hb=hb):
                        r = None
                        for j in range(2):
                            for c in range(2):
                                r = e.matmul(PSF(j * 2 + c), lhsT=m_hid[hb][:, j * 128:(j + 1) * 128], rhs=Wdn[:, f, c * 512:(c + 1) * 512], start=(f == 0), stop=(f == 31))
                        return r
                    P.op("pe", down, reads=[("mhid", hb), ("wdn", f // 4)], writes=[("ps", 0), ("ps", 1), ("ps", 2), ("ps", 3)])
                    if f == 3 and gi + 1 < NT // 2:
                        mlp_prologue(gi + 1)
                for j in range(2):
                    for c in range(2):
                        P.op("dve", lambda e, s=s, j=j, c=c: e.tensor_tensor(out=m_x[s][:, j * 1024 + c * 512:j * 1024 + (c + 1) * 512], in0=PSF(j * 2 + c),
                                                                              in1=m_x[s][:, j * 1024 + c * 512:j * 1024 + (c + 1) * 512], op=ALU.add),
                             reads=[("ps", j * 2 + c), ("mx", s)], writes=[("mx", s)])
                if not lastl:
                    DMA("sp", xs[tok2, :].rearrange("(j p) n -> p j n", p=128), m_x[s].rearrange("p (j n) -> p j n", j=2), [("mx", s)], [("xsm", gi)], ("mx", s))
                else:
                    for j in range(2):
                        P.op("act", lambda e, s=s, j=j: e.activation(out=m_junk, in_=m_x[s][:, j * 1024:(j + 1) * 1024], func=AF.Square, accum_out=m_st[:, 8 + j:9 + j]),
                             reads=[("mx", s)], writes=[("fssq", j)])
                        P.op("dve", lambda e, j=j: e.tensor_scalar(out=m_st[:, 10 + j:11 + j], in0=m_st[:, 8 + j:9 + j], scalar1=1.0 / D, scalar2=EPS, op0=ALU.mult, op1=ALU.add),
                             reads=[("fssq", j)], writes=[("fms", j)])
                        P.op("act", lambda e, j=j: e.activation(out=m_st[:, 14 + j:15 + j], in_=m_st[:, 10 + j:11 + j], func=AF.Sqrt), reads=[("fms", j)], writes=[("fsq", j)])
                        P.op("dve", lambda e, j=j: e.reciprocal(out=m_st[:, 12 + j:13 + j], in_=m_st[:, 14 + j:15 + j]), reads=[("fsq", j)], writes=[("frstd", j)])
                        P.op("dve", lambda e, s=s, j=j: e.scalar_tensor_tensor(out=m_x[s][:, j * 1024:j * 1024 + 512], in0=m_x[s][:, j * 1024:j * 1024 + 512],
                                                                                scalar=m_st[:, 12 + j:13 + j], in1=gret[:], op0=ALU.mult, op1=ALU.mult),
                             reads=[("mx", s), ("frstd", j), "gf0"], writes=[("mx", s)])
                        P.op("dve", lambda e, s=s, j=j: e.scalar_tensor_tensor(out=m_x[s][:, j * 1024 + 512:(j + 1) * 1024], in0=m_x[s][:, j * 1024 + 512:(j + 1) * 1024],
                                                                                 scalar=m_st[:, 12 + j:13 + j], in1=gfin2, op0=ALU.mult, op1=ALU.mult),
                             reads=[("mx", s), ("frstd", j), "gf1"], writes=[("mx", s)])
                    DMA("sp", out_d[tok2, :].rearrange("(j p) n -> p j n", p=128), m_x[s].rearrange("p (j n) -> p j n", j=2), [("mx", s)], [("outd", gi)], ("mx", s))
            P.barrier()
        counts = P.emit(nc)
    return nc, counts


def _const_tables():
    qc = np.arange(64)
    kc = np.arange(64)
    cs_ = np.clip(qc - 8, 0, 48)
    colvalid = (kc[:, None] >= cs_[None, :]) & (kc[:, None] < cs_[None, :] + 16)
    dcidx = np.clip(kc[:, None] - qc[None, :], -15, 15) + 15
    mask = np.zeros((128, 16, 64), np.float32)
    dridx = np.zeros((128, 16), np.int64)
    for v in range(16):
        for kin in range(2):
            ok = (VAR_MODE[v] == 0) or (VAR_MODE[v] == 1 and kin == 0) or (VAR_MODE[v] == 2 and kin == 1)
            mask[kin * 64:(kin + 1) * 64, v, :] = colvalid * (1.0 if ok else 0.0)
            dridx[kin * 64:(kin + 1) * 64, v] = np.clip(VAR_DR0[v] + kin, -7, 7) + 7
    i = np.arange(128)
    jj, ii = np.meshgrid(i, i, indexing="ij")
    A = np.where(jj <= ii, -128.0, (jj - ii - 128.0)).astype(np.float32)
    B = np.maximum(jj - ii, 0).astype(np.float32)
    ab = np.concatenate([A, B], axis=1).astype(np.float32)
    c4 = np.stack([i + 1.0, 128.0 - i, 127.0 - i, i * 1.0], axis=1).astype(np.float32)
    return mask, dridx, dcidx, ab, c4


def _rm_table(hf):
    rm = np.zeros((128, 42), np.float32)
    def valid(r_loc, krow_loc):
        gr = r_loc + 64 * hf
        gk = krow_loc + 64 * hf
        rs = min(max(gr - 4, 0), 120)
        return 1.0 if (0 <= gk <= 127 and rs <= gk <= rs + 7) else 0.0
    for r in range(4):
        for ki, kr in enumerate(TOP_KR):
            for kin in range(2):
                rm[kin * 64:(kin + 1) * 64, r * 6 + ki] = valid(r, kr + kin)
    for r in range(61, 64):
        for ki, kr in enumerate(BOT_KR):
            for kin in range(2):
                rm[kin * 64:(kin + 1) * 64, 24 + (r - 61) * 6 + ki] = valid(r, kr + kin)
    return rm


_CACHE = {}


def kernel(x, w_in, w_out, na_rpb, ret_decay_fwd, ret_decay_bwd, ret_norm_gain,
           norm_mix, norm_mlp, w_up, w_down, norm_final, _depth=DEPTH, _stop=9, _debug=False, _lite=False):
    f32 = lambda a: np.ascontiguousarray(np.asarray(a, dtype=np.float32))
    x = f32(x)
    key = (_depth, _stop, _debug, _lite)
    if key not in _CACHE:
        _CACHE[key] = build(_depth, _stop, _debug, _lite)
    nc, _ = _CACHE[key]
    mask, dridx, dcidx, ab, c4 = _const_tables()
    rpb = f32(na_rpb)
    pk = np.arange(128) % 64
    gt = rpb[:, :, dridx[:, :, None], dcidx[pk][:, None, :]]
    gtab = np.ascontiguousarray(np.transpose(gt, (0, 2, 1, 3, 4))).reshape(DEPTH, 128, 8 * 16 * 64)
    maskt = mask.reshape(128, 1024).astype(ml_dtypes.bfloat16)
    decays = np.ascontiguousarray(np.concatenate([f32(ret_decay_fwd), f32(ret_decay_bwd)], axis=1))
    ident = np.eye(128, dtype=np.float32).astype(ml_dtypes.bfloat16)
    inv = (1.0 / (np.float32(10000.0) ** (np.arange(0, 64, 2, dtype=np.float32) / np.float32(64)))).astype(np.float32)
    shared = {
        "w_in": f32(w_in)[:1 if _lite else DEPTH], "w_out": f32(w_out)[:1 if _lite else DEPTH], "w_up": f32(w_up)[:1 if _lite else DEPTH], "w_down": f32(w_down)[:1 if _lite else DEPTH],
        "norm_mix": f32(norm_mix), "norm_mlp": f32(norm_mlp), "norm_final": f32(norm_final).reshape(1, D),
        "ret_norm_gain": f32(ret_norm_gain), "decays": decays, "gtab": gtab, "maskt": maskt,
        "abt": ab, "c4t": c4, "ident": ident,
    }
    in_maps = []
    for c in range(8):
        b, hf = c // 2, c % 2
        pos = (np.arange(TOK) + hf * TOK).astype(np.float32)
        ang = (pos[:, None] * inv[None, :]).astype(np.float32)
        cs = np.concatenate([np.cos(ang), np.sin(ang)], axis=1).astype(np.float32)
        cm = np.zeros((128, 4), np.float32)
        if hf == 0:
            cm[64:, 1] = 1.0
            cm[:, 3] = 1.0
        else:
            cm[:64, 0] = 1.0
            cm[:, 2] = 1.0
        m = dict(shared)
        m["x"] = np.ascontiguousarray(x[b, hf * TOK:(hf + 1) * TOK, :])
        m["cs"] = cs
        m["cmask"] = cm
        m["rmt"] = _rm_table(hf)
        in_maps.append(m)
    res = run_bass_kernel_spmd(nc, in_maps, core_ids=list(range(8)))
    if _debug:
        return res.results
    out = np.empty((4, 2 * TOK, D), np.float32)
    for c in range(8):
        out[c // 2, (c % 2) * TOK:(c % 2 + 1) * TOK, :] = np.asarray(res.results[c]["out"], dtype=np.float32)
    return out
```

```python
import contextlib
import numpy as np
import ml_dtypes
import concourse.bass as bass
import concourse.mybir as mybir
from concourse.bass_utils import run_bass_kernel_spmd

F32 = mybir.dt.float32
BF16 = mybir.dt.bfloat16
AF = mybir.ActivationFunctionType
ALU = mybir.AluOpType
AX = mybir.AxisListType

D = 1024
TOK = 4096
NT = 32
DEPTH = 4
EPS = 1e-6
VAR_DR0 = [3, 2, 1, 0, -1, -2, -3, -4, -5, 3, 4, 5, 6, -5, -6, -7]
VAR_MODE = [1, 0, 0, 0, 0, 0, 0, 0, 2, 0, 0, 0, 0, 0, 0, 0]
VIDX_BOTH = {2: 1, 1: 2, 0: 3, -1: 4, -2: 5, -3: 6, -4: 7, 3: 9, 4: 10, 5: 11, 6: 12, -5: 13, -6: 14, -7: 15}
DBG_OUT = ('xs', 'qna', 'kna', 'vna', 'qtx', 'intra', 'sgd', 'nat')
TOP_KR = [-4, -2, 0, 2, 4, 6]
BOT_KR = [56, 58, 60, 62, 64, 66]


class _Op:
    __slots__ = ("eng", "fn", "deps", "dma", "semkey", "signal", "val", "inc", "barrier")

    def __init__(self, eng, fn, dma, semkey, inc):
        self.eng = eng
        self.fn = fn
        self.deps = ()
        self.dma = dma
        self.semkey = semkey
        self.signal = dma
        self.val = 0
        self.inc = inc
        self.barrier = False


class Prog:
    ENGS = ("pe", "act", "dve", "pool", "sp")
    SAME_ENG_SYNC = ("act", "dve", "pool")

    def __init__(self):
        self.ops = []
        self.res = {}

    def op(self, eng, fn, reads=(), writes=(), dma=False, semkey=None):
        import os
        if len(self.ops) >= int(os.environ.get('KN_MAXOPS', '100000000')):
            return -1
        deps = set()
        for r in reads:
            st = self.res.get(r)
            if st is not None and st[0] is not None:
                deps.add(st[0])
        for w in writes:
            st = self.res.get(w)
            if st is not None:
                if st[0] is not None:
                    deps.add(st[0])
                deps.update(st[1].values())
                deps.update(st[2])
        idx = len(self.ops)
        o = _Op(eng, fn, dma, semkey, 16 if dma else 1)
        pr = set()
        for j in deps:
            oj = self.ops[j]
            if (not oj.dma) and (not dma) and oj.eng == eng and eng not in self.SAME_ENG_SYNC:
                continue
            pr.add(j)
        o.deps = pr
        self.ops.append(o)
        for r in reads:
            st = self.res.setdefault(r, [None, {}, []])
            if dma:
                st[2].append(idx)
            else:
                st[1][eng] = idx
        for w in writes:
            self.res[w] = [idx, {}, []]
        return idx

    def barrier(self):
        o = _Op("sp", None, False, None, 0)
        o.barrier = True
        self.ops.append(o)
        self.res = {}

    def emit(self, nc):
        ops = self.ops
        last = {}
        for o in ops:
            if o.barrier:
                for e, lo in last.items():
                    lo.signal = True
                continue
            for j in o.deps:
                ops[j].signal = True
            if not o.dma:
                last[o.eng] = o
        engcnt = {e: 0 for e in self.ENGS}
        dmacnt = {}
        for o in ops:
            if o.barrier:
                continue
            if o.dma:
                dmacnt[o.semkey] = dmacnt.get(o.semkey, 0) + o.inc
                o.val = dmacnt[o.semkey]
            elif o.signal:
                engcnt[o.eng] += 1
                o.val = engcnt[o.eng]
        with contextlib.ExitStack() as es:
            engsem = {e: es.enter_context(nc.semaphore("s_" + e)) for e in self.ENGS}
            dmasem = {}
            for k in dmacnt:
                dmasem[k] = es.enter_context(nc.semaphore("d_%d" % len(dmasem)))
            streams = {e: [] for e in self.ENGS}
            waited = {e: {} for e in self.ENGS}
            cur_eng = {e: 0 for e in self.ENGS}
            cur_dma = {}
            for o in ops:
                if o.barrier:
                    for e in self.ENGS:
                        wl = []
                        for e2 in self.ENGS:
                            if e2 != e and cur_eng[e2] > waited[e].get(("e", e2), 0):
                                waited[e][("e", e2)] = cur_eng[e2]
                                wl.append((engsem[e2], cur_eng[e2]))
                        for k, v in cur_dma.items():
                            if v > waited[e].get(("d", k), 0):
                                waited[e][("d", k)] = v
                                wl.append((dmasem[k], v))
                        if wl:
                            streams[e].append((wl, None, None, 0))
                    continue
                need = {}
                for j in o.deps:
                    oj = ops[j]
                    if oj.dma:
                        s = ("d", oj.semkey)
                        sem = dmasem[oj.semkey]
                    else:
                        s = ("e", oj.eng)
                        sem = engsem[oj.eng]
                    if oj.val > need.get(s, (None, 0))[1]:
                        need[s] = (sem, oj.val)
                wl = []
                for s, (sem, v) in need.items():
                    if waited[o.eng].get(s, 0) < v:
                        waited[o.eng][s] = v
                        wl.append((sem, v))
                if o.dma:
                    mysem, inc = dmasem[o.semkey], o.inc
                    cur_dma[o.semkey] = o.val
                elif o.signal:
                    mysem, inc = engsem[o.eng], 1
                    cur_eng[o.eng] = o.val
                else:
                    mysem, inc = None, 0
                streams[o.eng].append((wl, o.fn, mysem, inc))
            final = [(dmasem[k], v) for k, v in dmacnt.items()]
            final += [(engsem[e], engcnt[e]) for e in self.ENGS if engcnt[e] > 0]

            def run(stream, lastw=None):
                def f(eng):
                    for wl, fn, mysem, inc in stream:
                        for sem, v in wl:
                            eng.wait_ge(sem, v)
                        if fn is None:
                            continue
                        ins = fn(eng)
                        if mysem is not None:
                            ins.then_inc(mysem, inc)
                    if lastw:
                        for sem, v in lastw:
                            eng.wait_ge(sem, v)
                return f

            with nc.Block() as block:
                block.tensor(run(streams["pe"]))
                block.scalar(run(streams["act"]))
                block.vector(run(streams["dve"]))
                block.gpsimd(run(streams["pool"]))
                block.sync(run(streams["sp"], final))
        return {e: len(streams[e]) for e in self.ENGS}


def ap_of(t, offset, dims):
    return bass.AP(t, offset, [list(d) for d in dims])


def na_units(g):
    R0 = 8 * g
    ilo, ihi = R0, R0 + 7
    bnd = None
    if g == 0:
        ilo = 4
        bnd = (0, 3, TOP_KR)
    if g == 7:
        ihi = 60
        bnd = (61, 63, BOT_KR)
    units = []
    klo = ilo - 5
    klo += klo % 2
    khi = ihi + 3
    khi -= khi % 2
    for kr in range(klo, khi + 1, 2):
        r0 = max(ilo, kr - 3)
        r1 = min(ihi, kr + 5)
        if r0 <= r1:
            units.append((kr, r0, r1, "int"))
    if bnd is not None:
        for kr in bnd[2]:
            units.append((kr, bnd[0], bnd[1], "bnd"))
    return units


def build(depth=DEPTH, stop=9, debug=False, lite=False):
    nc = bass.Bass("TRN2", target_bir_lowering=False)
    dt_in = lambda name, shape, dt=F32: nc.dram_tensor(name, list(shape), dt, kind="ExternalInput").ap()
    x_in = dt_in("x", [TOK, D])
    LD = 1 if lite else DEPTH
    w_in = dt_in("w_in", [LD, D, 3072])
    w_out = dt_in("w_out", [LD, D, D])
    w_up = dt_in("w_up", [LD, D, 4096])
    w_dn = dt_in("w_down", [LD, 4096, D])
    nmix = dt_in("norm_mix", [DEPTH, D])
    nmlp = dt_in("norm_mlp", [DEPTH, D])
    nfin = dt_in("norm_final", [1, D])
    gret_d = dt_in("ret_norm_gain", [DEPTH, 512])
    dec_d = dt_in("decays", [DEPTH, 8])
    gtab_d = dt_in("gtab", [DEPTH, 128, 8 * 16 * 64])
    mask_d = dt_in("maskt", [128, 16 * 64], BF16)
    rm_d = dt_in("rmt", [128, 42])
    cs_d = dt_in("cs", [TOK, 64])
    ab_d = dt_in("abt", [128, 256])
    c4_d = dt_in("c4t", [128, 4])
    cm_d = dt_in("cmask", [128, 4])
    id_d = dt_in("ident", [128, 128], BF16)
    out_d = nc.dram_tensor("out", [TOK, D], F32, kind="ExternalOutput").ap()

    import os as _os0
    _sw0 = _os0.environ.get("KN_SW", "")
    def dscr(name, shape, dt):
        if "e" in _sw0:
            shape = [8, 8]
        return (nc.dram_tensor(name, list(shape), dt, kind="ExternalOutput").ap() if debug and name in DBG_OUT else nc.dram_tensor(name, list(shape), dt).ap())
    xs = dscr("xs", [TOK, D], F32)
    qna = dscr("qna", [8, 64, TOK], BF16)
    kna = dscr("kna", [8, 64, TOK], BF16)
    vna = dscr("vna", [NT, 128, 640], BF16)
    qtx = dscr("qtx", [NT, 128, 512], BF16)
    intra_d = dscr("intra", [NT, 128, 512], F32)
    sg_d = dscr("sgd", [NT, 128, 512], F32)
    nat = dscr("nat", [8, 64, TOK], BF16)
    cst_in = dscr("cst_in", [128, 512], F32)
    cst_out = dscr("cst_out", [256, 512], F32)
    HW = 2048 + 2560
    chal_in = dscr("chal_in", [128, HW], BF16)
    chal_out = dscr("chal_out", [256, HW], BF16)
    import os as _os3
    RG = [[2 * i, 2 * i + 1] for i in range(int(_os3.environ.get('KN_NCORES', '8')) // 2)]

    P = Prog()
    cc_count = [0]
    with contextlib.ExitStack() as es:
        sbt = lambda name, shape, dt: es.enter_context(nc.sbuf_tensor(name, list(shape), dt))
        import os as _os2
        BIG = sbt("BIG", [128, 65536 if not _os2.environ.get("KN_SMALL") else 32768], BF16)
        WF = sbt("WF", [128, 7168], F32)
        WB = sbt("WB", [128, 12288], BF16)
        ident = sbt("ident_s", [128, 128], BF16)
        onesf = sbt("onesf", [128, 64], F32)
        zerosb = sbt("zerosb", [128, 512], BF16)
        ABt = sbt("ABt", [128, 256], F32)
        C4 = sbt("C4", [128, 4], F32)
        cmask = sbt("cmask_s", [128, 4], F32)
        lg = sbt("lg", [128, 8], F32)
        lgp = sbt("lgp", [128, 4], F32)
        TS = sbt("TS", [128, 16], F32)
        Mpp = sbt("Mpp", [128, 512], F32)
        Dfull = sbt("Dfull", [128, 512], F32)
        rmt = sbt("rmt_s", [128, 42], F32)
        gain = sbt("gain", [128, 1024], F32)
        gret = sbt("gret", [128, 512], F32)
        khalo = sbt("khalo", [64, 4096], BF16)
        vhalo = sbt("vhalo", [128, 2560], BF16)
        small = sbt("small", [128, 64], F32)
        ccdummy = sbt("ccdummy", [128, 8], F32)
        ccsem = es.enter_context(nc.semaphore("ccsem"))
        psb = [es.enter_context(nc.psum_tensor("psb%d" % i, [128, 512], F32)) for i in range(1 if "f" in _sw0 else 8)]

        def PSF(i, rows=128, c0=0, c1=512):
            return psb[i][0:rows, c0:c1]

        def PSB16(i, rows=128):
            return psb[i][0:rows, :].bitcast(BF16)

        Win = BIG[:, 0:24576].rearrange("p (k n) -> p k n", k=8)
        EB = BIG[:, 0:8192].rearrange("p (h v q) -> p h v q", h=8, v=16)
        WoutNA = BIG[0:64, 8192:16384].rearrange("p (h n) -> p h n", h=8)
        WoutR = BIG[:, 16384:20480].rearrange("p (k n) -> p k n", k=4)
        gstage = BIG[:, 20480:24576].bitcast(F32)
        KV = BIG[:, 24576:40960].rearrange("p (s n) -> p s n", s=32)
        NAq = BIG[0:64, 24576:28672].rearrange("p (h n) -> p h n", h=8)
        NAk = BIG[0:64, 28672:36864].rearrange("p (h n) -> p h n", h=8)
        NAv = WB[:, 7168:12288].rearrange("p (t n) -> p t n", t=8)
        State = BIG[:, 40960:57344].rearrange("p (s n) -> p s n", s=32)
        klo = BIG[0:64, 57344:61440]
        khi = BIG[0:64, 61440:65536]
        Wup = BIG[:, 0:32768].rearrange("p (k n) -> p k n", k=8)
        Wdn = BIG[:, 32768:65536].rearrange("p (f n) -> p f n", f=32)

        dma_ct = [0]

        def DMA(q, out, in_, reads, writes, semkey):
            P.op(q, lambda e, o=out, i=in_: e.dma_start(out=o, in_=i), reads=reads, writes=writes, dma=True, semkey=semkey)

        import os as _os
        _sw = _os.environ.get("KN_SW", "")
        if "a" not in _sw:
            DMA("sp", ident[:], id_d, [], ["ident"], "c0")
            DMA("sp", ABt[:], ab_d, [], ["ABt"], "c1")
        if "b" not in _sw:
            DMA("sp", C4[:], c4_d, [], ["C4"], "c2")
            DMA("sp", cmask[:], cm_d, [], ["cmask"], "c3")
            DMA("sp", rmt[:], rm_d, [], ["rmt"], "c4")
        if "c" not in _sw:
            P.op("pool", lambda e: e.memset(onesf[:], 1.0), writes=["onesf"])
            P.op("pool", lambda e: e.memset(zerosb[:], 0.0), writes=["zerosb"])
        if "d" not in _sw:
            P.barrier()

        for l in range(depth):
            if stop <= 0:
                break
            xsrc = x_in if l == 0 else xs
            for k in range(8):
                DMA("pool", Win[:, k, :], w_in[l, k * 128:(k + 1) * 128, :], [], [("win", k)], ("win", k))
            DMA("sp", gain[:], nmix[l, :].partition_broadcast(128), [], ["gain"], "gain")
            DMA("sp", lg[:], dec_d[l, :].partition_broadcast(128), [], ["lg"], "lg")
            DMA("sp", lgp[0:64, :], dec_d[l, 0:4].partition_broadcast(64), [], ["lgp0"], "lgp0")
            DMA("sp", lgp[64:128, :], dec_d[l, 4:8].partition_broadcast(64), [], ["lgp1"], "lgp1")
            P.op("act", lambda e: e.activation(out=lg[:], in_=lg[:], func=AF.Exp), reads=["lg"], writes=["lg"])
            P.op("act", lambda e: e.activation(out=lg[:], in_=lg[:], func=AF.Ln, scale=-1.0, bias=1.0), reads=["lg"], writes=["lg"])
            P.op("act", lambda e: e.activation(out=lgp[:], in_=lgp[:], func=AF.Exp), reads=["lgp0", "lgp1"], writes=["lgp"])
            P.op("act", lambda e: e.activation(out=lgp[:], in_=lgp[:], func=AF.Ln, scale=-1.0, bias=1.0), reads=["lgp"], writes=["lgp"])
            P.op("act", lambda e: e.activation(out=small[:, 0:4], in_=lgp[:], func=AF.Exp, scale=128.0), reads=["lgp"], writes=["small"])
            P.op("dve", lambda e: e.tensor_copy(out=Dfull[:].rearrange("p (h n) -> p h n", h=4),
                                                in_=ap_of(small, 0, [small[:, 0:1].ap[0], [1, 4], [0, 128]])),
                 reads=["small"], writes=["Dfull"])
            for kind, (cc, d0) in enumerate([(0, 0), (1, 4), (2, 0), (3, 4)]):
                P.op("dve", lambda e, kind=kind, cc=cc, d0=d0: e.tensor_scalar(
                    out=TS[:, kind * 4:(kind + 1) * 4], in0=lg[:, d0:d0 + 4], scalar1=C4[:, cc:cc + 1], scalar2=None, op0=ALU.mult),
                    reads=["lg", "C4"], writes=[("TSr", kind)])
            P.op("act", lambda e: e.activation(out=TS[:], in_=TS[:], func=AF.Exp), reads=[("TSr", i) for i in range(4)], writes=["TS"])
            P.op("dve", lambda e: e.tensor_scalar(out=TS[:, 8:16], in0=TS[:, 8:16], scalar1=0.125, scalar2=None, op0=ALU.mult), reads=["TS"], writes=["TS"])
            for h in range(4):
                P.op("dve", lambda e, h=h: e.tensor_scalar(out=Mpp[:, h * 128:(h + 1) * 128], in0=ABt[:, 0:128], scalar1=lg[:, h:h + 1], scalar2=None, op0=ALU.mult),
                     reads=["lg", "ABt"], writes=[("Mpp", h)])
                P.op("dve", lambda e, h=h: e.scalar_tensor_tensor(out=Mpp[:, h * 128:(h + 1) * 128], in0=ABt[:, 128:256], scalar=lg[:, 4 + h:5 + h],
                                                                   in1=Mpp[:, h * 128:(h + 1) * 128], op0=ALU.mult, op1=ALU.add),
                     reads=["lg", "ABt", ("Mpp", h)], writes=[("Mpp", h)])
            P.op("act", lambda e: e.activation(out=Mpp[:], in_=Mpp[:], func=AF.Exp), reads=[("Mpp", h) for h in range(4)], writes=["MppF"])

            f_xin = [WF[:, 0:1024], WF[:, 1024:2048]]
            f_rqk = WF[:, 2048:2560]
            f_rot = WF[:, 2560:3072]
            f_t = [WF[:, 3072 + i * 256:3072 + (i + 1) * 256] for i in range(4)]
            f_sg = [WF[:, 4096:4608], WF[:, 4608:5120]]
            f_intra = [WF[:, 5120:5632], WF[:, 5632:6144]]
            f_cs = [WF[:, 6144:6208], WF[:, 6208:6272]]
            f_st = WF[:, 6272:6336]
            b_junk = WB[:, 0:1024]
            b_h = WB[:, 1024:2048]
            b_hT = WB[:, 2048:3072].rearrange("p (k n) -> p k n", k=8)
            b_qtok = WB[:, 3072:3584]
            b_ktok = WB[:, 3584:4096]
            b_qT = WB[0:64, 4096:5120].rearrange("p (h n) -> p h n", h=8)
            b_kT = WB[0:64, 5120:6144].rearrange("p (h n) -> p h n", h=8)
            b_vaug = [WB[:, 6144:6784], WB[:, 6784:7424]]
            b_rv = WB[:, 7424:7936]
            b_Qx = WB[:, 7936:8448]
            b_Kx = WB[:, 8448:8960]
            b_QTx = [WB[:, 8960:9472], WB[:, 9472:9984]]
            b_KT = WB[0:64, 9984:10496].rearrange("p (h n) -> p h n", h=4)
            b_SM = WB[:, 10496:11008]
            for s in range(2):
                P.op("pool", lambda e, s=s: e.memset(b_vaug[s], 1.0), writes=[("vaug", s)])
            for t in range(NT):
                s = t % 2
                tok = slice(t * 128, (t + 1) * 128)
                DMA("sp", f_xin[s], xsrc[tok, :], [], [("xin", s)], ("xin", s))
                DMA("sp", f_cs[s], cs_d[tok, :], [], [("cs", s)], ("cs", s))
                P.op("act", lambda e, s=s: e.activation(out=b_junk, in_=f_xin[s], func=AF.Square, accum_out=f_st[:, 0:1]),
                     reads=[("xin", s)], writes=["ssq"])
                P.op("dve", lambda e: e.tensor_scalar(out=f_st[:, 1:2], in0=f_st[:, 0:1], scalar1=1.0 / D, scalar2=EPS, op0=ALU.mult, op1=ALU.add),
                     reads=["ssq"], writes=["ms"])
                P.op("act", lambda e: e.activation(out=f_st[:, 3:4], in_=f_st[:, 1:2], func=AF.Sqrt), reads=["ms"], writes=["sqv"])
                P.op("dve", lambda e: e.reciprocal(out=f_st[:, 2:3], in_=f_st[:, 3:4]), reads=["sqv"], writes=["rstd"])
                P.op("dve", lambda e, s=s: e.scalar_tensor_tensor(out=b_h, in0=f_xin[s], scalar=f_st[:, 2:3], in1=gain[:], op0=ALU.mult, op1=ALU.mult),
                     reads=[("xin", s), "rstd", "gain"], writes=["h"])
                def tr_h(e):
                    r = None
                    for k in range(8):
                        r = e.transpose(PSB16(0)[:, k * 128:(k + 1) * 128], b_h[:, k * 128:(k + 1) * 128], ident[:])
                    return r
                P.op("pe", tr_h, reads=["h", "ident"], writes=["ps0"])
                P.op("act", lambda e: e.activation(out=b_hT.rearrange("p k n -> p (k n)"), in_=PSB16(0), func=AF.Copy), reads=["ps0"], writes=["hT"])
                def proj(c, bank):
                    def f(e):
                        r = None
                        for k in range(8):
                            r = e.matmul(PSF(bank), lhsT=b_hT[:, k, :], rhs=Win[:, k, c * 512:(c + 1) * 512], start=(k == 0), stop=(k == 7))
                        return r
                    return f
                winr = [("win", k) for k in range(8)]
                P.op("pe", proj(0, 1), reads=["hT"] + winr, writes=["ps1"])
                P.op("act", lambda e: e.activation(out=b_qtok, in_=PSF(1), func=AF.Copy), reads=["ps1"], writes=["qtok"])
                P.op("pe", proj(1, 2), reads=["hT"] + winr, writes=["ps2"])
                P.op("dve", lambda e: e.tensor_copy(out=b_ktok, in_=PSF(2)), reads=["ps2"], writes=["ktok"])
                P.op("pe", proj(2, 1), reads=["hT"] + winr, writes=["ps1"])
                P.op("act", lambda e, s=s: e.activation(out=b_vaug[s].rearrange("p (h n) -> p h n", h=8)[:, :, 0:64],
                                                        in_=PSF(1).rearrange("p (h n) -> p h n", h=8), func=AF.Copy),
                     reads=["ps1"], writes=[("vaug", s)])
                DMA("sp", vna[t], b_vaug[s], [("vaug", s)], [("vna", t)], ("vaug", s))
                def tr_na(src):
                    def f(e):
                        r = None
                        for h in range(8):
                            r = e.transpose(PSB16(3, 64)[:, h * 128:(h + 1) * 128], src[:, h * 64:(h + 1) * 64], ident[:])
                        return r
                    return f
                P.op("pe", tr_na(b_qtok), reads=["qtok", "ident"], writes=["ps3"])
                P.op("dve", lambda e: e.tensor_copy(out=b_qT.rearrange("p h n -> p (h n)"), in_=PSB16(3, 64)), reads=["ps3"], writes=["qT"])
                DMA("sp", qna[:, :, tok].rearrange("h d k -> d h k"), b_qT, ["qT"], [("qna", t)], "qT")
                P.op("pe", tr_na(b_ktok), reads=["ktok", "ident"], writes=["ps3"])
                P.op("act", lambda e: e.activation(out=b_kT.rearrange("p h n -> p (h n)"), in_=PSB16(3, 64), func=AF.Copy), reads=["ps3"], writes=["kT"])
                DMA("sp", kna[:, :, tok].rearrange("h d k -> d h k"), b_kT, ["kT"], [("kna", t)], "kT")
                P.op("pe", proj(3, 2), reads=["hT"] + winr, writes=["ps2"])
                P.op("act", lambda e: e.activation(out=f_rqk, in_=PSF(2), func=AF.Copy), reads=["ps2"], writes=["rqk"])
                P.op("pe", proj(4, 1), reads=["hT"] + winr, writes=["ps1"])
                P.op("dve", lambda e: e.tensor_copy(out=b_rv, in_=PSF(1)), reads=["ps1"], writes=["rv"])
                P.op("pe", proj(5, 2), reads=["hT"] + winr, writes=["ps2"])
                P.op("act", lambda e, s=s: e.activation(out=f_sg[s], in_=PSF(2), func=AF.Silu), reads=["ps2"], writes=[("sg", s)])
                DMA("sp", sg_d[t], f_sg[s], [("sg", s)], [("sgd", t)], ("sg", s))
                x4 = f_rqk.rearrange("p (g a c) -> p g a c", g=8, a=2)
                r4 = f_rot.rearrange("p (g a c) -> p g a c", g=8, a=2)
                x1, x2 = x4[:, :, 0, :], x4[:, :, 1, :]
                csap = f_cs[s]
                cosb = ap_of(csap.tensor, csap.offset, [csap.ap[0], [0, 8], [1, 32]])
                sinb = ap_of(csap.tensor, csap.offset + 32, [csap.ap[0], [0, 8], [1, 32]])
                tv = [f_t[i].rearrange("p (g c) -> p g c", g=8) for i in range(4)]
                rd = ["rqk", ("cs", s)]
                P.op("dve", lambda e, x1=x1, cosb=cosb: e.tensor_tensor(out=tv[0], in0=x1, in1=cosb, op=ALU.mult), reads=rd, writes=["t0"])
                P.op("dve", lambda e, x2=x2, sinb=sinb: e.tensor_tensor(out=tv[1], in0=x2, in1=sinb, op=ALU.mult), reads=rd, writes=["t1"])
                P.op("dve", lambda e, r4=r4: e.tensor_tensor(out=r4[:, :, 0, :], in0=tv[0], in1=tv[1], op=ALU.subtract), reads=["t0", "t1"], writes=["rot0"])
                P.op("pool", lambda e, x1=x1, sinb=sinb: e.tensor_tensor(out=tv[2], in0=x1, in1=sinb, op=ALU.mult), reads=rd, writes=["t2"])
                P.op("pool", lambda e, x2=x2, cosb=cosb: e.tensor_tensor(out=tv[3], in0=x2, in1=cosb, op=ALU.mult), reads=rd, writes=["t3"])
                P.op("pool", lambda e, r4=r4: e.tensor_tensor(out=r4[:, :, 1, :], in0=tv[2], in1=tv[3], op=ALU.add), reads=["t2", "t3"], writes=["rot1"])
                ro = f_rot
                qin = ap_of(ro.tensor, ro.offset, [ro.ap[0], [64, 4], [0, 2], [1, 64]])
                kin = ap_of(ro.tensor, ro.offset + 256, [ro.ap[0], [64, 4], [0, 2], [1, 64]])
                tsq = ap_of(TS, 0, [TS[:, 0:1].ap[0], [1, 4], [4, 2], [0, 64]])
                tsk = ap_of(TS, 8, [TS[:, 0:1].ap[0], [1, 4], [4, 2], [0, 64]])
                P.op("dve", lambda e, qin=qin: e.tensor_tensor(out=b_Qx.rearrange("p (h a c) -> p h a c", h=4, a=2), in0=qin, in1=tsq, op=ALU.mult),
                     reads=["rot0", "rot1", "TS"], writes=["Qx"])
                P.op("pool", lambda e, kin=kin: e.tensor_tensor(out=b_Kx.rearrange("p (h a c) -> p h a c", h=4, a=2), in0=kin, in1=tsk, op=ALU.mult),
                     reads=["rot0", "rot1", "TS"], writes=["Kx"])
                def tr_ret(e):
                    r = None
                    for h in range(4):
                        r = e.transpose(PSB16(4)[:, h * 128:(h + 1) * 128], b_Qx[:, h * 128:(h + 1) * 128], ident[:])
                    for h in range(4):
                        r = e.transpose(PSB16(3, 64)[:, h * 128:(h + 1) * 128], b_Kx[:, h * 128:h * 128 + 64], ident[:])
                    return r
                P.op("pe", tr_ret, reads=["Qx", "Kx", "ident"], writes=["ps4", "ps3"])
                P.op("act", lambda e, s=s: e.activation(out=b_QTx[s], in_=PSB16(4)[:, 0:512], func=AF.Copy), reads=["ps4"], writes=[("QTx", s)])
                P.op("dve", lambda e: e.tensor_copy(out=b_KT.rearrange("p h n -> p (h n)"), in_=PSB16(3, 64)[:, 0:512]), reads=["ps3"], writes=["KT"])
                DMA("sp", qtx[t], b_QTx[s], [("QTx", s)], [("qtx", t)], ("QTx", s))
                def st_mm(e, s=s):
                    r = None
                    for h in range(4):
                        r = e.matmul(PSF(5, 128, h * 128, (h + 1) * 128), lhsT=b_KT[:, h, :], rhs=b_QTx[s][0:64, h * 128:(h + 1) * 128], start=True, stop=True)
                    return r
                P.op("pe", st_mm, reads=["KT", ("QTx", s)], writes=["ps5"])
                P.op("dve", lambda e: e.tensor_tensor(out=b_SM, in0=PSF(5), in1=Mpp[:], op=ALU.mult), reads=["ps5", "MppF"], writes=["SM"])
                def in_mm(e):
                    r = None
                    for h in range(4):
                        r = e.matmul(PSF(6, 128, h * 128, (h + 1) * 128), lhsT=b_SM[:, h * 128:(h + 1) * 128], rhs=b_rv[:, h * 128:(h + 1) * 128], start=True, stop=True)
                    return r
                P.op("pe", in_mm, reads=["SM", "rv"], writes=["ps6"])
                P.op("act", lambda e, s=s: e.activation(out=f_intra[s], in_=PSF(6), func=AF.Copy), reads=["ps6"], writes=[("intra", s)])
                DMA("sp", intra_d[t], f_intra[s], [("intra", s)], [("intrad", t)], ("intra", s))
                def kv_mm(e):
                    r = None
                    for h in range(4):
                        r = e.matmul(PSF(7, 128, h * 128, (h + 1) * 128), lhsT=b_Kx[:, h * 128:(h + 1) * 128], rhs=b_rv[:, h * 128:(h + 1) * 128], start=True, stop=True)
                    return r
                P.op("pe", kv_mm, reads=["Kx", "rv"], writes=["ps7"])
                P.op("dve", lambda e, t=t: e.tensor_copy(out=KV[0:64, t, :], in_=PSF(7, 64)), reads=["ps7"], writes=[("KVf", t)])
                P.op("dve", lambda e, t=t: e.tensor_copy(out=KV[64:128, 31 - t, :], in_=psb[7][64:128, :]), reads=["ps7"], writes=[("KVb", 31 - t)])
            P.barrier()

            if stop <= 1:
                break
            ACC = WF[:, 0:512]
            G2 = WF[:, 512:1536]
            vlo = WB[:, 0:2560]
            vhi = WB[:, 2560:5120]
            P.op("pool", lambda e: e.memset(ACC, 0.0), writes=["ACC"])
            for s_ in range(32):
                P.op("dve", lambda e: e.tensor_tensor(out=ACC, in0=ACC, in1=Dfull[:], op=ALU.mult), reads=["ACC", "Dfull"], writes=["ACC"])
                P.op("dve", lambda e, s_=s_: e.tensor_tensor(out=ACC, in0=ACC, in1=KV[:, s_, :], op=ALU.add), reads=["ACC"], writes=["ACC"])
            DMA("sp", cst_in, ACC, ["ACC"], ["cst_in"], "cst")
            for i, kt in enumerate([0, 1, 30, 31]):
                DMA("sp", chal_in[(i // 2) * 64:(i // 2 + 1) * 64, (i % 2) * 1024:(i % 2 + 1) * 1024].rearrange("d (h k) -> d h k", h=8),
                    kna[:, :, kt * 128:(kt + 1) * 128].rearrange("h d k -> d h k"), [], [("chk", i)], ("chk", i))
                DMA("sp", chal_in[:, 2048 + i * 640:2048 + (i + 1) * 640], vna[kt], [], [("chv", i)], ("chv", i))

            def cc1(e):
                cc_count[0] += 1
                i = e.collective_compute("AllGather", ALU.bypass, replica_groups=RG, ins=[cst_in], outs=[cst_out])
                i.then_inc(ccsem)
                e.wait_ge(ccsem, cc_count[0])
                return e.memset(ccdummy[:], 0.0)

            def cc2(e):
                cc_count[0] += 1
                i = e.collective_compute("AllGather", ALU.bypass, replica_groups=RG, ins=[chal_in], outs=[chal_out])
                i.then_inc(ccsem)
                e.wait_ge(ccsem, cc_count[0])
                return e.memset(ccdummy[:], 0.0)
            P.op("pool", cc1, reads=["cst_in"], writes=["cst_out"])
            P.op("pool", cc2, reads=[("chk", i) for i in range(4)] + [("chv", i) for i in range(4)], writes=["chal_out"])
            DMA("sp", G2.rearrange("p (r n) -> p r n", r=2), cst_out.rearrange("(r p) n -> p r n", p=128), ["cst_out"], ["G2"], "G2")
            P.op("dve", lambda e: e.tensor_scalar(out=ACC, in0=G2[:, 0:512], scalar1=cmask[:, 0:1], scalar2=None, op0=ALU.mult), reads=["G2", "cmask"], writes=["ACC"])
            P.op("dve", lambda e: e.scalar_tensor_tensor(out=ACC, in0=G2[:, 512:1024], scalar=cmask[:, 1:2], in1=ACC, op0=ALU.mult, op1=ALU.add),
                 reads=["G2", "cmask", "ACC"], writes=["ACC"])
            for s_ in range(32):
                P.op("act", lambda e, s_=s_: e.activation(out=State[0:64, s_, :], in_=ACC[0:64, :], func=AF.Copy), reads=["ACC"], writes=[("StateF", s_)])
                P.op("act", lambda e, s_=s_: e.activation(out=State[64:128, 31 - s_, :], in_=ACC[64:128, :], func=AF.Copy), reads=["ACC"], writes=[("StateB", s_)])
                P.op("dve", lambda e: e.tensor_tensor(out=ACC, in0=ACC, in1=Dfull[:], op=ALU.mult), reads=["ACC", "Dfull"], writes=["ACC"])
                P.op("dve", lambda e, s_=s_: e.tensor_tensor(out=ACC, in0=ACC, in1=KV[:, s_, :], op=ALU.add), reads=["ACC"], writes=["ACC"])
            DMA("sp", klo[:, 0:2048], chal_out[0:64, 0:2048], ["chal_out"], ["klo0"], "klo0")
            DMA("sp", klo[:, 2048:4096], chal_out[64:128, 0:2048], ["chal_out"], ["klo1"], "klo1")
            DMA("sp", khi[:, 0:2048], chal_out[128:192, 0:2048], ["chal_out"], ["khi0"], "khi0")
            DMA("sp", khi[:, 2048:4096], chal_out[192:256, 0:2048], ["chal_out"], ["khi1"], "khi1")
            DMA("sp", vlo, chal_out[0:128, 2048:HW], ["chal_out"], ["vlo"], "vlo")
            DMA("sp", vhi, chal_out[128:256, 2048:HW], ["chal_out"], ["vhi"], "vhi")
            P.op("pool", lambda e: e.tensor_scalar(out=khalo[:], in0=klo, scalar1=cmask[0:64, 2:3], scalar2=None, op0=ALU.mult), reads=["klo0", "klo1", "cmask"], writes=["khalo"])
            P.op("dve", lambda e: e.scalar_tensor_tensor(out=khalo[:], in0=khi, scalar=cmask[0:64, 3:4], in1=khalo[:], op0=ALU.mult, op1=ALU.add),
                 reads=["khi0", "khi1", "cmask", "khalo"], writes=["khalo"])
            P.op("pool", lambda e: e.tensor_scalar(out=vhalo[:], in0=vlo, scalar1=cmask[:, 2:3], scalar2=None, op0=ALU.mult), reads=["vlo", "cmask"], writes=["vhalo"])
            P.op("dve", lambda e: e.scalar_tensor_tensor(out=vhalo[:], in0=vhi, scalar=cmask[:, 3:4], in1=vhalo[:], op0=ALU.mult, op1=ALU.add),
                 reads=["vhi", "cmask", "vhalo"], writes=["vhalo"])
            P.barrier()

            if stop <= 2:
                break
            maskT = WB[:, 0:1024]
            DMA("sp", maskT, mask_d, [], ["maskT"], "maskT")
            for h in range(8):
                DMA("sp", gstage[:, (h % 2) * 1024:(h % 2 + 1) * 1024], gtab_d[l, :, h * 1024:(h + 1) * 1024], [], [("gst", h % 2)], ("gst", h % 2))
                P.op("act", lambda e, h=h: e.activation(out=gstage[:, (h % 2) * 1024:(h % 2 + 1) * 1024], in_=gstage[:, (h % 2) * 1024:(h % 2 + 1) * 1024], func=AF.Exp),
                     reads=[("gst", h % 2)], writes=[("gst", h % 2)])
                P.op("dve", lambda e, h=h: e.tensor_tensor(out=EB[:, h, :, :].rearrange("p v q -> p (v q)"), in0=gstage[:, (h % 2) * 1024:(h % 2 + 1) * 1024], in1=maskT, op=ALU.mult),
                     reads=[("gst", h % 2), "maskT"], writes=[("EB", h)])
            b_e = [WB[:, 1024:1536], WB[:, 1536:2048]]
            b_p = [WB[:, 2048:2560], WB[:, 2560:3072]]
            b_nat = WB[0:64, 3072:7168].rearrange("p (h n) -> p h n", h=8)
            f_rc = WF[:, 0:512]
            f_bc = WF[0:64, 512:1024]
            kh4 = khalo[:].rearrange("p (i h k) -> p i h k", i=4, h=8)
            vh4 = vhalo[:].rearrange("p (i n) -> p i n", i=4)
            ucount = 0
            for g in range(8):
                w0, w1 = max(0, 4 * g - 2), min(31, 4 * g + 5)
                nw = w1 - w0 + 1
                gt = slice(g * 512, (g + 1) * 512)
                DMA("sp", NAq, qna[:, :, gt].rearrange("h d k -> d h k"), [], ["NAq"], "NAq")
                DMA("sp", NAk[:, :, 0:nw * 128], kna[:, :, w0 * 128:(w1 + 1) * 128].rearrange("h d k -> d h k"), [], ["NAk"], "NAk")
                DMA("sp", NAv[:, 0:nw, :], vna[w0:w1 + 1].rearrange("t p n -> p t n"), [], ["NAv"], "NAv")
                units = na_units(g)
                for h in range(8):
                    ob = 2 + (h % 2)
                    P.op("pe", lambda e, ob=ob: e.matmul(PSF(ob, 65), lhsT=zerosb[0:1, 0:65], rhs=zerosb[0:1, 0:512], start=True, stop=False),
                         reads=["zerosb"], writes=[("ps", ob)])
                    for ui, (kr, r0, r1, kind) in enumerate(units):
                        sbk = ucount % 2
                        ucount += 1
                        nq = r1 - r0 + 1
                        c0 = (r0 - 8 * g) * 64
                        c1 = c0 + nq * 64
                        if kr < 0:
                            kap = kh4[:, (kr + 4) // 2 + 2, h, :]
                            vap = vh4[:, (kr + 4) // 2 + 2, h * 80:h * 80 + 65]
                            kres, vres = "khalo", "vhalo"
                        elif kr >= 64:
                            kap = kh4[:, (kr - 64) // 2, h, :]
                            vap = vh4[:, (kr - 64) // 2, h * 80:h * 80 + 65]
                            kres, vres = "khalo", "vhalo"
                        else:
                            wi = kr // 2 - w0
                            kap = NAk[:, h, wi * 128:(wi + 1) * 128]
                            vap = NAv[:, wi, h * 80:h * 80 + 65]
                            kres, vres = "NAk", "NAv"
                        P.op("pe", lambda e, sbk=sbk, kap=kap, c0=c0, c1=c1, nq=nq, h=h: e.matmul(PSF(sbk, 128, 0, nq * 64), lhsT=kap, rhs=NAq[:, h, c0:c1], start=True, stop=True),
                             reads=[kres, "NAq"], writes=[("ps", sbk)])
                        P.op("act", lambda e, sbk=sbk, nq=nq: e.activation(out=b_e[sbk][:, 0:nq * 64], in_=PSF(sbk, 128, 0, nq * 64), func=AF.Exp, scale=0.125),
                             reads=[("ps", sbk)], writes=[("e", sbk)])
                        meng = "dve" if (ucount % 2 == 0) else "pool"
                        if kind == "int":
                            v0 = 3 - (kr - r0)
                            P.op(meng, lambda e, sbk=sbk, nq=nq, v0=v0, h=h: e.tensor_tensor(
                                out=b_p[sbk][:, 0:nq * 64], in0=b_e[sbk][:, 0:nq * 64],
                                in1=EB[:, h, v0:v0 + nq, :].rearrange("p v q -> p (v q)"), op=ALU.mult),
                                reads=[("e", sbk), ("EB", h)], writes=[("p", sbk)])
                        else:
                            for r in range(r0, r1 + 1):
                                vi = VIDX_BOTH[kr - r]
                                if r0 == 0:
                                    u = r * 6 + TOP_KR.index(kr)
                                else:
                                    u = 24 + (r - 61) * 6 + BOT_KR.index(kr)
                                j = r - r0
                                P.op("dve", lambda e, sbk=sbk, j=j, vi=vi, u=u, h=h: e.scalar_tensor_tensor(
                                    out=b_p[sbk][:, j * 64:(j + 1) * 64], in0=b_e[sbk][:, j * 64:(j + 1) * 64], scalar=rmt[:, u:u + 1],
                                    in1=EB[:, h, vi, :], op0=ALU.mult, op1=ALU.mult),
                                    reads=[("e", sbk), ("EB", h), "rmt"], writes=[("p", sbk)])
                        last = ui == len(units) - 1
                        P.op("pe", lambda e, ob=ob, vap=vap, sbk=sbk, c0=c0, c1=c1, nq=nq, last=last: e.matmul(
                            PSF(ob, 65, c0, c1), lhsT=vap, rhs=b_p[sbk][:, 0:nq * 64], start=False, stop=last),
                            reads=[vres, ("p", sbk)], writes=[("ps", ob)])
                    P.op("dve", lambda e, ob=ob: e.reciprocal(out=f_rc[64:65, :], in_=psb[ob][64:65, :]), reads=[("ps", ob)], writes=["rc"])
                    P.op("pe", lambda e: e.matmul(PSF(4, 64), lhsT=onesf[64:65, 0:64], rhs=f_rc[64:65, :], start=True, stop=True), reads=["rc", "onesf"], writes=[("ps", 4)])
                    P.op("act", lambda e: e.activation(out=f_bc, in_=PSF(4, 64), func=AF.Copy), reads=[("ps", 4)], writes=["bc"])
                    P.op("dve", lambda e, ob=ob, h=h: e.tensor_tensor(out=b_nat[:, h, :], in0=PSF(ob, 64), in1=f_bc, op=ALU.mult), reads=[("ps", ob), "bc"], writes=["natsb"])
                DMA("sp", nat[:, :, gt].rearrange("h d k -> d h k"), b_nat, ["natsb"], [("nat", g)], "natsb")
            P.barrier()

            if stop <= 3:
                break
            DMA("pool", WoutNA, w_out[l, 0:512, :].rearrange("(h d) n -> d h n", d=64), [], ["WoutNA"], "WoutNA")
            DMA("pool", WoutR, w_out[l, 512:1024, :].rearrange("(k p) n -> p k n", p=128), [], ["WoutR"], "WoutR")
            DMA("sp", gret[:], gret_d[l, :].partition_broadcast(128), [], ["gret"], "gret")
            f_x = [WF[:, 0:1024], WF[:, 1024:2048]]
            f_in = [WF[:, 2048:2560], WF[:, 2560:3072]]
            f_sgl = [WF[:, 3072:3584], WF[:, 3584:4096]]
            f_ys = [WF[:, 4096:4608], WF[:, 5184:5696]]
            f_ysqs = [WF[:, 4608:5120], WF[:, 5696:6208]]
            f_st2s = [WF[:, 5120:5184], WF[:, 6208:6272]]
            f_xo = [WB[:, 0:2048].bitcast(F32), WB[:, 2048:4096].bitcast(F32)]
            b_qx = [WB[:, 4096:4608], WB[:, 4608:5120]]
            b_na = [WB[0:64, 5120:6144].rearrange("p (h n) -> p h n", h=8), WB[0:64, 6144:7168].rearrange("p (h n) -> p h n", h=8)]
            b_mixs = [WB[:, 7168:7680], WB[:, 8192:8704]]
            b_mixTs = [WB[:, 7680:8192].rearrange("p (k n) -> p k n", k=4), WB[:, 8704:9216].rearrange("p (k n) -> p k n", k=4)]

            def p2_tile(t):
                steps = []
                s = t % 2
                pb = 4 * s
                tok = slice(t * 128, (t + 1) * 128)
                f_y, f_ysq, f_st2, b_mix, b_mixT = f_ys[s], f_ysqs[s], f_st2s[s], b_mixs[s], b_mixTs[s]
                OP = lambda *a, **k: steps.append(lambda: P.op(*a, **k))
                DM = lambda *a: steps.append(lambda: DMA(*a))
                DM("sp", f_x[s], xsrc[tok, :], [], [("x2", s)], ("x2", s))
                DM("sp", b_qx[s], qtx[t], [], [("qx2", s)], ("qx2", s))
                DM("sp", f_in[s], intra_d[t], [], [("in2", s)], ("in2", s))
                DM("sp", f_sgl[s], sg_d[t], [], [("sg2", s)], ("sg2", s))
                DM("sp", b_na[s], nat[:, :, tok].rearrange("h d k -> d h k"), [], [("na2", s)], ("na2", s))

                def cross(e):
                    r = None
                    for h in range(4):
                        r = e.matmul(PSF(pb, 128, h * 128, (h + 1) * 128), lhsT=b_qx[s][:, h * 128:(h + 1) * 128], rhs=State[:, t, h * 128:(h + 1) * 128], start=True, stop=True)
                    return r
                OP("pe", cross, reads=[("qx2", s)], writes=[("ps", pb)])
                OP("dve", lambda e: e.tensor_tensor(out=f_y, in0=PSF(pb), in1=f_in[s], op=ALU.add), reads=[("ps", pb), ("in2", s)], writes=[("y", s)])
                y3 = f_y.rearrange("p (h n) -> p h n", h=4)
                for h in range(4):
                    OP("act", lambda e, h=h: e.activation(out=f_ysq[:, h * 128:(h + 1) * 128], in_=f_y[:, h * 128:(h + 1) * 128], func=AF.Copy, accum_out=f_st2[:, h:h + 1]),
                       reads=[("y", s)], writes=[("s1", s, h)])
                for h in range(4):
                    OP("act", lambda e, h=h: e.activation(out=f_ysq[:, h * 128:(h + 1) * 128], in_=f_y[:, h * 128:(h + 1) * 128], func=AF.Square, accum_out=f_st2[:, 4 + h:5 + h]),
                       reads=[("y", s)], writes=[("s2", s, h)])
                OP("dve", lambda e: e.tensor_scalar(out=f_st2[:, 8:12], in0=f_st2[:, 0:4], scalar1=1.0 / 128, scalar2=None, op0=ALU.mult), reads=[("s1", s, h) for h in range(4)], writes=[("mean", s)])
                OP("dve", lambda e: e.tensor_tensor(out=f_st2[:, 12:16], in0=f_st2[:, 8:12], in1=f_st2[:, 8:12], op=ALU.mult), reads=[("mean", s)], writes=[("msq", s)])
                OP("dve", lambda e: e.scalar_tensor_tensor(out=f_st2[:, 16:20], in0=f_st2[:, 4:8], scalar=1.0 / 128, in1=f_st2[:, 12:16], op0=ALU.mult, op1=ALU.subtract),
                   reads=[("s2", s, h) for h in range(4)] + [("msq", s)], writes=[("var", s)])
                OP("dve", lambda e: e.tensor_scalar(out=f_st2[:, 24:28], in0=f_st2[:, 16:20], scalar1=EPS, scalar2=None, op0=ALU.add), reads=[("var", s)], writes=[("vare", s)])
                OP("act", lambda e: e.activation(out=f_st2[:, 28:32], in_=f_st2[:, 24:28], func=AF.Sqrt), reads=[("vare", s)], writes=[("sqv2", s)])
                OP("dve", lambda e: e.reciprocal(out=f_st2[:, 20:24], in_=f_st2[:, 28:32]), reads=[("sqv2", s)], writes=[("rstd2", s)])
                st = f_st2
                meanb = ap_of(st.tensor, st.offset + 8, [st.ap[0], [1, 4], [0, 128]])
                rstdb = ap_of(st.tensor, st.offset + 20, [st.ap[0], [1, 4], [0, 128]])
                OP("dve", lambda e: e.tensor_tensor(out=y3, in0=y3, in1=meanb, op=ALU.subtract), reads=[("y", s), ("mean", s), ("s2", s, 3)], writes=[("y", s)])
                OP("pool", lambda e: e.tensor_tensor(out=y3, in0=y3, in1=rstdb, op=ALU.mult), reads=[("y", s), ("rstd2", s)], writes=[("y", s)])
                OP("dve", lambda e: e.tensor_tensor(out=f_y, in0=f_y, in1=gret[:], op=ALU.mult), reads=[("y", s), "gret"], writes=[("y", s)])
                OP("pool", lambda e: e.tensor_tensor(out=b_mix, in0=f_y, in1=f_sgl[s], op=ALU.mult), reads=[("y", s), ("sg2", s)], writes=[("mix", s)])

                def tr_mix(e):
                    r = None
                    for k in range(4):
                        r = e.transpose(PSB16(pb + 1)[:, k * 128:(k + 1) * 128], b_mix[:, k * 128:(k + 1) * 128], ident[:])
                    return r
                OP("pe", tr_mix, reads=[("mix", s), "ident"], writes=[("ps", pb + 1)])
                OP("act", lambda e: e.activation(out=b_mixT.rearrange("p k n -> p (k n)"), in_=PSB16(pb + 1)[:, 0:512], func=AF.Copy), reads=[("ps", pb + 1)], writes=[("mixT", s)])
                for c in range(2):
                    def oproj(e, c=c):
                        r = None
                        for h in range(8):
                            e.matmul(PSF(pb + 2 + c), lhsT=b_na[s][:, h, :], rhs=WoutNA[:, h, c * 512:(c + 1) * 512], start=(h == 0), stop=False)
                        for k in range(4):
                            r = e.matmul(PSF(pb + 2 + c), lhsT=b_mixT[:, k, :], rhs=WoutR[:, k, c * 512:(c + 1) * 512], start=False, stop=(k == 3))
                        return r
                    OP("pe", oproj, reads=[("na2", s), ("mixT", s), "WoutNA", "WoutR"], writes=[("ps", pb + 2 + c)])
                    OP("dve", lambda e, c=c: e.tensor_tensor(out=f_xo[s][:, c * 512:(c + 1) * 512], in0=PSF(pb + 2 + c), in1=f_x[s][:, c * 512:(c + 1) * 512], op=ALU.add),
                       reads=[("ps", pb + 2 + c), ("x2", s)], writes=[("xo", s, c)])
                DM("sp", xs[tok, :], f_xo[s], [("xo", s, 0), ("xo", s, 1)], [("xs", t)], ("xo", s))
                return steps

            for t0 in range(0, NT, 2):
                sa, sb_ = p2_tile(t0), p2_tile(t0 + 1)
                for i in range(max(len(sa), len(sb_))):
                    if i < len(sa):
                        sa[i]()
                    if i < len(sb_):
                        sb_[i]()
            P.barrier()

            if stop <= 4:
                break
            for k in range(8):
                DMA("pool", Wup[:, k, :], w_up[l, k * 128:(k + 1) * 128, :], [], [("wup", k)], ("wup", k))
            for f8 in range(8):
                DMA("pool", Wdn[:, f8 * 4:(f8 + 1) * 4, :], w_dn[l, f8 * 512:(f8 + 1) * 512, :].rearrange("(f p) n -> p f n", p=128), [], [("wdn", f8)], ("wdn", f8))
            DMA("sp", gain[:], nmlp[l, :].partition_broadcast(128), [], ["gain"], "gain")
            lastl = l == depth - 1
            if lastl:
                DMA("sp", gret[:], nfin[0, 0:512].partition_broadcast(128), [], ["gf0"], "gf0")
                gfin2 = WF[:, 6144:6656]
                DMA("sp", gfin2, nfin[0, 512:1024].partition_broadcast(128), [], ["gf1"], "gf1")
            m_x = [WF[:, 0:2048], WF[:, 2048:4096]]
            m_r = [WF[:, 4096:4352], WF[:, 4352:4608]]
            m_sts = [WF[:, 4608:4672], WF[:, 4672:4736]]
            m_hs = [WB[:, 0:2048], WB[:, 6144:8192]]
            m_hTs = [WB[:, 2048:4096].rearrange("p (k n) -> p k n", k=8), WB[:, 8192:10240].rearrange("p (k n) -> p k n", k=8)]
            m_hid = [WB[:, 4096 + i * 256:4096 + (i + 1) * 256] for i in range(4)]
            m_junk = WB[:, 5120:6144]
            wupr = [("wup", k) for k in range(8)]

            def mlp_prologue(gi):
                s = gi % 2
                m_st, m_h, m_hT = m_sts[s], m_hs[s], m_hTs[s]
                tok2 = slice(gi * 256, (gi + 1) * 256)
                DMA("sp", m_x[s].rearrange("p (j n) -> p j n", j=2), xs[tok2, :].rearrange("(j p) n -> p j n", p=128), [], [("mx", s)], ("mx", s))
                for j in range(2):
                    P.op("act", lambda e, s=s, j=j, m_st=m_st: e.activation(out=m_junk, in_=m_x[s][:, j * 1024:(j + 1) * 1024], func=AF.Square, accum_out=m_st[:, j:j + 1]),
                         reads=[("mx", s)], writes=[("mssq", s, j)])
                    P.op("dve", lambda e, j=j, m_st=m_st: e.tensor_scalar(out=m_st[:, 2 + j:3 + j], in0=m_st[:, j:j + 1], scalar1=1.0 / D, scalar2=EPS, op0=ALU.mult, op1=ALU.add),
                         reads=[("mssq", s, j)], writes=[("mms", s, j)])
                    P.op("act", lambda e, j=j, m_st=m_st: e.activation(out=m_st[:, 6 + j:7 + j], in_=m_st[:, 2 + j:3 + j], func=AF.Sqrt), reads=[("mms", s, j)], writes=[("msq", s, j)])
                    P.op("dve", lambda e, j=j, m_st=m_st: e.reciprocal(out=m_st[:, 4 + j:5 + j], in_=m_st[:, 6 + j:7 + j]), reads=[("msq", s, j)], writes=[("mrstd", s, j)])
                    P.op("dve", lambda e, s=s, j=j, m_st=m_st, m_h=m_h: e.scalar_tensor_tensor(out=m_h[:, j * 1024:(j + 1) * 1024], in0=m_x[s][:, j * 1024:(j + 1) * 1024],
                                                                                   scalar=m_st[:, 4 + j:5 + j], in1=gain[:], op0=ALU.mult, op1=ALU.mult),
                         reads=[("mx", s), ("mrstd", s, j), "gain"], writes=[("mh", s, j)])

                    def tr_m(e, j=j, m_h=m_h):
                        r = None
                        for k in range(8):
                            r = e.transpose(PSB16(6 + j)[:, k * 128:(k + 1) * 128], m_h[:, j * 1024 + k * 128:j * 1024 + (k + 1) * 128], ident[:])
                        return r
                    P.op("pe", tr_m, reads=[("mh", s, j), "ident"], writes=[("ps", 6 + j)])
                    P.op("act", lambda e, j=j, m_hT=m_hT: e.activation(out=m_hT[:, :, j * 128:(j + 1) * 128], in_=PSB16(6 + j).rearrange("p (k n) -> p k n", k=8), func=AF.Copy),
                         reads=[("ps", 6 + j)], writes=[("mhT", s, j)])

            mlp_prologue(0)
            for gi in range(NT // 2):
                s = gi % 2
                m_st, m_hT = m_sts[s], m_hTs[s]
                tok2 = slice(gi * 256, (gi + 1) * 256)

                def emit_up(f, s=s, m_hT=m_hT):
                    ub = 4 + (f % 2)

                    def up(e, f=f, ub=ub):
                        r = None
                        for k in range(8):
                            r = e.matmul(PSF(ub, 128, 0, 256), lhsT=Wup[:, k, f * 128:(f + 1) * 128], rhs=m_hT[:, k, :], start=(k == 0), stop=(k == 7))
                        return r
                    P.op("pe", up, reads=[("mhT", s, 0), ("mhT", s, 1)] + wupr, writes=[("ps", ub)])
                emit_up(0)
                for f in range(32):
                    ub = 4 + (f % 2)
                    hb = f % 4
                    if f + 1 < 32:
                        emit_up(f + 1)
                    P.op("act", lambda e, ub=ub, f=f: e.activation(out=m_r[f % 2], in_=PSF(ub, 128, 0, 256), func=AF.Relu), reads=[("ps", ub)], writes=[("mr", f % 2)])
                    P.op("dve" if f % 2 == 0 else "pool", lambda e, f=f, hb=hb: e.tensor_tensor(out=m_hid[hb], in0=m_r[f % 2], in1=m_r[f % 2], op=ALU.mult),
                         reads=[("mr", f % 2)], writes=[("mhid", hb)])

                    def down(e, f=f, hb=hb):
                        r = None
                        for j in range(2):
                            for c in range(2):
                                r = e.matmul(PSF(j * 2 + c), lhsT=m_hid[hb][:, j * 128:(j + 1) * 128], rhs=Wdn[:, f, c * 512:(c + 1) * 512], start=(f == 0), stop=(f == 31))
                        return r
                    P.op("pe", down, reads=[("mhid", hb), ("wdn", f // 4)], writes=[("ps", 0), ("ps", 1), ("ps", 2), ("ps", 3)])
                    if f == 3 and gi + 1 < NT // 2:
                        mlp_prologue(gi + 1)
                for j in range(2):
                    for c in range(2):
                        P.op("dve", lambda e, s=s, j=j, c=c: e.tensor_tensor(out=m_x[s][:, j * 1024 + c * 512:j * 1024 + (c + 1) * 512], in0=PSF(j * 2 + c),
                                                                              in1=m_x[s][:, j * 1024 + c * 512:j * 1024 + (c + 1) * 512], op=ALU.add),
                             reads=[("ps", j * 2 + c), ("mx", s)], writes=[("mx", s)])
                if not lastl:
                    DMA("sp", xs[tok2, :].rearrange("(j p) n -> p j n", p=128), m_x[s].rearrange("p (j n) -> p j n", j=2), [("mx", s)], [("xsm", gi)], ("mx", s))
                else:
                    for j in range(2):
                        P.op("act", lambda e, s=s, j=j: e.activation(out=m_junk, in_=m_x[s][:, j * 1024:(j + 1) * 1024], func=AF.Square, accum_out=m_st[:, 8 + j:9 + j]),
                             reads=[("mx", s)], writes=[("fssq", j)])
                        P.op("dve", lambda e, j=j: e.tensor_scalar(out=m_st[:, 10 + j:11 + j], in0=m_st[:, 8 + j:9 + j], scalar1=1.0 / D, scalar2=EPS, op0=ALU.mult, op1=ALU.add),
                             reads=[("fssq", j)], writes=[("fms", j)])
                        P.op("act", lambda e, j=j: e.activation(out=m_st[:, 14 + j:15 + j], in_=m_st[:, 10 + j:11 + j], func=AF.Sqrt), reads=[("fms", j)], writes=[("fsq", j)])
                        P.op("dve", lambda e, j=j: e.reciprocal(out=m_st[:, 12 + j:13 + j], in_=m_st[:, 14 + j:15 + j]), reads=[("fsq", j)], writes=[("frstd", j)])
                        P.op("dve", lambda e, s=s, j=j: e.scalar_tensor_tensor(out=m_x[s][:, j * 1024:j * 1024 + 512], in0=m_x[s][:, j * 1024:j * 1024 + 512],
                                                                                scalar=m_st[:, 12 + j:13 + j], in1=gret[:], op0=ALU.mult, op1=ALU.mult),
                             reads=[("mx", s), ("frstd", j), "gf0"], writes=[("mx", s)])
                        P.op("dve", lambda e, s=s, j=j: e.scalar_tensor_tensor(out=m_x[s][:, j * 1024 + 512:(j + 1) * 1024], in0=m_x[s][:, j * 1024 + 512:(j + 1) * 1024],
                                                                                 scalar=m_st[:, 12 + j:13 + j], in1=gfin2, op0=ALU.mult, op1=ALU.mult),
                             reads=[("mx", s), ("frstd", j), "gf1"], writes=[("mx", s)])
                    DMA("sp", out_d[tok2, :].rearrange("(j p) n -> p j n", p=128), m_x[s].rearrange("p (j n) -> p j n", j=2), [("mx", s)], [("outd", gi)], ("mx", s))
            P.barrier()
        counts = P.emit(nc)
    return nc, counts


def _const_tables():
    qc = np.arange(64)
    kc = np.arange(64)
    cs_ = np.clip(qc - 8, 0, 48)
    colvalid = (kc[:, None] >= cs_[None, :]) & (kc[:, None] < cs_[None, :] + 16)
    dcidx = np.clip(kc[:, None] - qc[None, :], -15, 15) + 15
    mask = np.zeros((128, 16, 64), np.float32)
    dridx = np.zeros((128, 16), np.int64)
    for v in range(16):
        for kin in range(2):
            ok = (VAR_MODE[v] == 0) or (VAR_MODE[v] == 1 and kin == 0) or (VAR_MODE[v] == 2 and kin == 1)
            mask[kin * 64:(kin + 1) * 64, v, :] = colvalid * (1.0 if ok else 0.0)
            dridx[kin * 64:(kin + 1) * 64, v] = np.clip(VAR_DR0[v] + kin, -7, 7) + 7
    i = np.arange(128)
    jj, ii = np.meshgrid(i, i, indexing="ij")
    A = np.where(jj <= ii, -128.0, (jj - ii - 128.0)).astype(np.float32)
    B = np.maximum(jj - ii, 0).astype(np.float32)
    ab = np.concatenate([A, B], axis=1).astype(np.float32)
    c4 = np.stack([i + 1.0, 128.0 - i, 127.0 - i, i * 1.0], axis=1).astype(np.float32)
    return mask, dridx, dcidx, ab, c4


def _rm_table(hf):
    rm = np.zeros((128, 42), np.float32)
    def valid(r_loc, krow_loc):
        gr = r_loc + 64 * hf
        gk = krow_loc + 64 * hf
        rs = min(max(gr - 4, 0), 120)
        return 1.0 if (0 <= gk <= 127 and rs <= gk <= rs + 7) else 0.0
    for r in range(4):
        for ki, kr in enumerate(TOP_KR):
            for kin in range(2):
                rm[kin * 64:(kin + 1) * 64, r * 6 + ki] = valid(r, kr + kin)
    for r in range(61, 64):
        for ki, kr in enumerate(BOT_KR):
            for kin in range(2):
                rm[kin * 64:(kin + 1) * 64, 24 + (r - 61) * 6 + ki] = valid(r, kr + kin)
    return rm


_CACHE = {}


def kernel(x, w_in, w_out, na_rpb, ret_decay_fwd, ret_decay_bwd, ret_norm_gain,
           norm_mix, norm_mlp, w_up, w_down, norm_final, _depth=DEPTH, _stop=9, _debug=False, _lite=False):
    f32 = lambda a: np.ascontiguousarray(np.asarray(a, dtype=np.float32))
    x = f32(x)
    key = (_depth, _stop, _debug, _lite)
    if key not in _CACHE:
        _CACHE[key] = build(_depth, _stop, _debug, _lite)
    nc, _ = _CACHE[key]
    mask, dridx, dcidx, ab, c4 = _const_tables()
    rpb = f32(na_rpb)
    pk = np.arange(128) % 64
    gt = rpb[:, :, dridx[:, :, None], dcidx[pk][:, None, :]]
    gtab = np.ascontiguousarray(np.transpose(gt, (0, 2, 1, 3, 4))).reshape(DEPTH, 128, 8 * 16 * 64)
    maskt = mask.reshape(128, 1024).astype(ml_dtypes.bfloat16)
    decays = np.ascontiguousarray(np.concatenate([f32(ret_decay_fwd), f32(ret_decay_bwd)], axis=1))
    ident = np.eye(128, dtype=np.float32).astype(ml_dtypes.bfloat16)
    inv = (1.0 / (np.float32(10000.0) ** (np.arange(0, 64, 2, dtype=np.float32) / np.float32(64)))).astype(np.float32)
    shared = {
        "w_in": f32(w_in)[:1 if _lite else DEPTH], "w_out": f32(w_out)[:1 if _lite else DEPTH], "w_up": f32(w_up)[:1 if _lite else DEPTH], "w_down": f32(w_down)[:1 if _lite else DEPTH],
        "norm_mix": f32(norm_mix), "norm_mlp": f32(norm_mlp), "norm_final": f32(norm_final).reshape(1, D),
        "ret_norm_gain": f32(ret_norm_gain), "decays": decays, "gtab": gtab, "maskt": maskt,
        "abt": ab, "c4t": c4, "ident": ident,
    }
    in_maps = []
    for c in range(8):
        b, hf = c // 2, c % 2
        pos = (np.arange(TOK) + hf * TOK).astype(np.float32)
        ang = (pos[:, None] * inv[None, :]).astype(np.float32)
        cs = np.concatenate([np.cos(ang), np.sin(ang)], axis=1).astype(np.float32)
        cm = np.zeros((128, 4), np.float32)
        if hf == 0:
            cm[64:, 1] = 1.0
            cm[:, 3] = 1.0
        else:
            cm[:64, 0] = 1.0
            cm[:, 2] = 1.0
        m = dict(shared)
        m["x"] = np.ascontiguousarray(x[b, hf * TOK:(hf + 1) * TOK, :])
        m["cs"] = cs
        m["cmask"] = cm
        m["rmt"] = _rm_table(hf)
        in_maps.append(m)
    res = run_bass_kernel_spmd(nc, in_maps, core_ids=list(range(8)))
    if _debug:
        return res.results
    out = np.empty((4, 2 * TOK, D), np.float32)
    for c in range(8):
        out[c // 2, (c % 2) * TOK:(c % 2 + 1) * TOK, :] = np.asarray(res.results[c]["out"], dtype=np.float32)
    return out
```

```python
import contextlib
import numpy as np
import ml_dtypes
import concourse.bass as bass
import concourse.mybir as mybir
from concourse.bass_utils import run_bass_kernel_spmd

F32 = mybir.dt.float32
BF16 = mybir.dt.bfloat16
AF = mybir.ActivationFunctionType
ALU = mybir.AluOpType
AX = mybir.AxisListType

D = 1024
TOK = 4096
NT = 32
DEPTH = 4
EPS = 1e-6
VAR_DR0 = [3, 2, 1, 0, -1, -2, -3, -4, -5, 3, 4, 5, 6, -5, -6, -7]
VAR_MODE = [1, 0, 0, 0, 0, 0, 0, 0, 2, 0, 0, 0, 0, 0, 0, 0]
VIDX_BOTH = {2: 1, 1: 2, 0: 3, -1: 4, -2: 5, -3: 6, -4: 7, 3: 9, 4: 10, 5: 11, 6: 12, -5: 13, -6: 14, -7: 15}
DBG_OUT = ('xs', 'qna', 'kna', 'vna', 'qtx', 'intra', 'sgd', 'nat')
TOP_KR = [-4, -2, 0, 2, 4, 6]
BOT_KR = [56, 58, 60, 62, 64, 66]


class _Op:
    __slots__ = ("eng", "fn", "deps", "dma", "semkey", "signal", "val", "inc", "barrier")

    def __init__(self, eng, fn, dma, semkey, inc):
        self.eng = eng
        self.fn = fn
        self.deps = ()
        self.dma = dma
        self.semkey = semkey
        self.signal = dma
        self.val = 0
        self.inc = inc
        self.barrier = False


class Prog:
    ENGS = ("pe", "act", "dve", "pool", "sp")
    SAME_ENG_SYNC = ("act", "dve", "pool")

    def __init__(self):
        self.ops = []
        self.res = {}

    def op(self, eng, fn, reads=(), writes=(), dma=False, semkey=None):
        import os
        if len(self.ops) >= int(os.environ.get('KN_MAXOPS', '100000000')):
            return -1
        deps = set()
        for r in reads:
            st = self.res.get(r)
            if st is not None and st[0] is not None:
                deps.add(st[0])
        for w in writes:
            st = self.res.get(w)
            if st is not None:
                if st[0] is not None:
                    deps.add(st[0])
                deps.update(st[1].values())
                deps.update(st[2])
        idx = len(self.ops)
        o = _Op(eng, fn, dma, semkey, 16 if dma else 1)
        pr = set()
        for j in deps:
            oj = self.ops[j]
            if (not oj.dma) and (not dma) and oj.eng == eng and eng not in self.SAME_ENG_SYNC:
                continue
            pr.add(j)
        o.deps = pr
        self.ops.append(o)
        for r in reads:
            st = self.res.setdefault(r, [None, {}, []])
            if dma:
                st[2].append(idx)
            else:
                st[1][eng] = idx
        for w in writes:
            self.res[w] = [idx, {}, []]
        return idx

    def barrier(self):
        o = _Op("sp", None, False, None, 0)
        o.barrier = True
        self.ops.append(o)
        self.res = {}

    def emit(self, nc):
        ops = self.ops
        last = {}
        for o in ops:
            if o.barrier:
                for e, lo in last.items():
                    lo.signal = True
                continue
            for j in o.deps:
                ops[j].signal = True
            if not o.dma:
                last[o.eng] = o
        engcnt = {e: 0 for e in self.ENGS}
        dmacnt = {}
        for o in ops:
            if o.barrier:
                continue
            if o.dma:
                dmacnt[o.semkey] = dmacnt.get(o.semkey, 0) + o.inc
                o.val = dmacnt[o.semkey]
            elif o.signal:
                engcnt[o.eng] += 1
                o.val = engcnt[o.eng]
        with contextlib.ExitStack() as es:
            engsem = {e: es.enter_context(nc.semaphore("s_" + e)) for e in self.ENGS}
            dmasem = {}
            for k in dmacnt:
                dmasem[k] = es.enter_context(nc.semaphore("d_%d" % len(dmasem)))
            streams = {e: [] for e in self.ENGS}
            waited = {e: {} for e in self.ENGS}
            cur_eng = {e: 0 for e in self.ENGS}
            cur_dma = {}
            for o in ops:
                if o.barrier:
                    for e in self.ENGS:
                        wl = []
                        for e2 in self.ENGS:
                            if e2 != e and cur_eng[e2] > waited[e].get(("e", e2), 0):
                                waited[e][("e", e2)] = cur_eng[e2]
                                wl.append((engsem[e2], cur_eng[e2]))
                        for k, v in cur_dma.items():
                            if v > waited[e].get(("d", k), 0):
                                waited[e][("d", k)] = v
                                wl.append((dmasem[k], v))
                        if wl:
                            streams[e].append((wl, None, None, 0))
                    continue
                need = {}
                for j in o.deps:
                    oj = ops[j]
                    if oj.dma:
                        s = ("d", oj.semkey)
                        sem = dmasem[oj.semkey]
                    else:
                        s = ("e", oj.eng)
                        sem = engsem[oj.eng]
                    if oj.val > need.get(s, (None, 0))[1]:
                        need[s] = (sem, oj.val)
                wl = []
                for s, (sem, v) in need.items():
                    if waited[o.eng].get(s, 0) < v:
                        waited[o.eng][s] = v
                        wl.append((sem, v))
                if o.dma:
                    mysem, inc = dmasem[o.semkey], o.inc
                    cur_dma[o.semkey] = o.val
                elif o.signal:
                    mysem, inc = engsem[o.eng], 1
                    cur_eng[o.eng] = o.val
                else:
                    mysem, inc = None, 0
                streams[o.eng].append((wl, o.fn, mysem, inc))
            final = [(dmasem[k], v) for k, v in dmacnt.items()]
            final += [(engsem[e], engcnt[e]) for e in self.ENGS if engcnt[e] > 0]

            def run(stream, lastw=None):
                def f(eng):
                    for wl, fn, mysem, inc in stream:
                        for sem, v in wl:
                            eng.wait_ge(sem, v)
                        if fn is None:
                            continue
                        ins = fn(eng)
                        if mysem is not None:
                            ins.then_inc(mysem, inc)
                    if lastw:
                        for sem, v in lastw:
                            eng.wait_ge(sem, v)
                return f

            with nc.Block() as block:
                block.tensor(run(streams["pe"]))
                block.scalar(run(streams["act"]))
                block.vector(run(streams["dve"]))
                block.gpsimd(run(streams["pool"]))
                block.sync(run(streams["sp"], final))
        return {e: len(streams[e]) for e in self.ENGS}


def ap_of(t, offset, dims):
    return bass.AP(t, offset, [list(d) for d in dims])


def na_units(g):
    R0 = 8 * g
    ilo, ihi = R0, R0 + 7
    bnd = None
    if g == 0:
        ilo = 4
        bnd = (0, 3, TOP_KR)
    if g == 7:
        ihi = 60
        bnd = (61, 63, BOT_KR)
    units = []
    klo = ilo - 5
    klo += klo % 2
    khi = ihi + 3
    khi -= khi % 2
    for kr in range(klo, khi + 1, 2):
        r0 = max(ilo, kr - 3)
        r1 = min(ihi, kr + 5)
        if r0 <= r1:
            units.append((kr, r0, r1, "int"))
    if bnd is not None:
        for kr in bnd[2]:
            units.append((kr, bnd[0], bnd[1], "bnd"))
    return units


def build(depth=DEPTH, stop=9, debug=False, lite=False):
    nc = bass.Bass("TRN2", target_bir_lowering=False)
    dt_in = lambda name, shape, dt=F32: nc.dram_tensor(name, list(shape), dt, kind="ExternalInput").ap()
    x_in = dt_in("x", [TOK, D])
    LD = 1 if lite else DEPTH
    w_in = dt_in("w_in", [LD, D, 3072])
    w_out = dt_in("w_out", [LD, D, D])
    w_up = dt_in("w_up", [LD, D, 4096])
    w_dn = dt_in("w_down", [LD, 4096, D])
    nmix = dt_in("norm_mix", [DEPTH, D])
    nmlp = dt_in("norm_mlp", [DEPTH, D])
    nfin = dt_in("norm_final", [1, D])
    gret_d = dt_in("ret_norm_gain", [DEPTH, 512])
    dec_d = dt_in("decays", [DEPTH, 8])
    gtab_d = dt_in("gtab", [DEPTH, 128, 8 * 16 * 64])
    mask_d = dt_in("maskt", [128, 16 * 64], BF16)
    rm_d = dt_in("rmt", [128, 42])
    cs_d = dt_in("cs", [TOK, 64])
    ab_d = dt_in("abt", [128, 256])
    c4_d = dt_in("c4t", [128, 4])
    cm_d = dt_in("cmask", [128, 4])
    id_d = dt_in("ident", [128, 128], BF16)
    out_d = nc.dram_tensor("out", [TOK, D], F32, kind="ExternalOutput").ap()

    import os as _os0
    _sw0 = _os0.environ.get("KN_SW", "")
    def dscr(name, shape, dt):
        if "e" in _sw0:
            shape = [8, 8]
        return (nc.dram_tensor(name, list(shape), dt, kind="ExternalOutput").ap() if debug and name in DBG_OUT else nc.dram_tensor(name, list(shape), dt).ap())
    xs = dscr("xs", [TOK, D], F32)
    qna = dscr("qna", [8, 64, TOK], BF16)
    kna = dscr("kna", [8, 64, TOK], BF16)
    vna = dscr("vna", [NT, 128, 640], BF16)
    qtx = dscr("qtx", [NT, 128, 512], BF16)
    intra_d = dscr("intra", [NT, 128, 512], F32)
    sg_d = dscr("sgd", [NT, 128, 512], F32)
    nat = dscr("nat", [8, 64, TOK], BF16)
    cst_in = dscr("cst_in", [128, 512], F32)
    cst_out = dscr("cst_out", [256, 512], F32)
    HW = 2048 + 2560
    chal_in = dscr("chal_in", [128, HW], BF16)
    chal_out = dscr("chal_out", [256, HW], BF16)
    import os as _os3
    RG = [[2 * i, 2 * i + 1] for i in range(int(_os3.environ.get('KN_NCORES', '8')) // 2)]

    P = Prog()
    cc_count = [0]
    with contextlib.ExitStack() as es:
        sbt = lambda name, shape, dt: es.enter_context(nc.sbuf_tensor(name, list(shape), dt))
        import os as _os2
        BIG = sbt("BIG", [128, 65536 if not _os2.environ.get("KN_SMALL") else 32768], BF16)
        WF = sbt("WF", [128, 7168], F32)
        WB = sbt("WB", [128, 12288], BF16)
        ident = sbt("ident_s", [128, 128], BF16)
        onesf = sbt("onesf", [128, 64], F32)
        zerosb = sbt("zerosb", [128, 512], BF16)
        ABt = sbt("ABt", [128, 256], F32)
        C4 = sbt("C4", [128, 4], F32)
        cmask = sbt("cmask_s", [128, 4], F32)
        lg = sbt("lg", [128, 8], F32)
        lgp = sbt("lgp", [128, 4], F32)
        TS = sbt("TS", [128, 16], F32)
        Mpp = sbt("Mpp", [128, 512], F32)
        Dfull = sbt("Dfull", [128, 512], F32)
        rmt = sbt("rmt_s", [128, 42], F32)
        gain = sbt("gain", [128, 1024], F32)
        gret = sbt("gret", [128, 512], F32)
        khalo = sbt("khalo", [64, 4096], BF16)
        vhalo = sbt("vhalo", [128, 2560], BF16)
        small = sbt("small", [128, 64], F32)
        ccdummy = sbt("ccdummy", [128, 8], F32)
        ccsem = es.enter_context(nc.semaphore("ccsem"))
        psb = [es.enter_context(nc.psum_tensor("psb%d" % i, [128, 512], F32)) for i in range(1 if "f" in _sw0 else 8)]

        def PSF(i, rows=128, c0=0, c1=512):
            return psb[i][0:rows, c0:c1]

        def PSB16(i, rows=128):
            return psb[i][0:rows, :].bitcast(BF16)

        Win = BIG[:, 0:24576].rearrange("p (k n) -> p k n", k=8)
        EB = BIG[:, 0:8192].rearrange("p (h v q) -> p h v q", h=8, v=16)
        WoutNA = BIG[0:64, 8192:16384].rearrange("p (h n) -> p h n", h=8)
        WoutR = BIG[:, 16384:20480].rearrange("p (k n) -> p k n", k=4)
        gstage = BIG[:, 20480:24576].bitcast(F32)
        KV = BIG[:, 24576:40960].rearrange("p (s n) -> p s n", s=32)
        NAq = BIG[0:64, 24576:28672].rearrange("p (h n) -> p h n", h=8)
        NAk = BIG[0:64, 28672:36864].rearrange("p (h n) -> p h n", h=8)
        NAv = WB[:, 7168:12288].rearrange("p (t n) -> p t n", t=8)
        State = BIG[:, 40960:57344].rearrange("p (s n) -> p s n", s=32)
        klo = BIG[0:64, 57344:61440]
        khi = BIG[0:64, 61440:65536]
        Wup = BIG[:, 0:32768].rearrange("p (k n) -> p k n", k=8)
        Wdn = BIG[:, 32768:65536].rearrange("p (f n) -> p f n", f=32)

        dma_ct = [0]

        def DMA(q, out, in_, reads, writes, semkey):
            P.op(q, lambda e, o=out, i=in_: e.dma_start(out=o, in_=i), reads=reads, writes=writes, dma=True, semkey=semkey)

        import os as _os
        _sw = _os.environ.get("KN_SW", "")
        if "a" not in _sw:
            DMA("sp", ident[:], id_d, [], ["ident"], "c0")
            DMA("sp", ABt[:], ab_d, [], ["ABt"], "c1")
        if "b" not in _sw:
            DMA("sp", C4[:], c4_d, [], ["C4"], "c2")
            DMA("sp", cmask[:], cm_d, [], ["cmask"], "c3")
            DMA("sp", rmt[:], rm_d, [], ["rmt"], "c4")
        if "c" not in _sw:
            P.op("pool", lambda e: e.memset(onesf[:], 1.0), writes=["onesf"])
            P.op("pool", lambda e: e.memset(zerosb[:], 0.0), writes=["zerosb"])
        if "d" not in _sw:
            P.barrier()

        for l in range(depth):
            if stop <= 0:
                break
            xsrc = x_in if l == 0 else xs
            for k in range(8):
                DMA("pool", Win[:, k, :], w_in[l, k * 128:(k + 1) * 128, :], [], [("win", k)], ("win", k))
            DMA("sp", gain[:], nmix[l, :].partition_broadcast(128), [], ["gain"], "gain")
            DMA("sp", lg[:], dec_d[l, :].partition_broadcast(128), [], ["lg"], "lg")
            DMA("sp", lgp[0:64, :], dec_d[l, 0:4].partition_broadcast(64), [], ["lgp0"], "lgp0")
            DMA("sp", lgp[64:128, :], dec_d[l, 4:8].partition_broadcast(64), [], ["lgp1"], "lgp1")
            P.op("act", lambda e: e.activation(out=lg[:], in_=lg[:], func=AF.Exp), reads=["lg"], writes=["lg"])
            P.op("act", lambda e: e.activation(out=lg[:], in_=lg[:], func=AF.Ln, scale=-1.0, bias=1.0), reads=["lg"], writes=["lg"])
            P.op("act", lambda e: e.activation(out=lgp[:], in_=lgp[:], func=AF.Exp), reads=["lgp0", "lgp1"], writes=["lgp"])
            P.op("act", lambda e: e.activation(out=lgp[:], in_=lgp[:], func=AF.Ln, scale=-1.0, bias=1.0), reads=["lgp"], writes=["lgp"])
            P.op("act", lambda e: e.activation(out=small[:, 0:4], in_=lgp[:], func=AF.Exp, scale=128.0), reads=["lgp"], writes=["small"])
            P.op("dve", lambda e: e.tensor_copy(out=Dfull[:].rearrange("p (h n) -> p h n", h=4),
                                                in_=ap_of(small, 0, [small[:, 0:1].ap[0], [1, 4], [0, 128]])),
                 reads=["small"], writes=["Dfull"])
            for kind, (cc, d0) in enumerate([(0, 0), (1, 4), (2, 0), (3, 4)]):
                P.op("dve", lambda e, kind=kind, cc=cc, d0=d0: e.tensor_scalar(
                    out=TS[:, kind * 4:(kind + 1) * 4], in0=lg[:, d0:d0 + 4], scalar1=C4[:, cc:cc + 1], scalar2=None, op0=ALU.mult),
                    reads=["lg", "C4"], writes=[("TSr", kind)])
            P.op("act", lambda e: e.activation(out=TS[:], in_=TS[:], func=AF.Exp), reads=[("TSr", i) for i in range(4)], writes=["TS"])
            P.op("dve", lambda e: e.tensor_scalar(out=TS[:, 8:16], in0=TS[:, 8:16], scalar1=0.125, scalar2=None, op0=ALU.mult), reads=["TS"], writes=["TS"])
            for h in range(4):
                P.op("dve", lambda e, h=h: e.tensor_scalar(out=Mpp[:, h * 128:(h + 1) * 128], in0=ABt[:, 0:128], scalar1=lg[:, h:h + 1], scalar2=None, op0=ALU.mult),
                     reads=["lg", "ABt"], writes=[("Mpp", h)])
                P.op("dve", lambda e, h=h: e.scalar_tensor_tensor(out=Mpp[:, h * 128:(h + 1) * 128], in0=ABt[:, 128:256], scalar=lg[:, 4 + h:5 + h],
                                                                   in1=Mpp[:, h * 128:(h + 1) * 128], op0=ALU.mult, op1=ALU.add),
                     reads=["lg", "ABt", ("Mpp", h)], writes=[("Mpp", h)])
            P.op("act", lambda e: e.activation(out=Mpp[:], in_=Mpp[:], func=AF.Exp), reads=[("Mpp", h) for h in range(4)], writes=["MppF"])

            f_xin = [WF[:, 0:1024], WF[:, 1024:2048]]
            f_rqk = WF[:, 2048:2560]
            f_rot = WF[:, 2560:3072]
            f_t = [WF[:, 3072 + i * 256:3072 + (i + 1) * 256] for i in range(4)]
            f_sg = [WF[:, 4096:4608], WF[:, 4608:5120]]
            f_intra = [WF[:, 5120:5632], WF[:, 5632:6144]]
            f_cs = [WF[:, 6144:6208], WF[:, 6208:6272]]
            f_st = WF[:, 6272:6336]
            b_junk = WB[:, 0:1024]
            b_h = WB[:, 1024:2048]
            b_hT = WB[:, 2048:3072].rearrange("p (k n) -> p k n", k=8)
            b_qtok = WB[:, 3072:3584]
            b_ktok = WB[:, 3584:4096]
            b_qT = WB[0:64, 4096:5120].rearrange("p (h n) -> p h n", h=8)
            b_kT = WB[0:64, 5120:6144].rearrange("p (h n) -> p h n", h=8)
            b_vaug = [WB[:, 6144:6784], WB[:, 6784:7424]]
            b_rv = WB[:, 7424:7936]
            b_Qx = WB[:, 7936:8448]
            b_Kx = WB[:, 8448:8960]
            b_QTx = [WB[:, 8960:9472], WB[:, 9472:9984]]
            b_KT = WB[0:64, 9984:10496].rearrange("p (h n) -> p h n", h=4)
            b_SM = WB[:, 10496:11008]
            for s in range(2):
                P.op("pool", lambda e, s=s: e.memset(b_vaug[s], 1.0), writes=[("vaug", s)])
            for t in range(NT):
                s = t % 2
                tok = slice(t * 128, (t + 1) * 128)
                DMA("sp", f_xin[s], xsrc[tok, :], [], [("xin", s)], ("xin", s))
                DMA("sp", f_cs[s], cs_d[tok, :], [], [("cs", s)], ("cs", s))
                P.op("act", lambda e, s=s: e.activation(out=b_junk, in_=f_xin[s], func=AF.Square, accum_out=f_st[:, 0:1]),
                     reads=[("xin", s)], writes=["ssq"])
                P.op("dve", lambda e: e.tensor_scalar(out=f_st[:, 1:2], in0=f_st[:, 0:1], scalar1=1.0 / D, scalar2=EPS, op0=ALU.mult, op1=ALU.add),
                     reads=["ssq"], writes=["ms"])
                P.op("act", lambda e: e.activation(out=f_st[:, 3:4], in_=f_st[:, 1:2], func=AF.Sqrt), reads=["ms"], writes=["sqv"])
                P.op("dve", lambda e: e.reciprocal(out=f_st[:, 2:3], in_=f_st[:, 3:4]), reads=["sqv"], writes=["rstd"])
                P.op("dve", lambda e, s=s: e.scalar_tensor_tensor(out=b_h, in0=f_xin[s], scalar=f_st[:, 2:3], in1=gain[:], op0=ALU.mult, op1=ALU.mult),
                     reads=[("xin", s), "rstd", "gain"], writes=["h"])
                def tr_h(e):
                    r = None
                    for k in range(8):
                        r = e.transpose(PSB16(0)[:, k * 128:(k + 1) * 128], b_h[:, k * 128:(k + 1) * 128], ident[:])
                    return r
                P.op("pe", tr_h, reads=["h", "ident"], writes=["ps0"])
                P.op("act", lambda e: e.activation(out=b_hT.rearrange("p k n -> p (k n)"), in_=PSB16(0), func=AF.Copy), reads=["ps0"], writes=["hT"])
                def proj(c, bank):
                    def f(e):
                        r = None
                        for k in range(8):
                            r = e.matmul(PSF(bank), lhsT=b_hT[:, k, :], rhs=Win[:, k, c * 512:(c + 1) * 512], start=(k == 0), stop=(k == 7))
                        return r
                    return f
                winr = [("win", k) for k in range(8)]
                P.op("pe", proj(0, 1), reads=["hT"] + winr, writes=["ps1"])
                P.op("act", lambda e: e.activation(out=b_qtok, in_=PSF(1), func=AF.Copy), reads=["ps1"], writes=["qtok"])
                P.op("pe", proj(1, 2), reads=["hT"] + winr, writes=["ps2"])
                P.op("dve", lambda e: e.tensor_copy(out=b_ktok, in_=PSF(2)), reads=["ps2"], writes=["ktok"])
                P.op("pe", proj(2, 1), reads=["hT"] + winr, writes=["ps1"])
                P.op("act", lambda e, s=s: e.activation(out=b_vaug[s].rearrange("p (h n) -> p h n", h=8)[:, :, 0:64],
                                                        in_=PSF(1).rearrange("p (h n) -> p h n", h=8), func=AF.Copy),
                     reads=["ps1"], writes=[("vaug", s)])
                DMA("sp", vna[t], b_vaug[s], [("vaug", s)], [("vna", t)], ("vaug", s))
                def tr_na(src):
                    def f(e):
                        r = None
                        for h in range(8):
                            r = e.transpose(PSB16(3, 64)[:, h * 128:(h + 1) * 128], src[:, h * 64:(h + 1) * 64], ident[:])
                        return r
                    return f
                P.op("pe", tr_na(b_qtok), reads=["qtok", "ident"], writes=["ps3"])
                P.op("dve", lambda e: e.tensor_copy(out=b_qT.rearrange("p h n -> p (h n)"), in_=PSB16(3, 64)), reads=["ps3"], writes=["qT"])
                DMA("sp", qna[:, :, tok].rearrange("h d k -> d h k"), b_qT, ["qT"], [("qna", t)], "qT")
                P.op("pe", tr_na(b_ktok), reads=["ktok", "ident"], writes=["ps3"])
                P.op("act", lambda e: e.activation(out=b_kT.rearrange("p h n -> p (h n)"), in_=PSB16(3, 64), func=AF.Copy), reads=["ps3"], writes=["kT"])
                DMA("sp", kna[:, :, tok].rearrange("h d k -> d h k"), b_kT, ["kT"], [("kna", t)], "kT")
                P.op("pe", proj(3, 2), reads=["hT"] + winr, writes=["ps2"])
                P.op("act", lambda e: e.activation(out=f_rqk, in_=PSF(2), func=AF.Copy), reads=["ps2"], writes=["rqk"])
                P.op("pe", proj(4, 1), reads=["hT"] + winr, writes=["ps1"])
                P.op("dve", lambda e: e.tensor_copy(out=b_rv, in_=PSF(1)), reads=["ps1"], writes=["rv"])
                P.op("pe", proj(5, 2), reads=["hT"] + winr, writes=["ps2"])
                P.op("act", lambda e, s=s: e.activation(out=f_sg[s], in_=PSF(2), func=AF.Silu), reads=["ps2"], writes=[("sg", s)])
                DMA("sp", sg_d[t], f_sg[s], [("sg", s)], [("sgd", t)], ("sg", s))
                x4 = f_rqk.rearrange("p (g a c) -> p g a c", g=8, a=2)
                r4 = f_rot.rearrange("p (g a c) -> p g a c", g=8, a=2)
                x1, x2 = x4[:, :, 0, :], x4[:, :, 1, :]
                csap = f_cs[s]
                cosb = ap_of(csap.tensor, csap.offset, [csap.ap[0], [0, 8], [1, 32]])
                sinb = ap_of(csap.tensor, csap.offset + 32, [csap.ap[0], [0, 8], [1, 32]])
                tv = [f_t[i].rearrange("p (g c) -> p g c", g=8) for i in range(4)]
                rd = ["rqk", ("cs", s)]
                P.op("dve", lambda e, x1=x1, cosb=cosb: e.tensor_tensor(out=tv[0], in0=x1, in1=cosb, op=ALU.mult), reads=rd, writes=["t0"])
                P.op("dve", lambda e, x2=x2, sinb=sinb: e.tensor_tensor(out=tv[1], in0=x2, in1=sinb, op=ALU.mult), reads=rd, writes=["t1"])
                P.op("dve", lambda e, r4=r4: e.tensor_tensor(out=r4[:, :, 0, :], in0=tv[0], in1=tv[1], op=ALU.subtract), reads=["t0", "t1"], writes=["rot0"])
                P.op("pool", lambda e, x1=x1, sinb=sinb: e.tensor_tensor(out=tv[2], in0=x1, in1=sinb, op=ALU.mult), reads=rd, writes=["t2"])
                P.op("pool", lambda e, x2=x2, cosb=cosb: e.tensor_tensor(out=tv[3], in0=x2, in1=cosb, op=ALU.mult), reads=rd, writes=["t3"])
                P.op("pool", lambda e, r4=r4: e.tensor_tensor(out=r4[:, :, 1, :], in0=tv[2], in1=tv[3], op=ALU.add), reads=["t2", "t3"], writes=["rot1"])
                ro = f_rot
                qin = ap_of(ro.tensor, ro.offset, [ro.ap[0], [64, 4], [0, 2], [1, 64]])
                kin = ap_of(ro.tensor, ro.offset + 256, [ro.ap[0], [64, 4], [0, 2], [1, 64]])
                tsq = ap_of(TS, 0, [TS[:, 0:1].ap[0], [1, 4], [4, 2], [0, 64]])
                tsk = ap_of(TS, 8, [TS[:, 0:1].ap[0], [1, 4], [4, 2], [0, 64]])
                P.op("dve", lambda e, qin=qin: e.tensor_tensor(out=b_Qx.rearrange("p (h a c) -> p h a c", h=4, a=2), in0=qin, in1=tsq, op=ALU.mult),
                     reads=["rot0", "rot1", "TS"], writes=["Qx"])
                P.op("pool", lambda e, kin=kin: e.tensor_tensor(out=b_Kx.rearrange("p (h a c) -> p h a c", h=4, a=2), in0=kin, in1=tsk, op=ALU.mult),
                     reads=["rot0", "rot1", "TS"], writes=["Kx"])
                def tr_ret(e):
                    r = None
                    for h in range(4):
                        r = e.transpose(PSB16(4)[:, h * 128:(h + 1) * 128], b_Qx[:, h * 128:(h + 1) * 128], ident[:])
                    for h in range(4):
                        r = e.transpose(PSB16(3, 64)[:, h * 128:(h + 1) * 128], b_Kx[:, h * 128:h * 128 + 64], ident[:])
                    return r
                P.op("pe", tr_ret, reads=["Qx", "Kx", "ident"], writes=["ps4", "ps3"])
                P.op("act", lambda e, s=s: e.activation(out=b_QTx[s], in_=PSB16(4)[:, 0:512], func=AF.Copy), reads=["ps4"], writes=[("QTx", s)])
                P.op("dve", lambda e: e.tensor_copy(out=b_KT.rearrange("p h n -> p (h n)"), in_=PSB16(3, 64)[:, 0:512]), reads=["ps3"], writes=["KT"])
                DMA("sp", qtx[t], b_QTx[s], [("QTx", s)], [("qtx", t)], ("QTx", s))
                def st_mm(e, s=s):
                    r = None
                    for h in range(4):
                        r = e.matmul(PSF(5, 128, h * 128, (h + 1) * 128), lhsT=b_KT[:, h, :], rhs=b_QTx[s][0:64, h * 128:(h + 1) * 128], start=True, stop=True)
                    return r
                P.op("pe", st_mm, reads=["KT", ("QTx", s)], writes=["ps5"])
                P.op("dve", lambda e: e.tensor_tensor(out=b_SM, in0=PSF(5), in1=Mpp[:], op=ALU.mult), reads=["ps5", "MppF"], writes=["SM"])
                def in_mm(e):
                    r = None
                    for h in range(4):
                        r = e.matmul(PSF(6, 128, h * 128, (h + 1) * 128), lhsT=b_SM[:, h * 128:(h + 1) * 128], rhs=b_rv[:, h * 128:(h + 1) * 128], start=True, stop=True)
                    return r
                P.op("pe", in_mm, reads=["SM", "rv"], writes=["ps6"])
                P.op("act", lambda e, s=s: e.activation(out=f_intra[s], in_=PSF(6), func=AF.Copy), reads=["ps6"], writes=[("intra", s)])
                DMA("sp", intra_d[t], f_intra[s], [("intra", s)], [("intrad", t)], ("intra", s))
                def kv_mm(e):
                    r = None
                    for h in range(4):
                        r = e.matmul(PSF(7, 128, h * 128, (h + 1) * 128), lhsT=b_Kx[:, h * 128:(h + 1) * 128], rhs=b_rv[:, h * 128:(h + 1) * 128], start=True, stop=True)
                    return r
                P.op("pe", kv_mm, reads=["Kx", "rv"], writes=["ps7"])
                P.op("dve", lambda e, t=t: e.tensor_copy(out=KV[0:64, t, :], in_=PSF(7, 64)), reads=["ps7"], writes=[("KVf", t)])
                P.op("dve", lambda e, t=t: e.tensor_copy(out=KV[64:128, 31 - t, :], in_=psb[7][64:128, :]), reads=["ps7"], writes=[("KVb", 31 - t)])
            P.barrier()

            if stop <= 1:
                break
            ACC = WF[:, 0:512]
            G2 = WF[:, 512:1536]
            vlo = WB[:, 0:2560]
            vhi = WB[:, 2560:5120]
            P.op("pool", lambda e: e.memset(ACC, 0.0), writes=["ACC"])
            for s_ in range(32):
                P.op("dve", lambda e: e.tensor_tensor(out=ACC, in0=ACC, in1=Dfull[:], op=ALU.mult), reads=["ACC", "Dfull"], writes=["ACC"])
                P.op("dve", lambda e, s_=s_: e.tensor_tensor(out=ACC, in0=ACC, in1=KV[:, s_, :], op=ALU.add), reads=["ACC"], writes=["ACC"])
            DMA("sp", cst_in, ACC, ["ACC"], ["cst_in"], "cst")
            for i, kt in enumerate([0, 1, 30, 31]):
                DMA("sp", chal_in[(i // 2) * 64:(i // 2 + 1) * 64, (i % 2) * 1024:(i % 2 + 1) * 1024].rearrange("d (h k) -> d h k", h=8),
                    kna[:, :, kt * 128:(kt + 1) * 128].rearrange("h d k -> d h k"), [], [("chk", i)], ("chk", i))
                DMA("sp", chal_in[:, 2048 + i * 640:2048 + (i + 1) * 640], vna[kt], [], [("chv", i)], ("chv", i))

            def cc1(e):
                cc_count[0] += 1
                i = e.collective_compute("AllGather", ALU.bypass, replica_groups=RG, ins=[cst_in], outs=[cst_out])
                i.then_inc(ccsem)
                e.wait_ge(ccsem, cc_count[0])
                return e.memset(ccdummy[:], 0.0)

            def cc2(e):
                cc_count[0] += 1
                i = e.collective_compute("AllGather", ALU.bypass, replica_groups=RG, ins=[chal_in], outs=[chal_out])
                i.then_inc(ccsem)
                e.wait_ge(ccsem, cc_count[0])
                return e.memset(ccdummy[:], 0.0)
            P.op("pool", cc1, reads=["cst_in"], writes=["cst_out"])
            P.op("pool", cc2, reads=[("chk", i) for i in range(4)] + [("chv", i) for i in range(4)], writes=["chal_out"])
            DMA("sp", G2.rearrange("p (r n) -> p r n", r=2), cst_out.rearrange("(r p) n -> p r n", p=128), ["cst_out"], ["G2"], "G2")
            P.op("dve", lambda e: e.tensor_scalar(out=ACC, in0=G2[:, 0:512], scalar1=cmask[:, 0:1], scalar2=None, op0=ALU.mult), reads=["G2", "cmask"], writes=["ACC"])
            P.op("dve", lambda e: e.scalar_tensor_tensor(out=ACC, in0=G2[:, 512:1024], scalar=cmask[:, 1:2], in1=ACC, op0=ALU.mult, op1=ALU.add),
                 reads=["G2", "cmask", "ACC"], writes=["ACC"])
            for s_ in range(32):
                P.op("act", lambda e, s_=s_: e.activation(out=State[0:64, s_, :], in_=ACC[0:64, :], func=AF.Copy), reads=["ACC"], writes=[("StateF", s_)])
                P.op("act", lambda e, s_=s_: e.activation(out=State[64:128, 31 - s_, :], in_=ACC[64:128, :], func=AF.Copy), reads=["ACC"], writes=[("StateB", s_)])
                P.op("dve", lambda e: e.tensor_tensor(out=ACC, in0=ACC, in1=Dfull[:], op=ALU.mult), reads=["ACC", "Dfull"], writes=["ACC"])
                P.op("dve", lambda e, s_=s_: e.tensor_tensor(out=ACC, in0=ACC, in1=KV[:, s_, :], op=ALU.add), reads=["ACC"], writes=["ACC"])
            DMA("sp", klo[:, 0:2048], chal_out[0:64, 0:2048], ["chal_out"], ["klo0"], "klo0")
            DMA("sp", klo[:, 2048:4096], chal_out[64:128, 0:2048], ["chal_out"], ["klo1"], "klo1")
            DMA("sp", khi[:, 0:2048], chal_out[128:192, 0:2048], ["chal_out"], ["khi0"], "khi0")
            DMA("sp", khi[:, 2048:4096], chal_out[192:256, 0:2048], ["chal_out"], ["khi1"], "khi1")
            DMA("sp", vlo, chal_out[0:128, 2048:HW], ["chal_out"], ["vlo"], "vlo")
            DMA("sp", vhi, chal_out[128:256, 2048:HW], ["chal_out"], ["vhi"], "vhi")
            P.op("pool", lambda e: e.tensor_scalar(out=khalo[:], in0=klo, scalar1=cmask[0:64, 2:3], scalar2=None, op0=ALU.mult), reads=["klo0", "klo1", "cmask"], writes=["khalo"])
            P.op("dve", lambda e: e.scalar_tensor_tensor(out=khalo[:], in0=khi, scalar=cmask[0:64, 3:4], in1=khalo[:], op0=ALU.mult, op1=ALU.add),
                 reads=["khi0", "khi1", "cmask", "khalo"], writes=["khalo"])
            P.op("pool", lambda e: e.tensor_scalar(out=vhalo[:], in0=vlo, scalar1=cmask[:, 2:3], scalar2=None, op0=ALU.mult), reads=["vlo", "cmask"], writes=["vhalo"])
            P.op("dve", lambda e: e.scalar_tensor_tensor(out=vhalo[:], in0=vhi, scalar=cmask[:, 3:4], in1=vhalo[:], op0=ALU.mult, op1=ALU.add),
                 reads=["vhi", "cmask", "vhalo"], writes=["vhalo"])
            P.barrier()

            if stop <= 2:
                break
            maskT = WB[:, 0:1024]
            DMA("sp", maskT, mask_d, [], ["maskT"], "maskT")
            for h in range(8):
                DMA("sp", gstage[:, (h % 2) * 1024:(h % 2 + 1) * 1024], gtab_d[l, :, h * 1024:(h + 1) * 1024], [], [("gst", h % 2)], ("gst", h % 2))
                P.op("act", lambda e, h=h: e.activation(out=gstage[:, (h % 2) * 1024:(h % 2 + 1) * 1024], in_=gstage[:, (h % 2) * 1024:(h % 2 + 1) * 1024], func=AF.Exp),
                     reads=[("gst", h % 2)], writes=[("gst", h % 2)])
                P.op("dve", lambda e, h=h: e.tensor_tensor(out=EB[:, h, :, :].rearrange("p v q -> p (v q)"), in0=gstage[:, (h % 2) * 1024:(h % 2 + 1) * 1024], in1=maskT, op=ALU.mult),
                     reads=[("gst", h % 2), "maskT"], writes=[("EB", h)])
            P.barrier()
            WFb = WF[:, 1024:1536].bitcast(BF16)
            b_e = [WB[:, 1024:1536], WB[:, 1536:2048], WB[:, 0:512], WB[:, 512:1024]]
            b_p = [WB[:, 2048:2560], WB[:, 2560:3072], WFb[:, 0:512], WFb[:, 512:1024]]
            SBK = [0, 1, 5, 6]
            PD = 3
            b_nat = WB[0:64, 3072:7168].rearrange("p (h n) -> p h n", h=8)
            f_rc = WF[:, 0:512]
            f_bc = WF[0:64, 512:1024]
            kh4 = khalo[:].rearrange("p (i h k) -> p i h k", i=4, h=8)
            vh4 = vhalo[:].rearrange("p (i n) -> p i n", i=4)
            NAq2 = BIG[0:64, 20480:24576].rearrange("p (h n) -> p h n", h=8)
            NAk2 = BIG[0:64, 57344:65536].rearrange("p (h n) -> p h n", h=8)
            NAv2 = WF[:, 1536:4096].bitcast(BF16).rearrange("p (t n) -> p t n", t=8)
            NAqs, NAks, NAvs = [NAq, NAq2], [NAk, NAk2], [NAv, NAv2]
            ucount = [0]
            for g in range(8):
                gs = g % 2
                NAq_, NAk_, NAv_ = NAqs[gs], NAks[gs], NAvs[gs]
                w0, w1 = max(0, 4 * g - 2), min(31, 4 * g + 5)
                nw = w1 - w0 + 1
                gt = slice(g * 512, (g + 1) * 512)
                DMA("sp", NAq_, qna[:, :, gt].rearrange("h d k -> d h k"), [], [("NAq", gs)], ("NAq", gs))
                DMA("sp", NAk_[:, :, 0:nw * 128], kna[:, :, w0 * 128:(w1 + 1) * 128].rearrange("h d k -> d h k"), [], [("NAk", gs)], ("NAk", gs))
                DMA("sp", NAv_[:, 0:nw, :], vna[w0:w1 + 1].rearrange("t p n -> p t n"), [], [("NAv", gs)], ("NAv", gs))
                units = na_units(g)
                nu = len(units)

                def unit_front(h, ui, slot):
                    kr, r0, r1, kind = units[ui]
                    sbank = SBK[slot]
                    nq = r1 - r0 + 1
                    c0 = (r0 - 8 * g) * 64
                    c1 = c0 + nq * 64
                    if kr < 0:
                        kap = kh4[:, (kr + 4) // 2 + 2, h, :]
                        kres = "khalo"
                    elif kr >= 64:
                        kap = kh4[:, (kr - 64) // 2, h, :]
                        kres = "khalo"
                    else:
                        wi = kr // 2 - w0
                        kap = NAk_[:, h, wi * 128:(wi + 1) * 128]
                        kres = ("NAk", gs)
                    qap = NAq_[:, h, c0:c1]
                    P.op("pe", lambda e: e.matmul(PSF(sbank, 128, 0, nq * 64), lhsT=kap, rhs=qap, start=True, stop=True),
                         reads=[kres, ("NAq", gs)], writes=[("ps", sbank)])
                    P.op("act", lambda e: e.activation(out=b_e[slot][:, 0:nq * 64], in_=PSF(sbank, 128, 0, nq * 64), func=AF.Exp, scale=0.125),
                         reads=[("ps", sbank)], writes=[("e", slot)])
                    if kind == "int":
                        v0 = 3 - (kr - r0)
                        meng = "dve" if (slot % 2 == 0) else "pool"
                        P.op(meng, lambda e: e.tensor_tensor(out=b_p[slot][:, 0:nq * 64], in0=b_e[slot][:, 0:nq * 64],
                                                             in1=EB[:, h, v0:v0 + nq, :].rearrange("p v q -> p (v q)"), op=ALU.mult),
                             reads=[("e", slot), ("EB", h)], writes=[("p", slot)])
                    else:
                        for r in range(r0, r1 + 1):
                            vi = VIDX_BOTH[kr - r]
                            if r0 == 0:
                                u = r * 6 + TOP_KR.index(kr)
                            else:
                                u = 24 + (r - 61) * 6 + BOT_KR.index(kr)
                            j = r - r0
                            P.op("dve", lambda e, j=j, vi=vi, u=u: e.scalar_tensor_tensor(
                                out=b_p[slot][:, j * 64:(j + 1) * 64], in0=b_e[slot][:, j * 64:(j + 1) * 64], scalar=rmt[:, u:u + 1],
                                in1=EB[:, h, vi, :], op0=ALU.mult, op1=ALU.mult),
                                reads=[("e", slot), ("EB", h), "rmt"], writes=[("p", slot)])

                def unit_back(h, ui, slot):
                    kr, r0, r1, kind = units[ui]
                    ob = 2 + (h % 2)
                    nq = r1 - r0 + 1
                    c0 = (r0 - 8 * g) * 64
                    c1 = c0 + nq * 64
                    if kr < 0:
                        vap = vh4[:, (kr + 4) // 2 + 2, h * 80:h * 80 + 65]
                        vres = "vhalo"
                    elif kr >= 64:
                        vap = vh4[:, (kr - 64) // 2, h * 80:h * 80 + 65]
                        vres = "vhalo"
                    else:
                        wi = kr // 2 - w0
                        vap = NAv_[:, wi, h * 80:h * 80 + 65]
                        vres = ("NAv", gs)
                    last = ui == nu - 1
                    P.op("pe", lambda e: e.matmul(PSF(ob, 65, c0, c1), lhsT=vap, rhs=b_p[slot][:, 0:nq * 64], start=False, stop=last),
                         reads=[vres, ("p", slot)], writes=[("ps", ob)])

                def preamble(h):
                    ob = 2 + (h % 2)
                    P.op("pe", lambda e: e.matmul(PSF(ob, 65), lhsT=zerosb[0:1, 0:65], rhs=zerosb[0:1, 0:512], start=True, stop=False),
                         reads=["zerosb"], writes=[("ps", ob)])
                    base = ucount[0]
                    ucount[0] += nu
                    for ui in range(min(PD, nu)):
                        unit_front(h, ui, (base + ui) % 4)
                    return base

                def normalize(h):
                    ob = 2 + (h % 2)
                    P.op("dve", lambda e: e.reciprocal(out=f_rc[64:65, :], in_=psb[ob][64:65, :]), reads=[("ps", ob)], writes=["rc"])
                    P.op("pe", lambda e: e.matmul(PSF(4, 64), lhsT=onesf[64:65, 0:64], rhs=f_rc[64:65, :], start=True, stop=True), reads=["rc", "onesf"], writes=[("ps", 4)])
                    P.op("act", lambda e: e.activation(out=f_bc, in_=PSF(4, 64), func=AF.Copy), reads=[("ps", 4)], writes=["bc"])
                    P.op("dve", lambda e: e.tensor_tensor(out=b_nat[:, h, :], in0=PSF(ob, 64), in1=f_bc, op=ALU.mult), reads=[("ps", ob), "bc"], writes=["natsb"])

                base = preamble(0)
                for h in range(8):
                    for ui in range(nu):
                        if ui + PD < nu:
                            unit_front(h, ui + PD, (base + ui + PD) % 4)
                        unit_back(h, ui, (base + ui) % 4)
                    nbase = preamble(h + 1) if h + 1 < 8 else None
                    normalize(h)
                    base = nbase
                DMA("sp", nat[:, :, gt].rearrange("h d k -> d h k"), b_nat, ["natsb"], [("nat", g)], "natsb")
            P.barrier()

            if stop <= 3:
                break
            DMA("pool", WoutNA, w_out[l, 0:512, :].rearrange("(h d) n -> d h n", d=64), [], ["WoutNA"], "WoutNA")
            DMA("pool", WoutR, w_out[l, 512:1024, :].rearrange("(k p) n -> p k n", p=128), [], ["WoutR"], "WoutR")
            DMA("sp", gret[:], gret_d[l, :].partition_broadcast(128), [], ["gret"], "gret")
            f_x = [WF[:, 0:1024], WF[:, 1024:2048]]
            f_in = [WF[:, 2048:2560], WF[:, 2560:3072]]
            f_sgl = [WF[:, 3072:3584], WF[:, 3584:4096]]
            f_ys = [WF[:, 4096:4608], WF[:, 5184:5696]]
            f_ysqs = [WF[:, 4608:5120], WF[:, 5696:6208]]
            f_st2s = [WF[:, 5120:5184], WF[:, 6208:6272]]
            f_xo = [WB[:, 0:2048].bitcast(F32), WB[:, 2048:4096].bitcast(F32)]
            b_qx = [WB[:, 4096:4608], WB[:, 4608:5120]]
            b_na = [WB[0:64, 5120:6144].rearrange("p (h n) -> p h n", h=8), WB[0:64, 6144:7168].rearrange("p (h n) -> p h n", h=8)]
            b_mixs = [WB[:, 7168:7680], WB[:, 8192:8704]]
            b_mixTs = [WB[:, 7680:8192].rearrange("p (k n) -> p k n", k=4), WB[:, 8704:9216].rearrange("p (k n) -> p k n", k=4)]

            def p2_tile(t):
                steps = []
                s = t % 2
                pb = 4 * s
                tok = slice(t * 128, (t + 1) * 128)
                f_y, f_ysq, f_st2, b_mix, b_mixT = f_ys[s], f_ysqs[s], f_st2s[s], b_mixs[s], b_mixTs[s]
                OP = lambda *a, **k: steps.append(lambda: P.op(*a, **k))
                DM = lambda *a: steps.append(lambda: DMA(*a))
                DM("sp", f_x[s], xsrc[tok, :], [], [("x2", s)], ("x2", s))
                DM("sp", b_qx[s], qtx[t], [], [("qx2", s)], ("qx2", s))
                DM("sp", f_in[s], intra_d[t], [], [("in2", s)], ("in2", s))
                DM("sp", f_sgl[s], sg_d[t], [], [("sg2", s)], ("sg2", s))
                DM("sp", b_na[s], nat[:, :, tok].rearrange("h d k -> d h k"), [], [("na2", s)], ("na2", s))

                def cross(e):
                    r = None
                    for h in range(4):
                        r = e.matmul(PSF(pb, 128, h * 128, (h + 1) * 128), lhsT=b_qx[s][:, h * 128:(h + 1) * 128], rhs=State[:, t, h * 128:(h + 1) * 128], start=True, stop=True)
                    return r
                OP("pe", cross, reads=[("qx2", s)], writes=[("ps", pb)])
                OP("dve", lambda e: e.tensor_tensor(out=f_y, in0=PSF(pb), in1=f_in[s], op=ALU.add), reads=[("ps", pb), ("in2", s)], writes=[("y", s)])
                y3 = f_y.rearrange("p (h n) -> p h n", h=4)
                for h in range(4):
                    OP("act", lambda e, h=h: e.activation(out=f_ysq[:, h * 128:(h + 1) * 128], in_=f_y[:, h * 128:(h + 1) * 128], func=AF.Copy, accum_out=f_st2[:, h:h + 1]),
                       reads=[("y", s)], writes=[("s1", s, h)])
                for h in range(4):
                    OP("act", lambda e, h=h: e.activation(out=f_ysq[:, h * 128:(h + 1) * 128], in_=f_y[:, h * 128:(h + 1) * 128], func=AF.Square, accum_out=f_st2[:, 4 + h:5 + h]),
                       reads=[("y", s)], writes=[("s2", s, h)])
                OP("dve", lambda e: e.tensor_scalar(out=f_st2[:, 8:12], in0=f_st2[:, 0:4], scalar1=1.0 / 128, scalar2=None, op0=ALU.mult), reads=[("s1", s, h) for h in range(4)], writes=[("mean", s)])
                OP("dve", lambda e: e.tensor_tensor(out=f_st2[:, 12:16], in0=f_st2[:, 8:12], in1=f_st2[:, 8:12], op=ALU.mult), reads=[("mean", s)], writes=[("msq", s)])
                OP("dve", lambda e: e.scalar_tensor_tensor(out=f_st2[:, 16:20], in0=f_st2[:, 4:8], scalar=1.0 / 128, in1=f_st2[:, 12:16], op0=ALU.mult, op1=ALU.subtract),
                   reads=[("s2", s, h) for h in range(4)] + [("msq", s)], writes=[("var", s)])
                OP("dve", lambda e: e.tensor_scalar(out=f_st2[:, 24:28], in0=f_st2[:, 16:20], scalar1=EPS, scalar2=None, op0=ALU.add), reads=[("var", s)], writes=[("vare", s)])
                OP("act", lambda e: e.activation(out=f_st2[:, 28:32], in_=f_st2[:, 24:28], func=AF.Sqrt), reads=[("vare", s)], writes=[("sqv2", s)])
                OP("dve", lambda e: e.reciprocal(out=f_st2[:, 20:24], in_=f_st2[:, 28:32]), reads=[("sqv2", s)], writes=[("rstd2", s)])
                st = f_st2
                meanb = ap_of(st.tensor, st.offset + 8, [st.ap[0], [1, 4], [0, 128]])
                rstdb = ap_of(st.tensor, st.offset + 20, [st.ap[0], [1, 4], [0, 128]])
                OP("dve", lambda e: e.tensor_tensor(out=y3, in0=y3, in1=meanb, op=ALU.subtract), reads=[("y", s), ("mean", s), ("s2", s, 3)], writes=[("y", s)])
                OP("pool", lambda e: e.tensor_tensor(out=y3, in0=y3, in1=rstdb, op=ALU.mult), reads=[("y", s), ("rstd2", s)], writes=[("y", s)])
                OP("dve", lambda e: e.tensor_tensor(out=f_y, in0=f_y, in1=gret[:], op=ALU.mult), reads=[("y", s), "gret"], writes=[("y", s)])
                OP("pool", lambda e: e.tensor_tensor(out=b_mix, in0=f_y, in1=f_sgl[s], op=ALU.mult), reads=[("y", s), ("sg2", s)], writes=[("mix", s)])

                def tr_mix(e):
                    r = None
                    for k in range(4):
                        r = e.transpose(PSB16(pb + 1)[:, k * 128:(k + 1) * 128], b_mix[:, k * 128:(k + 1) * 128], ident[:])
                    return r
                OP("pe", tr_mix, reads=[("mix", s), "ident"], writes=[("ps", pb + 1)])
                OP("act", lambda e: e.activation(out=b_mixT.rearrange("p k n -> p (k n)"), in_=PSB16(pb + 1)[:, 0:512], func=AF.Copy), reads=[("ps", pb + 1)], writes=[("mixT", s)])
                for c in range(2):
                    def oproj(e, c=c):
                        r = None
                        for h in range(8):
                            e.matmul(PSF(pb + 2 + c), lhsT=b_na[s][:, h, :], rhs=WoutNA[:, h, c * 512:(c + 1) * 512], start=(h == 0), stop=False)
                        for k in range(4):
                            r = e.matmul(PSF(pb + 2 + c), lhsT=b_mixT[:, k, :], rhs=WoutR[:, k, c * 512:(c + 1) * 512], start=False, stop=(k == 3))
                        return r
                    OP("pe", oproj, reads=[("na2", s), ("mixT", s), "WoutNA", "WoutR"], writes=[("ps", pb + 2 + c)])
                    OP("dve", lambda e, c=c: e.tensor_tensor(out=f_xo[s][:, c * 512:(c + 1) * 512], in0=PSF(pb + 2 + c), in1=f_x[s][:, c * 512:(c + 1) * 512], op=ALU.add),
                       reads=[("ps", pb + 2 + c), ("x2", s)], writes=[("xo", s, c)])
                DM("sp", xs[tok, :], f_xo[s], [("xo", s, 0), ("xo", s, 1)], [("xs", t)], ("xo", s))
                return steps

            for t0 in range(0, NT, 2):
                sa, sb_ = p2_tile(t0), p2_tile(t0 + 1)
                for i in range(max(len(sa), len(sb_))):
                    if i < len(sa):
                        sa[i]()
                    if i < len(sb_):
                        sb_[i]()
            P.barrier()

            if stop <= 4:
                break
            for k in range(8):
                DMA("pool", Wup[:, k, :], w_up[l, k * 128:(k + 1) * 128, :], [], [("wup", k)], ("wup", k))
            for f8 in range(8):
                DMA("pool", Wdn[:, f8 * 4:(f8 + 1) * 4, :], w_dn[l, f8 * 512:(f8 + 1) * 512, :].rearrange("(f p) n -> p f n", p=128), [], [("wdn", f8)], ("wdn", f8))
            DMA("sp", gain[:], nmlp[l, :].partition_broadcast(128), [], ["gain"], "gain")
            lastl = l == depth - 1
            if lastl:
                DMA("sp", gret[:], nfin[0, 0:512].partition_broadcast(128), [], ["gf0"], "gf0")
                gfin2 = WF[:, 6144:6656]
                DMA("sp", gfin2, nfin[0, 512:1024].partition_broadcast(128), [], ["gf1"], "gf1")
            m_x = [WF[:, 0:2048], WF[:, 2048:4096]]
            m_r = [WF[:, 4096:4352], WF[:, 4352:4608]]
            m_sts = [WF[:, 4608:4672], WF[:, 4672:4736]]
            m_hs = [WB[:, 0:2048], WB[:, 6144:8192]]
            m_hTs = [WB[:, 2048:4096].rearrange("p (k n) -> p k n", k=8), WB[:, 8192:10240].rearrange("p (k n) -> p k n", k=8)]
            m_hid = [WB[:, 4096 + i * 256:4096 + (i + 1) * 256] for i in range(4)]
            m_junk = WB[:, 5120:6144]
            wupr = [("wup", k) for k in range(8)]

            def mlp_prologue(gi):
                s = gi % 2
                m_st, m_h, m_hT = m_sts[s], m_hs[s], m_hTs[s]
                tok2 = slice(gi * 256, (gi + 1) * 256)
                DMA("sp", m_x[s].rearrange("p (j n) -> p j n", j=2), xs[tok2, :].rearrange("(j p) n -> p j n", p=128), [], [("mx", s)], ("mx", s))
                for j in range(2):
                    P.op("act", lambda e, s=s, j=j, m_st=m_st: e.activation(out=m_junk, in_=m_x[s][:, j * 1024:(j + 1) * 1024], func=AF.Square, accum_out=m_st[:, j:j + 1]),
                         reads=[("mx", s)], writes=[("mssq", s, j)])
                    P.op("dve", lambda e, j=j, m_st=m_st: e.tensor_scalar(out=m_st[:, 2 + j:3 + j], in0=m_st[:, j:j + 1], scalar1=1.0 / D, scalar2=EPS, op0=ALU.mult, op1=ALU.add),
                         reads=[("mssq", s, j)], writes=[("mms", s, j)])
                    P.op("act", lambda e, j=j, m_st=m_st: e.activation(out=m_st[:, 6 + j:7 + j], in_=m_st[:, 2 + j:3 + j], func=AF.Sqrt), reads=[("mms", s, j)], writes=[("msq", s, j)])
                    P.op("dve", lambda e, j=j, m_st=m_st: e.reciprocal(out=m_st[:, 4 + j:5 + j], in_=m_st[:, 6 + j:7 + j]), reads=[("msq", s, j)], writes=[("mrstd", s, j)])
                    P.op("dve", lambda e, s=s, j=j, m_st=m_st, m_h=m_h: e.scalar_tensor_tensor(out=m_h[:, j * 1024:(j + 1) * 1024], in0=m_x[s][:, j * 1024:(j + 1) * 1024],
                                                                                   scalar=m_st[:, 4 + j:5 + j], in1=gain[:], op0=ALU.mult, op1=ALU.mult),
                         reads=[("mx", s), ("mrstd", s, j), "gain"], writes=[("mh", s, j)])

                    def tr_m(e, j=j, m_h=m_h):
                        r = None
                        for k in range(8):
                            r = e.transpose(PSB16(6 + j)[:, k * 128:(k + 1) * 128], m_h[:, j * 1024 + k * 128:j * 1024 + (k + 1) * 128], ident[:])
                        return r
                    P.op("pe", tr_m, reads=[("mh", s, j), "ident"], writes=[("ps", 6 + j)])
                    P.op("act", lambda e, j=j, m_hT=m_hT: e.activation(out=m_hT[:, :, j * 128:(j + 1) * 128], in_=PSB16(6 + j).rearrange("p (k n) -> p k n", k=8), func=AF.Copy),
                         reads=[("ps", 6 + j)], writes=[("mhT", s, j)])

            mlp_prologue(0)
            for gi in range(NT // 2):
                s = gi % 2
                m_st, m_hT = m_sts[s], m_hTs[s]
                tok2 = slice(gi * 256, (gi + 1) * 256)

                def emit_up(f, s=s, m_hT=m_hT):
                    ub = 4 + (f % 2)

                    def up(e, f=f, ub=ub):
                        r = None
                        for k in range(8):
                            r = e.matmul(PSF(ub, 128, 0, 256), lhsT=Wup[:, k, f * 128:(f + 1) * 128], rhs=m_hT[:, k, :], start=(k == 0), stop=(k == 7))
                        return r
                    P.op("pe", up, reads=[("mhT", s, 0), ("mhT", s, 1)] + wupr, writes=[("ps", ub)])
                emit_up(0)
                for f in range(32):
                    ub = 4 + (f % 2)
                    hb = f % 4
                    if f + 1 < 32:
                        emit_up(f + 1)
                    P.op("act", lambda e, ub=ub, f=f: e.activation(out=m_r[f % 2], in_=PSF(ub, 128, 0, 256), func=AF.Relu), reads=[("ps", ub)], writes=[("mr", f % 2)])
                    P.op("dve" if f % 2 == 0 else "pool", lambda e, f=f, hb=hb: e.tensor_tensor(out=m_hid[hb], in0=m_r[f % 2], in1=m_r[f % 2], op=ALU.mult),
                         reads=[("mr", f % 2)], writes=[("mhid", hb)])

                    def down(e, f=f, hb=hb):
                        r = None
                        for j in range(2):
                            for c in range(2):
                                r = e.matmul(PSF(j * 2 + c), lhsT=m_hid[hb][:, j * 128:(j + 1) * 128], rhs=Wdn[:, f, c * 512:(c + 1) * 512], start=(f == 0), stop=(f == 31))
                        return r
                    P.op("pe", down, reads=[("mhid", hb), ("wdn", f // 4)], writes=[("ps", 0), ("ps", 1), ("ps", 2), ("ps", 3)])
                    if f == 3 and gi + 1 < NT // 2:
                        mlp_prologue(gi + 1)
                for j in range(2):
                    for c in range(2):
                        P.op("dve", lambda e, s=s, j=j, c=c: e.tensor_tensor(out=m_x[s][:, j * 1024 + c * 512:j * 1024 + (c + 1) * 512], in0=PSF(j * 2 + c),
                                                                              in1=m_x[s][:, j * 1024 + c * 512:j * 1024 + (c + 1) * 512], op=ALU.add),
                             reads=[("ps", j * 2 + c), ("mx", s)], writes=[("mx", s)])
                if not lastl:
                    DMA("sp", xs[tok2, :].rearrange("(j p) n -> p j n", p=128), m_x[s].rearrange("p (j n) -> p j n", j=2), [("mx", s)], [("xsm", gi)], ("mx", s))
                else:
                    for j in range(2):
                        P.op("act", lambda e, s=s, j=j: e.activation(out=m_junk, in_=m_x[s][:, j * 1024:(j + 1) * 1024], func=AF.Square, accum_out=m_st[:, 8 + j:9 + j]),
                             reads=[("mx", s)], writes=[("fssq", j)])
                        P.op("dve", lambda e, j=j: e.tensor_scalar(out=m_st[:, 10 + j:11 + j], in0=m_st[:, 8 + j:9 + j], scalar1=1.0 / D, scalar2=EPS, op0=ALU.mult, op1=ALU.add),
                             reads=[("fssq", j)], writes=[("fms", j)])
                        P.op("act", lambda e, j=j: e.activation(out=m_st[:, 14 + j:15 + j], in_=m_st[:, 10 + j:11 + j], func=AF.Sqrt), reads=[("fms", j)], writes=[("fsq", j)])
                        P.op("dve", lambda e, j=j: e.reciprocal(out=m_st[:, 12 + j:13 + j], in_=m_st[:, 14 + j:15 + j]), reads=[("fsq", j)], writes=[("frstd", j)])
                        P.op("dve", lambda e, s=s, j=j: e.scalar_tensor_tensor(out=m_x[s][:, j * 1024:j * 1024 + 512], in0=m_x[s][:, j * 1024:j * 1024 + 512],
                                                                                scalar=m_st[:, 12 + j:13 + j], in1=gret[:], op0=ALU.mult, op1=ALU.mult),
                             reads=[("mx", s), ("frstd", j), "gf0"], writes=[("mx", s)])
                        P.op("dve", lambda e, s=s, j=j: e.scalar_tensor_tensor(out=m_x[s][:, j * 1024 + 512:(j + 1) * 1024], in0=m_x[s][:, j * 1024 + 512:(j + 1) * 1024],
                                                                                 scalar=m_st[:, 12 + j:13 + j], in1=gfin2, op0=ALU.mult, op1=ALU.mult),
                             reads=[("mx", s), ("frstd", j), "gf1"], writes=[("mx", s)])
                    DMA("sp", out_d[tok2, :].rearrange("(j p) n -> p j n", p=128), m_x[s].rearrange("p (j n) -> p j n", j=2), [("mx", s)], [("outd", gi)], ("mx", s))
            P.barrier()
        counts = P.emit(nc)
    return nc, counts


def _const_tables():
    qc = np.arange(64)
    kc = np.arange(64)
    cs_ = np.clip(qc - 8, 0, 48)
    colvalid = (kc[:, None] >= cs_[None, :]) & (kc[:, None] < cs_[None, :] + 16)
    dcidx = np.clip(kc[:, None] - qc[None, :], -15, 15) + 15
    mask = np.zeros((128, 16, 64), np.float32)
    dridx = np.zeros((128, 16), np.int64)
    for v in range(16):
        for kin in range(2):
            ok = (VAR_MODE[v] == 0) or (VAR_MODE[v] == 1 and kin == 0) or (VAR_MODE[v] == 2 and kin == 1)
            mask[kin * 64:(kin + 1) * 64, v, :] = colvalid * (1.0 if ok else 0.0)
            dridx[kin * 64:(kin + 1) * 64, v] = np.clip(VAR_DR0[v] + kin, -7, 7) + 7
    i = np.arange(128)
    jj, ii = np.meshgrid(i, i, indexing="ij")
    A = np.where(jj <= ii, -128.0, (jj - ii - 128.0)).astype(np.float32)
    B = np.maximum(jj - ii, 0).astype(np.float32)
    ab = np.concatenate([A, B], axis=1).astype(np.float32)
    c4 = np.stack([i + 1.0, 128.0 - i, 127.0 - i, i * 1.0], axis=1).astype(np.float32)
    return mask, dridx, dcidx, ab, c4


def _rm_table(hf):
    rm = np.zeros((128, 42), np.float32)
    def valid(r_loc, krow_loc):
        gr = r_loc + 64 * hf
        gk = krow_loc + 64 * hf
        rs = min(max(gr - 4, 0), 120)
        return 1.0 if (0 <= gk <= 127 and rs <= gk <= rs + 7) else 0.0
    for r in range(4):
        for ki, kr in enumerate(TOP_KR):
            for kin in range(2):
                rm[kin * 64:(kin + 1) * 64, r * 6 + ki] = valid(r, kr + kin)
    for r in range(61, 64):
        for ki, kr in enumerate(BOT_KR):
            for kin in range(2):
                rm[kin * 64:(kin + 1) * 64, 24 + (r - 61) * 6 + ki] = valid(r, kr + kin)
    return rm


_CACHE = {}


def kernel(x, w_in, w_out, na_rpb, ret_decay_fwd, ret_decay_bwd, ret_norm_gain,
           norm_mix, norm_mlp, w_up, w_down, norm_final, _depth=DEPTH, _stop=9, _debug=False, _lite=False, _trace=False):
    f32 = lambda a: np.ascontiguousarray(np.asarray(a, dtype=np.float32))
    x = f32(x)
    key = (_depth, _stop, _debug, _lite)
    if key not in _CACHE:
        _CACHE[key] = build(_depth, _stop, _debug, _lite)
    nc, _ = _CACHE[key]
    mask, dridx, dcidx, ab, c4 = _const_tables()
    rpb = f32(na_rpb)
    pk = np.arange(128) % 64
    gt = rpb[:, :, dridx[:, :, None], dcidx[pk][:, None, :]]
    gtab = np.ascontiguousarray(np.transpose(gt, (0, 2, 1, 3, 4))).reshape(DEPTH, 128, 8 * 16 * 64)
    maskt = mask.reshape(128, 1024).astype(ml_dtypes.bfloat16)
    decays = np.ascontiguousarray(np.concatenate([f32(ret_decay_fwd), f32(ret_decay_bwd)], axis=1))
    ident = np.eye(128, dtype=np.float32).astype(ml_dtypes.bfloat16)
    inv = (1.0 / (np.float32(10000.0) ** (np.arange(0, 64, 2, dtype=np.float32) / np.float32(64)))).astype(np.float32)
    shared = {
        "w_in": f32(w_in)[:1 if _lite else DEPTH], "w_out": f32(w_out)[:1 if _lite else DEPTH], "w_up": f32(w_up)[:1 if _lite else DEPTH], "w_down": f32(w_down)[:1 if _lite else DEPTH],
        "norm_mix": f32(norm_mix), "norm_mlp": f32(norm_mlp), "norm_final": f32(norm_final).reshape(1, D),
        "ret_norm_gain": f32(ret_norm_gain), "decays": decays, "gtab": gtab, "maskt": maskt,
        "abt": ab, "c4t": c4, "ident": ident,
    }
    in_maps = []
    for c in range(8):
        b, hf = c // 2, c % 2
        pos = (np.arange(TOK) + hf * TOK).astype(np.float32)
        ang = (pos[:, None] * inv[None, :]).astype(np.float32)
        cs = np.concatenate([np.cos(ang), np.sin(ang)], axis=1).astype(np.float32)
        cm = np.zeros((128, 4), np.float32)
        if hf == 0:
            cm[64:, 1] = 1.0
            cm[:, 3] = 1.0
        else:
            cm[:64, 0] = 1.0
            cm[:, 2] = 1.0
        m = dict(shared)
        m["x"] = np.ascontiguousarray(x[b, hf * TOK:(hf + 1) * TOK, :])
        m["cs"] = cs
        m["cmask"] = cm
        m["rmt"] = _rm_table(hf)
        in_maps.append(m)
    res = run_bass_kernel_spmd(nc, in_maps, core_ids=list(range(8)), **({'trace': True} if _trace else {}))
    if _trace:
        return res.exec_time_ns
    if _debug:
        return res.results
    out = np.empty((4, 2 * TOK, D), np.float32)
    for c in range(8):
        out[c // 2, (c % 2) * TOK:(c % 2 + 1) * TOK, :] = np.asarray(res.results[c]["out"], dtype=np.float32)
    return out
```
